# Optimizing a Trainium2 kernel written in Bass

```python
import math
import jax, jax.numpy as jnp
from jax import lax
import numpy as np

D_MODEL = 2048
BATCH = 8
SEQ = 2048
DEPTH = 2

CHUNK = 64
QBLOCK = 128
HEAD_DIM = 128
SB_HEADS = 8
DIFF_HEADS = 4
DIFF_V_DIM = 2 * HEAD_DIM
ROT_DIM = HEAD_DIM // 4
ROPE_THETA = 500000.0
SG_BLOCK = 128
SG_GROUPS = 16
SG_GROUP_DIM = 128
SG_WIDTH = SG_GROUPS * SG_GROUP_DIM
D_FF = 4 * D_MODEL
PLE_DIM = 256
N_EVEN = (DEPTH + 1) // 2
N_ODD = DEPTH // 2
SB_WIDTH = SB_HEADS * HEAD_DIM
DIFF_QK_WIDTH = DIFF_HEADS * 2 * HEAD_DIM
DIFF_V_WIDTH = DIFF_HEADS * DIFF_V_DIM
EVEN_IN_WIDTH = 3 * SB_WIDTH + 2 * DIFF_QK_WIDTH + DIFF_V_WIDTH
EVEN_OUT_WIDTH = SB_WIDTH + DIFF_V_WIDTH
EVEN_SPLITS = [SB_WIDTH, 2 * SB_WIDTH, 3 * SB_WIDTH,
               3 * SB_WIDTH + DIFF_QK_WIDTH, 3 * SB_WIDTH + 2 * DIFF_QK_WIDTH]
EPS = 1e-6

kernel_name = 'hybrid_sb_diff_gmlp_stream_block'


def rms_norm(x, g):
    x32 = x.astype(jnp.float32)
    y = x32 * lax.rsqrt(jnp.mean(x32 * x32, axis=-1, keepdims=True) + EPS)
    return (y * g.astype(jnp.float32)).astype(x.dtype)


def layer_norm(x, g, b):
    x32 = x.astype(jnp.float32)
    mu = jnp.mean(x32, axis=-1, keepdims=True)
    xc = x32 - mu
    y = xc * lax.rsqrt(jnp.mean(xc * xc, axis=-1, keepdims=True) + EPS)
    return (y * g.astype(jnp.float32) + b.astype(jnp.float32)).astype(x.dtype)


def partial_rope(x, cos, sin):
    half = ROT_DIM // 2
    x1 = x[..., :half].astype(jnp.float32)
    x2 = x[..., half:ROT_DIM].astype(jnp.float32)
    rot = jnp.concatenate([x1 * cos - x2 * sin, x2 * cos + x1 * sin], axis=-1).astype(x.dtype)
    return jnp.concatenate([rot, x[..., ROT_DIM:]], axis=-1)


def stick_breaking_attention(q, k, v):
    S = q.shape[1]
    scale = HEAD_DIM ** -0.5
    outs = []
    for qs in range(0, S, QBLOCK):
        kend = qs + QBLOCK
        qb = q[:, qs:kend].astype(jnp.float32)
        kb = k[:, :kend].astype(jnp.float32)
        z = jnp.einsum('bthd,bshd->bhts', qb, kb) * scale
        t_idx = qs + jnp.arange(QBLOCK)[:, None]
        s_idx = jnp.arange(kend)[None, :]
        mask = s_idx < t_idx
        log_keep = jnp.where(mask, jax.nn.log_sigmoid(-z), 0.0)
        later = lax.cumsum(log_keep, axis=3, reverse=True) - log_keep
        w = jnp.where(mask, jnp.exp(jax.nn.log_sigmoid(z) + later), 0.0)
        outs.append(jnp.einsum('bhts,bshd->bthd', w.astype(v.dtype), v[:, :kend]))
    return jnp.concatenate(outs, axis=1)


def differential_attention(q, k, v, lam, lambda_init, subln_g):
    S = q.shape[1]
    scale = HEAD_DIM ** -0.5
    outs = []
    for qs in range(0, S, QBLOCK):
        kend = qs + QBLOCK
        qb = q[:, qs:kend].astype(jnp.float32)
        kb = k[:, :kend].astype(jnp.float32)
        z = jnp.einsum('bthmd,bshmd->bhmts', qb, kb) * scale
        t_chunk = (qs + jnp.arange(QBLOCK)[:, None]) // CHUNK
        s_chunk = jnp.arange(kend)[None, :] // CHUNK
        z = jnp.where(s_chunk <= t_chunk, z, -jnp.inf)
        a = jax.nn.softmax(z, axis=-1)
        w = a[:, :, 0] - lam * a[:, :, 1]
        outs.append(jnp.einsum('bhts,bshe->bthe', w.astype(v.dtype), v[:, :kend]))
    o = jnp.concatenate(outs, axis=1)
    return rms_norm(o, subln_g) * (1.0 - lambda_init)


def even_mixer(h, w_in, lam_q1, lam_k1, lam_q2, lam_k2, subln_g, w_out, cos, sin, lambda_init):
    B, S, _ = h.shape
    proj = h @ w_in
    sb_q, sb_k, sb_v, df_q, df_k, df_v = jnp.split(proj, EVEN_SPLITS, axis=-1)
    sb_shape = (B, S, SB_HEADS, HEAD_DIM)
    sb_o = stick_breaking_attention(sb_q.reshape(sb_shape), sb_k.reshape(sb_shape),
                                    sb_v.reshape(sb_shape))
    qk_shape = (B, S, DIFF_HEADS * 2, HEAD_DIM)
    df_q = partial_rope(df_q.reshape(qk_shape), cos, sin).reshape(B, S, DIFF_HEADS, 2, HEAD_DIM)
    df_k = partial_rope(df_k.reshape(qk_shape), cos, sin).reshape(B, S, DIFF_HEADS, 2, HEAD_DIM)
    df_v = df_v.reshape(B, S, DIFF_HEADS, DIFF_V_DIM)
    lam = (jnp.exp(jnp.sum(lam_q1.astype(jnp.float32) * lam_k1.astype(jnp.float32)))
           - jnp.exp(jnp.sum(lam_q2.astype(jnp.float32) * lam_k2.astype(jnp.float32)))
           + lambda_init)
    df_o = differential_attention(df_q, df_k, df_v, lam, lambda_init, subln_g)
    merged = jnp.concatenate([sb_o.reshape(B, S, SB_WIDTH), df_o.reshape(B, S, DIFF_V_WIDTH)], axis=-1)
    return merged @ w_out


def odd_mixer(h, w_in, ln_g, ln_b, w_s, b_s, w_out):
    B, S, _ = h.shape
    u, v = jnp.split(jax.nn.gelu(h @ w_in), 2, axis=-1)
    v = layer_norm(v, ln_g, ln_b)
    n_blk = S // SG_BLOCK
    v = v.reshape(B, n_blk, SG_BLOCK, SG_GROUPS, SG_GROUP_DIM)
    pos = jnp.arange(SG_BLOCK)
    mask = (pos[None, :] // CHUNK) <= (pos[:, None] // CHUNK)
    w = jnp.where(mask[None], w_s, 0.0)
    mixed = jnp.einsum('gts,bnsgc->bntgc', w, v) + b_s.T[None, None, :, :, None]
    y = u * mixed.reshape(B, S, SG_WIDTH)
    return y @ w_out


def channel_mixer(h, w1, w2):
    a = jax.nn.relu(h @ w1)
    return (a * a) @ w2


def setup_inputs(seed: int = 0) -> dict:
    key = jax.random.key(seed)
    ks = jax.random.split(key, 32)
    f32 = jnp.float32

    def nrm(k, shape, scale):
        return jax.random.normal(k, shape, f32) * scale

    def gain(k, shape):
        return 1.0 + 0.05 * jax.random.normal(k, shape, f32)

    x = nrm(ks[0], (BATCH, SEQ, D_MODEL), 1.0)
    p = nrm(ks[1], (DEPTH, BATCH, SEQ, PLE_DIM), 1.0)
    start = jax.random.randint(ks[2], (BATCH, 1), 0, 65536, dtype=jnp.int32)
    positions = start + jnp.arange(SEQ, dtype=jnp.int32)[None, :]
    return {
        'x': x,
        'p': p,
        'positions': positions,
        'ev_norm_pre': gain(ks[3], (N_EVEN, D_MODEL)),
        'ev_w_in': nrm(ks[4], (N_EVEN, D_MODEL, EVEN_IN_WIDTH), D_MODEL ** -0.5),
        'ev_lam_q1': nrm(ks[5], (N_EVEN, HEAD_DIM), 0.1),
        'ev_lam_k1': nrm(ks[6], (N_EVEN, HEAD_DIM), 0.1),
        'ev_lam_q2': nrm(ks[7], (N_EVEN, HEAD_DIM), 0.1),
        'ev_lam_k2': nrm(ks[8], (N_EVEN, HEAD_DIM), 0.1),
        'ev_subln': gain(ks[9], (N_EVEN, DIFF_V_DIM)),
        'ev_w_out': nrm(ks[10], (N_EVEN, EVEN_OUT_WIDTH, D_MODEL), EVEN_OUT_WIDTH ** -0.5),
        'ev_norm_post': gain(ks[11], (N_EVEN, D_MODEL)),
        'od_norm_pre': gain(ks[12], (N_ODD, D_MODEL)),
        'od_w_in': nrm(ks[13], (N_ODD, D_MODEL, 2 * SG_WIDTH), D_MODEL ** -0.5),
        'od_ln_g': gain(ks[14], (N_ODD, SG_WIDTH)),
        'od_ln_b': nrm(ks[15], (N_ODD, SG_WIDTH), 0.02),
        'od_w_s': nrm(ks[16], (N_ODD, SG_GROUPS, SG_BLOCK, SG_BLOCK), SG_BLOCK ** -0.5),
        'od_b_s': 1.0 + nrm(ks[17], (N_ODD, SG_GROUPS, SG_BLOCK), 0.1),
        'od_w_out': nrm(ks[18], (N_ODD, SG_WIDTH, D_MODEL), SG_WIDTH ** -0.5),
        'od_norm_post': gain(ks[19], (N_ODD, D_MODEL)),
        'ffn_norm_pre': gain(ks[20], (DEPTH, D_MODEL)),
        'ffn_w1': nrm(ks[21], (DEPTH, D_MODEL, D_FF), D_MODEL ** -0.5),
        'ffn_w2': nrm(ks[22], (DEPTH, D_FF, D_MODEL), D_FF ** -0.5),
        'ffn_norm_post': gain(ks[23], (DEPTH, D_MODEL)),
        'ple_w_proj': nrm(ks[24], (DEPTH, PLE_DIM, D_MODEL), PLE_DIM ** -0.5),
        'ple_w_gate': nrm(ks[25], (DEPTH, D_MODEL, D_MODEL), D_MODEL ** -0.5),
        'ple_norm': gain(ks[26], (DEPTH, D_MODEL)),
    }


def reference(x, p, positions, ev_norm_pre, ev_w_in, ev_lam_q1, ev_lam_k1, ev_lam_q2, ev_lam_k2,
              ev_subln, ev_w_out, ev_norm_post, od_norm_pre, od_w_in, od_ln_g, od_ln_b, od_w_s,
              od_b_s, od_w_out, od_norm_post, ffn_norm_pre, ffn_w1, ffn_w2, ffn_norm_post,
              ple_w_proj, ple_w_gate, ple_norm):
    inv_freq = ROPE_THETA ** (-jnp.arange(0, ROT_DIM, 2, dtype=jnp.float32) / ROT_DIM)
    ang = positions.astype(jnp.float32)[..., None] * inv_freq
    cos = jnp.cos(ang)[:, :, None, :]
    sin = jnp.sin(ang)[:, :, None, :]
    h = x
    for i in range(DEPTH):
        if i % 2 == 0:
            j = i // 2
            lambda_init = 0.8 - 0.6 * math.exp(-0.3 * i)
            m = even_mixer(rms_norm(h, ev_norm_pre[j]), ev_w_in[j], ev_lam_q1[j], ev_lam_k1[j],
                           ev_lam_q2[j], ev_lam_k2[j], ev_subln[j], ev_w_out[j], cos, sin,
                           lambda_init)
            h = h + rms_norm(m, ev_norm_post[j])
        else:
            j = i // 2
            m = odd_mixer(rms_norm(h, od_norm_pre[j]), od_w_in[j], od_ln_g[j], od_ln_b[j],
                          od_w_s[j], od_b_s[j], od_w_out[j])
            h = h + rms_norm(m, od_norm_post[j])
        f = channel_mixer(rms_norm(h, ffn_norm_pre[i]), ffn_w1[i], ffn_w2[i])
        h = h + rms_norm(f, ffn_norm_post[i])
        gate = jax.nn.sigmoid(h @ ple_w_gate[i])
        e = p[i] @ ple_w_proj[i]
        h = h + rms_norm(gate * e, ple_norm[i])
    return h
```

```python
import math
from contextlib import ExitStack

import numpy as np
import concourse.bass as bass
import concourse.mybir as mybir
from concourse.bass_utils import run_bass_kernel_spmd

F32 = mybir.dt.float32
BF16 = mybir.dt.bfloat16
I32 = mybir.dt.int32
AF = mybir.ActivationFunctionType
ALU = mybir.AluOpType
AX = mybir.AxisListType

ENGINES = ("pe", "act", "dve", "pool", "sp")
DMA_SEMS_PER_QUEUE = 8


class Buf:
    __slots__ = ("name", "W", "R")

    def __init__(self, name):
        self.name = name
        self.W = []
        self.R = []


class _Op:
    __slots__ = ("eng", "fn", "deps", "is_dma", "idx", "signal", "ticket", "dsem")

    def __init__(self, eng, fn, is_dma, idx):
        self.eng = eng
        self.fn = fn
        self.is_dma = is_dma
        self.idx = idx
        self.deps = set()
        self.signal = False
        self.ticket = None
        self.dsem = None


class Sched:
    def __init__(self, nc):
        self.nc = nc
        self.ops = []

    def op(self, eng, fn, reads=(), writes=(), dma=False):
        o = _Op(eng, fn, dma, len(self.ops))
        deps = set()
        for b in reads:
            deps.update(b.W)
        for b in writes:
            deps.update(b.W)
            deps.update(b.R)
        o.deps = deps
        for b in writes:
            b.W = [o.idx]
            b.R = []
        for b in reads:
            if b in writes:
                continue
            if not dma:
                b.R = [i for i in b.R if not (self.ops[i].eng == eng and not self.ops[i].is_dma)]
            b.R.append(o.idx)
        self.ops.append(o)
        return o

    def pe(self, fn, reads=(), writes=()):
        return self.op("pe", fn, reads, writes)

    def act(self, fn, reads=(), writes=()):
        return self.op("act", fn, reads, writes)

    def dve(self, fn, reads=(), writes=()):
        return self.op("dve", fn, reads, writes)

    def pool(self, fn, reads=(), writes=()):
        return self.op("pool", fn, reads, writes)

    def dma(self, eng, fn, reads=(), writes=()):
        return self.op(eng, fn, reads, writes, dma=True)

    def emit(self, final_wait_ops=()):
        nc = self.nc
        ops = self.ops
        for o in ops:
            for d in o.deps:
                p = ops[d]
                if p.is_dma:
                    continue
                if p.eng != o.eng or o.is_dma or o.eng != "pe":
                    p.signal = True
        with ExitStack() as es:
            esem = {e: es.enter_context(nc.semaphore("s_" + e)) for e in ENGINES}
            dsems = {}
            for e in ("sp", "act", "pool"):
                dsems[e] = [es.enter_context(nc.semaphore(f"d_{e}_{i}")) for i in range(DMA_SEMS_PER_QUEUE)]
            cnt = {e: 0 for e in ENGINES}
            dcnt = {e: 0 for e in dsems}
            dtot = {e: [0] * DMA_SEMS_PER_QUEUE for e in dsems}
            for o in ops:
                if o.is_dma:
                    j = dcnt[o.eng] % DMA_SEMS_PER_QUEUE
                    dcnt[o.eng] += 1
                    o.dsem = (o.eng, j, dtot[o.eng][j])
                    dtot[o.eng][j] += 16
                    o.ticket = (dsems[o.eng][j], dtot[o.eng][j], ("d", o.eng, j))
                elif o.signal:
                    cnt[o.eng] += 1
                    o.ticket = (esem[o.eng], cnt[o.eng], ("e", o.eng))
            self.max_ticket = dict(cnt)
            per_eng = {e: [o for o in ops if o.eng == e] for e in ENGINES}
            final = list(final_wait_ops)
            blk = es.enter_context(nc.Block())

            def run(e, eng):
                seen = {}
                for o in per_eng[e]:
                    waits = {}
                    for d in o.deps:
                        p = ops[d]
                        if p.ticket is None:
                            continue
                        if (not p.is_dma) and (not o.is_dma) and p.eng == e and e == "pe":
                            continue
                        sem, val, key = p.ticket
                        if seen.get(key, 0) >= val:
                            continue
                        if key not in waits or waits[key][1] < val:
                            waits[key] = (sem, val)
                    if o.is_dma:
                        qe, j, prev = o.dsem
                        key = ("d", qe, j)
                        if prev > 0 and seen.get(key, 0) < prev:
                            if key not in waits or waits[key][1] < prev:
                                waits[key] = (dsems[qe][j], prev)
                    for key, (sem, val) in waits.items():
                        eng.wait_ge(sem, val)
                        seen[key] = val
                    ins = o.fn(eng)
                    if o.ticket is not None:
                        sem, val, key = o.ticket
                        ins.then_inc(sem, 16 if o.is_dma else 1)
                if e == "sp":
                    for o in final:
                        sem, val, key = o.ticket
                        eng.wait_ge(sem, val)

            @blk.tensor
            def _(eng):
                run("pe", eng)

            @blk.scalar
            def _(eng):
                run("act", eng)

            @blk.vector
            def _(eng):
                run("dve", eng)

            @blk.gpsimd
            def _(eng):
                run("pool", eng)

            @blk.sync
            def _(eng):
                run("sp", eng)


D = 2048
T = 2048
NT = 16
KC = 16
HD = 128
DFF = 8192
PLE = 256
EPS = 1e-6
ROPE_THETA = 500000.0
ROT = 32
SCALE = HD ** -0.5
MAGIC = 12582912.0
TWO_PI = 2.0 * math.pi
C1 = 6.28125
C2 = 1015.0 / 524288.0
C3 = TWO_PI - C1 - C2

W_SPECS = [
    ("ev_norm_pre", [1, D]), ("ev_w_in", [D, 6144]),
    ("ev_lam_q1", [1, HD]), ("ev_lam_k1", [1, HD]), ("ev_lam_q2", [1, HD]), ("ev_lam_k2", [1, HD]),
    ("ev_subln", [1, 256]), ("ev_w_out", [D, D]), ("ev_norm_post", [1, D]),
    ("od_norm_pre", [1, D]), ("od_w_in", [D, 4096]), ("od_ln_g", [1, D]), ("od_ln_b", [1, D]),
    ("od_w_s", [16, 128, 128]), ("od_b_s", [1, 2048]), ("od_w_out", [D, D]), ("od_norm_post", [1, D]),
    ("ffn_norm_pre", [2, D]), ("ffn_w1", [2, D, DFF]), ("ffn_w2", [2, DFF, D]), ("ffn_norm_post", [2, D]),
    ("ple_w_proj", [2, PLE, D]), ("ple_w_gate", [2, D, D]), ("ple_norm", [2, D]),
]


def build(stop=None, debug=False):
    nc = bass.Bass("TRN2", target_bir_lowering=False)
    x_in = nc.dram_tensor("x", [T, D], F32, kind="ExternalInput").ap()
    p_in = nc.dram_tensor("p", [2, T, PLE], F32, kind="ExternalInput").ap()
    pos_in = nc.dram_tensor("positions", [1, T], I32, kind="ExternalInput").ap()
    Wd = {}
    for name, shape in W_SPECS:
        Wd[name] = nc.dram_tensor(name, shape, F32, kind="ExternalInput").ap()
    out = nc.dram_tensor("out", [T, D], F32, kind="ExternalOutput").ap()
    dk = "ExternalOutput" if debug else "Internal"
    hs = nc.dram_tensor("hs", [T, D], F32, kind=dk).ap()
    ms = nc.dram_tensor("ms", [T, D], F32, kind=dk).ap()
    sbqT = nc.dram_tensor("sbqT", [1024, T], BF16, kind=dk).ap()
    sbkT = nc.dram_tensor("sbkT", [1024, T], BF16, kind=dk).ap()
    sbv = nc.dram_tensor("sbv", [T, 1024], BF16, kind=dk).ap()
    dfqT = nc.dram_tensor("dfqT", [1024, T], BF16, kind=dk).ap()
    dfkT = nc.dram_tensor("dfkT", [1024, T], BF16, kind=dk).ap()
    dfv = nc.dram_tensor("dfv", [T, 1024], BF16, kind=dk).ap()
    mergedT = nc.dram_tensor("mergedT", [D, T], BF16, kind=dk).ap()
    vsc = nc.dram_tensor("vsc", [T, D], F32, kind=dk).ap()

    es = ExitStack()
    with es:
        def sb(name, shape, dt):
            return es.enter_context(nc.sbuf_tensor(name, shape, dt))

        R1 = sb("R1", [128, 32768], BF16)
        R2 = sb("R2", [128, 32768], BF16)
        WS = [sb(f"ws{i}", [128, 16, 512], BF16) for i in range(2)]
        hst = sb("hst", [128, D], F32)
        mst = sb("mst", [128, D], F32)
        gpb = sb("gpb", [128, D], F32)
        xnb = sb("xnb", [128, D], BF16)
        stg = [sb(f"stg{i}", [128, 512], F32) for i in range(4)]
        gcols = sb("gcols", [128, 16 * 5], F32)
        ident = sb("ident", [128, 128], BF16)
        identf = sb("identf", [128, 128], F32)
        sm = sb("sm", [128, 128], F32)
        pmat = sb("pmat", [128, 32], F32)
        g16 = sb("g16", [16, 5 * 128], F32)
        PS = es.enter_context(nc.psum_tensor("PS", [128, 4096], F32))

        S = Sched(nc)
        A = R1[:].rearrange("p (k t) -> p k t", k=16)
        Bb = R2[:].rearrange("p (k t) -> p k t", k=16)
        R1f = R1[:].bitcast(F32)
        R2f = R2[:].bitcast(F32)

        def bank(b):
            return PS[:, b * 512:(b + 1) * 512]

        def bank_bf(b):
            return PS[:, b * 512:(b + 1) * 512].bitcast(BF16)

        PB = [Buf(f"pb{b}") for b in range(8)]
        Abuf = [Buf(f"A{t}") for t in range(NT)]
        Bbuf = [Buf(f"B{k}") for k in range(KC)]
        WSb = [Buf("ws0"), Buf("ws1")]
        b_hst, b_mst, b_gpb, b_xnb = Buf("hst"), Buf("mst"), Buf("gpb"), Buf("xnb")
        b_stg = [Buf(f"stg{i}") for i in range(4)]
        b_gcols, b_ident, b_sm, b_pmat, b_g16 = Buf("gcols"), Buf("ident"), Buf("sm"), Buf("pmat"), Buf("g16")
        b_hs = [Buf(f"hs{t}") for t in range(NT)]
        b_ms = [Buf(f"ms{t}") for t in range(NT)]
        b_out = Buf("out")
        b_R1 = Buf("R1")
        b_R2 = Buf("R2")
        b_scr = {n: Buf(n) for n in ["sbqT", "sbkT", "sbv", "dfqT", "dfkT", "dfv", "mergedT", "vsc"]}
        out_ops = []
        st = {"stg": 0, "ws": 0, "pb": 0, "sm": 20}
        R1_all = set(Abuf)
        R2_all = set(Bbuf)

        def region_switch(region_all, new_bufs):
            Wu, Ru = set(), set()
            for b in region_all:
                Wu.update(b.W)
                Ru.update(b.R)
            for b in new_bufs:
                b.W = sorted(set(b.W) | Wu)
                b.R = sorted(set(b.R) | Ru)
            region_all.update(new_bufs)

        SM_SS, SM_VE, SM_RSTD, SM_NEGH, SM_EPS, SM_SS2, SM_RSTD2 = 0, 1, 2, 3, 4, 5, 6
        SM_MX, SM_L1, SM_L2, SM_LB, SM_R1, SM_R2, SM_NLAM, SM_TMP, SM_TMP2 = 8, 9, 10, 11, 12, 13, 14, 15, 16
        SM_NTOT = 17

        def smc(c, n=128):
            return sm[0:n, c:c + 1]

        S.pool(lambda e: e.memset(ident[:], 1.0), writes=[b_ident])
        S.pool(lambda e: e.affine_select(out=ident[:], in_=ident[:], pattern=[[-1, 128]], compare_op=ALU.is_equal,
                                         fill=0.0, base=0, channel_multiplier=1), reads=[b_ident], writes=[b_ident])
        S.pool(lambda e: e.memset(identf[:], 1.0), writes=[b_ident])
        S.pool(lambda e: e.affine_select(out=identf[:], in_=identf[:], pattern=[[-1, 128]], compare_op=ALU.is_equal,
                                         fill=0.0, base=0, channel_multiplier=1), reads=[b_ident], writes=[b_ident])
        S.pool(lambda e: e.memset(sm[:], 0.0), writes=[b_sm])
        S.pool(lambda e: e.memset(smc(SM_NEGH), -0.5), writes=[b_sm])
        S.pool(lambda e: e.memset(smc(SM_EPS), EPS), writes=[b_sm])
        gnames = [("ev_norm_pre", 0), ("ffn_norm_pre", 0), ("od_norm_pre", 0), ("ffn_norm_pre", 1)]
        for j, (nm, li) in enumerate(gnames):
            src = Wd[nm][li:li + 1, :].rearrange("o (k p) -> (o k) p", p=128)
            S.dma("sp", lambda e, j=j, src=src: e.dma_start(out=g16[:, j * 128:(j + 1) * 128], in_=src), writes=[b_g16])
        for j in range(len(gnames)):
            S.pe(lambda e, j=j: e.matmul(PS[:, 3584 + j * 16:3584 + (j + 1) * 16], lhsT=g16[:, j * 128:(j + 1) * 128],
                                          rhs=identf[0:16, 0:16], start=True, stop=True),
                 reads=[b_g16, b_ident], writes=[PB[7]])
        S.dve(lambda e: e.tensor_copy(out=gcols[:, 0:64], in_=PS[:, 3584:3584 + 64]), reads=[PB[7]], writes=[b_gcols])
        GC_EV, GC_F0, GC_OD, GC_F1 = 0, 1, 2, 3

        def rstd_from_ss(c_ss, c_out, n_feat, np_=128):
            S.dve(lambda e: e.tensor_scalar(out=smc(SM_VE, np_), in0=smc(c_ss, np_), scalar1=1.0 / n_feat, scalar2=EPS,
                                            op0=ALU.mult, op1=ALU.add), reads=[b_sm], writes=[b_sm])
            S.pool(lambda e: e.tensor_tensor(out=smc(c_out, np_), in0=smc(SM_VE, np_), in1=smc(SM_NEGH, np_), op=ALU.pow),
                   reads=[b_sm], writes=[b_sm])

        b_negh = Buf("negh")
        negh = sm[:, 120:121]
        S.pool(lambda e: e.memset(negh, -0.5), writes=[b_negh])
        st["sm"] = 20

        def smalloc():
            c = st["sm"]
            st["sm"] += 1
            assert c < 120
            return sm[:, c:c + 1], Buf(f"sm{c}")

        def rstd_op(ss, b_ss, ve, b_ve, rs, b_rs, n_feat):
            S.dve(lambda e: e.tensor_scalar(out=ve, in0=ss, scalar1=1.0 / n_feat, scalar2=EPS, op0=ALU.mult, op1=ALU.add),
                  reads=[b_ss], writes=[b_ve])
            S.pool(lambda e: e.tensor_tensor(out=rs, in0=ve, in1=negh, op=ALU.pow), reads=[b_ve, b_negh], writes=[b_rs])

        upd_sm = [[smalloc() for _ in range(6)] for _ in range(2)]

        def upd_pass(h_src, m_src, g_post, h_dst, a_mode, gcol_idx, final=False):
            NB3 = 3
            hstb = [R2f[:, i * 2048:(i + 1) * 2048] for i in range(NB3)]
            mstb = [R2f[:, 6144 + i * 2048:6144 + (i + 1) * 2048] for i in range(NB3)]
            xnbb = [R2[:, 24576 + i * 2048:24576 + (i + 1) * 2048] for i in range(2)]
            junkb = R2[:, 28672:30720]
            bh = [Buf(f"uh{i}") for i in range(NB3)]
            bm = [Buf(f"um{i}") for i in range(NB3)]
            bx = [Buf("ux0"), Buf("ux1")]
            bj = Buf("ujunk")
            region_switch(R2_all, bh + bm + bx + [bj])
            if m_src is not None:
                S.dma("sp", lambda e: e.dma_start(out=gpb[:], in_=g_post.partition_broadcast(128)), writes=[b_gpb])
            if a_mode is not None:
                region_switch(R1_all, Abuf)
            def s1(tt):
                i = tt % NB3
                hs_, ms_ = hstb[i], mstb[i]
                (ssm, b_ssm), (vem, b_vem), (rsm, b_rsm) = upd_sm[tt % 2][0:3]
                rows = slice(tt * 128, (tt + 1) * 128)
                rd = [b_hs[tt]] if h_src is hs else []
                S.dma("sp", lambda e: e.dma_start(out=hs_, in_=h_src[rows, :]), reads=rd, writes=[bh[i]])
                if m_src is not None:
                    S.dma("sp", lambda e: e.dma_start(out=ms_, in_=m_src[rows, :]), reads=[b_ms[tt]], writes=[bm[i]])
                    S.act(lambda e: e.activation(out=junkb, in_=ms_, func=AF.Square, accum_out=ssm),
                          reads=[bm[i]], writes=[bj, b_ssm])
                    rstd_op(ssm, b_ssm, vem, b_vem, rsm, b_rsm, D)
                    S.dve(lambda e: e.scalar_tensor_tensor(out=ms_, in0=ms_, scalar=rsm, in1=gpb[:], op0=ALU.mult, op1=ALU.mult),
                          reads=[bm[i], b_rsm, b_gpb], writes=[bm[i]])
                    S.dve(lambda e: e.tensor_tensor(out=hs_, in0=hs_, in1=ms_, op=ALU.add),
                          reads=[bh[i], bm[i]], writes=[bh[i]])
                if h_dst is not None:
                    o = S.dma("pool", lambda e: e.dma_start(out=h_dst[rows, :], in_=hs_), reads=[bh[i]],
                              writes=[b_hs[tt]] if h_dst is hs else [b_out])
                    if final:
                        out_ops.append(o)

            def s2(tt):
                if a_mode is None:
                    return
                i = tt % NB3
                ix = tt % 2
                hs_, xn_ = hstb[i], xnbb[ix]
                (ssh, b_ssh), (veh, b_veh), (rsh, b_rsh) = upd_sm[ix][3:6]
                if a_mode == "norm":
                    S.act(lambda e: e.activation(out=junkb, in_=hs_, func=AF.Square, accum_out=ssh),
                          reads=[bh[i]], writes=[bj, b_ssh])
                    rstd_op(ssh, b_ssh, veh, b_veh, rsh, b_rsh, D)
                    S.act(lambda e: e.activation(out=xn_, in_=hs_, func=AF.Copy, scale=rsh),
                          reads=[bh[i], b_rsh], writes=[bx[ix]])
                else:
                    S.act(lambda e: e.activation(out=xn_, in_=hs_, func=AF.Copy), reads=[bh[i]], writes=[bx[ix]])
                for half in range(2):
                    bk = 4 + half
                    for j in range(8):
                        k = half * 8 + j
                        S.pe(lambda e, bk=bk, j=j, k=k: e.transpose(out=bank_bf(bk)[:, j * 128:(j + 1) * 128],
                                                                    in_=xn_[:, k * 128:(k + 1) * 128], identity=ident[:]),
                             reads=[bx[ix], b_ident], writes=[PB[bk]])
                    dst = A[:, half * 8:(half + 1) * 8, tt * 128:(tt + 1) * 128]
                    src = bank_bf(bk).rearrange("p (k t) -> p k t", k=8)
                    if a_mode == "norm":
                        gc = gcols[:, gcol_idx * 16 + half * 8: gcol_idx * 16 + half * 8 + 8].unsqueeze(2).to_broadcast([128, 8, 128])
                        S.dve(lambda e, dst=dst, src=src, gc=gc: e.tensor_tensor(out=dst, in0=src, in1=gc, op=ALU.mult),
                              reads=[PB[bk], b_gcols], writes=[Abuf[tt]])
                    else:
                        S.dve(lambda e, dst=dst, src=src: e.tensor_copy(out=dst, in_=src), reads=[PB[bk]], writes=[Abuf[tt]])

            for n in range(-1, NT):
                if n + 1 < NT:
                    s1(n + 1)
                if n >= 0:
                    s2(n)

        pref = {}

        def prefetch_w(wap, kc):
            k = repr(wap)
            if k in pref:
                return
            pref[k] = load_w(wap, kc, _nopref=True)

        def load_w(wap, kc, eng="pool", _nopref=False):
            if not _nopref:
                k = repr(wap)
                if k in pref:
                    return pref.pop(k)
            i = st["ws"] % 2
            st["ws"] += 1
            ncols = wap.shape[1]
            src = wap.rearrange("(k p) n -> p k n", p=128)
            S.dma(eng, lambda e: e.dma_start(out=WS[i][:, 0:kc, 0:ncols], in_=src), writes=[WSb[i]])
            return i

        def next_pb(n=4):
            b = st["pb"] % n
            st["pb"] += 1
            return b

        def next_stg():
            i = st["stg"] % 4
            st["stg"] += 1
            return i

        def linear_A(X, Xbufs_for_tb, wap, kc, evac):
            N = wap.shape[1]
            for cb in range(N // 512):
                wi = load_w(wap[:, cb * 512:(cb + 1) * 512], kc)
                for nch in range(4):
                    for tb in range(4):
                        b = next_pb()
                        for k in range(kc):
                            S.pe(lambda e, b=b, wi=wi, k=k, nch=nch, tb=tb: e.matmul(
                                bank(b), lhsT=WS[wi][:, k, nch * 128:(nch + 1) * 128], rhs=X[:, k, tb * 512:(tb + 1) * 512],
                                start=(k == 0), stop=(k == kc - 1)),
                                 reads=[WSb[wi]] + Xbufs_for_tb(tb), writes=[PB[b]])
                        evac(bank(b), PB[b], cb * 4 + nch, tb)

        def linear_B(X, Xbufs_for_tt, wap, kc, evac, extra=None):
            N = wap.shape[1]
            for cb in range(N // 512):
                wi = load_w(wap[:, cb * 512:(cb + 1) * 512], kc)
                ex = extra(cb) if extra is not None else None
                for tt in range(NT):
                    b = next_pb()
                    for k in range(kc):
                        S.pe(lambda e, b=b, wi=wi, k=k, tt=tt: e.matmul(
                            bank(b), lhsT=X[:, k, tt * 128:(tt + 1) * 128], rhs=WS[wi][:, k, 0:512],
                            start=(k == 0), stop=(k == kc - 1)),
                             reads=[WSb[wi]] + Xbufs_for_tt(tt), writes=[PB[b]])
                    evac(bank(b), PB[b], tt, cb, ex)

        def A_tb(tb):
            return Abuf[tb * 4:(tb + 1) * 4]

        def A_tt(tt):
            return [Abuf[tt]]

        def B_all(_):
            return Bbuf

        def evac_to_ms(ps, pbuf, tt, cb, ex=None):
            i = next_stg()
            S.act(lambda e: e.activation(out=stg[i][:], in_=ps, func=AF.Copy), reads=[pbuf], writes=[b_stg[i]])
            S.dma("sp", lambda e: e.dma_start(out=ms[tt * 128:(tt + 1) * 128, cb * 512:(cb + 1) * 512], in_=stg[i][:]),
                  reads=[b_stg[i]], writes=[b_ms[tt]])

        def ffn(li):
            w1 = Wd["ffn_w1"][li]
            w2 = Wd["ffn_w2"][li]
            region_switch(R2_all, Bbuf)
            for fb in range(4):
                def evac1(ps, pbuf, nchg, tb):
                    i = next_stg()
                    S.act(lambda e: e.activation(out=stg[i][:], in_=ps, func=AF.Relu), reads=[pbuf], writes=[b_stg[i]])
                    S.dve(lambda e: e.scalar_tensor_tensor(out=Bb[:, nchg, tb * 512:(tb + 1) * 512], in0=ps, scalar=0.0,
                                                           in1=stg[i][:], op0=ALU.max, op1=ALU.mult),
                          reads=[pbuf, b_stg[i]], writes=[Bbuf[nchg]])
                linear_A(A, A_tb, w1[:, fb * 2048:(fb + 1) * 2048], KC, evac1)

                def evac2(ps, pbuf, tt, cb, ex=None, fb=fb):
                    i = next_stg()
                    dst = ms[tt * 128:(tt + 1) * 128, cb * 512:(cb + 1) * 512]
                    if fb == 0:
                        S.act(lambda e: e.activation(out=stg[i][:], in_=ps, func=AF.Copy), reads=[pbuf], writes=[b_stg[i]])
                    else:
                        S.dma("sp", lambda e: e.dma_start(out=stg[i][:], in_=dst), reads=[b_ms[tt]], writes=[b_stg[i]])
                        S.dve(lambda e: e.tensor_tensor(out=stg[i][:], in0=ps, in1=stg[i][:], op=ALU.add),
                              reads=[pbuf, b_stg[i]], writes=[b_stg[i]])
                    S.dma("sp", lambda e: e.dma_start(out=dst, in_=stg[i][:]), reads=[b_stg[i]], writes=[b_ms[tt]])
                linear_B(Bb, B_all, w2[fb * 2048:(fb + 1) * 2048, :], KC, evac2)

        def ple(li):
            pT = R2[:, 0:4096].rearrange("p (k t) -> p k t", k=2)
            b_pT = Buf("pT")
            region_switch(R2_all, [b_pT])
            for tt in range(NT):
                i = next_stg()
                S.dma("sp", lambda e, i=i, tt=tt: e.dma_start(out=stg[i][:, 0:256], in_=p_in[li, tt * 128:(tt + 1) * 128, :]),
                      writes=[b_stg[i]])
                S.act(lambda e, i=i: e.activation(out=xnb[:, 0:256], in_=stg[i][:, 0:256], func=AF.Copy),
                      reads=[b_stg[i]], writes=[b_xnb])
                for k in range(2):
                    S.pe(lambda e, k=k: e.transpose(out=bank_bf(4)[:, k * 128:(k + 1) * 128], in_=xnb[:, k * 128:(k + 1) * 128],
                                                    identity=ident[:]), reads=[b_xnb, b_ident], writes=[PB[4]])
                S.dve(lambda e, tt=tt: e.tensor_copy(out=pT[:, :, tt * 128:(tt + 1) * 128],
                                                     in_=bank_bf(4)[:, 0:256].rearrange("p (k t) -> p k t", k=2)),
                      reads=[PB[4]], writes=[b_pT])
            wg = Wd["ple_w_gate"][li]
            wp = Wd["ple_w_proj"][li]

            def extra(cb):
                return load_w(wp[:, cb * 512:(cb + 1) * 512], 2)

            def evac(ps, pbuf, tt, cb, wpi):
                S.pe(lambda e: e.matmul(bank(6), lhsT=pT[:, 0, tt * 128:(tt + 1) * 128], rhs=WS[wpi][:, 0, 0:512], start=True, stop=False),
                     reads=[b_pT, WSb[wpi]], writes=[PB[6]])
                S.pe(lambda e: e.matmul(bank(6), lhsT=pT[:, 1, tt * 128:(tt + 1) * 128], rhs=WS[wpi][:, 1, 0:512], start=False, stop=True),
                     reads=[b_pT, WSb[wpi]], writes=[PB[6]])
                i = next_stg()
                S.act(lambda e: e.activation(out=stg[i][:], in_=ps, func=AF.Sigmoid), reads=[pbuf], writes=[b_stg[i]])
                S.dve(lambda e: e.tensor_tensor(out=stg[i][:], in0=bank(6), in1=stg[i][:], op=ALU.mult),
                      reads=[PB[6], b_stg[i]], writes=[b_stg[i]])
                S.dma("sp", lambda e: e.dma_start(out=ms[tt * 128:(tt + 1) * 128, cb * 512:(cb + 1) * 512], in_=stg[i][:]),
                      reads=[b_stg[i]], writes=[b_ms[tt]])
            N = D
            for cb in range(N // 512):
                wpi = extra(cb)
                wi = load_w(wg[:, cb * 512:(cb + 1) * 512], KC)
                for tt in range(NT):
                    b = next_pb()
                    for k in range(KC):
                        S.pe(lambda e, b=b, wi=wi, k=k, tt=tt: e.matmul(
                            bank(b), lhsT=A[:, k, tt * 128:(tt + 1) * 128], rhs=WS[wi][:, k, 0:512],
                            start=(k == 0), stop=(k == KC - 1)), reads=[WSb[wi], Abuf[tt]], writes=[PB[b]])
                    evac(bank(b), PB[b], tt, cb, wpi)

        def even_mixer():
            cosT = R2f[0:32, 0:2048]
            sinT = R2f[0:32, 2048:4096]
            ang = R2f[0:32, 4096:6144]
            tmp = R2f[0:32, 6144:8192]
            tmp2 = R2f[0:32, 8192:10240]
            posi = R2f[0:32, 10240:12288].bitcast(I32)
            frow = R2f[0:1, 15360:15360 + 32]
            one2 = R2f[0:1, 15392:15394]
            b_rope = Buf("rope")
            region_switch(R2_all, [b_rope])
            S.dma("sp", lambda e: e.dma_start(out=posi, in_=pos_in.partition_broadcast(32)), writes=[b_rope])
            S.dve(lambda e: e.tensor_copy(out=ang, in_=posi), reads=[b_rope], writes=[b_rope])
            inv = (np.float32(ROPE_THETA) ** (-(np.arange(0, ROT, 2, dtype=np.float32)) / np.float32(ROT))).astype(np.float32)
            for i in range(16):
                for hh in range(2):
                    c = hh * 16 + i
                    S.pool(lambda e, c=c, v=float(inv[i]): e.memset(frow[:, c:c + 1], v), writes=[b_rope])
            S.pool(lambda e: e.memset(one2, 1.0), writes=[b_rope])
            S.pe(lambda e: e.matmul(PS[0:32, 3584:3586], lhsT=frow, rhs=one2, start=True, stop=True), reads=[b_rope], writes=[PB[7]])
            S.dve(lambda e: e.tensor_copy(out=sm[0:32, SM_TMP:SM_TMP + 1], in_=PS[0:32, 3584:3585]), reads=[PB[7]], writes=[b_sm])
            S.dve(lambda e: e.tensor_scalar(out=ang, in0=ang, scalar1=sm[0:32, SM_TMP:SM_TMP + 1], scalar2=None, op0=ALU.mult),
                  reads=[b_rope, b_sm], writes=[b_rope])
            S.dve(lambda e: e.tensor_scalar(out=tmp, in0=ang, scalar1=1.0 / TWO_PI, scalar2=MAGIC, op0=ALU.mult, op1=ALU.add),
                  reads=[b_rope], writes=[b_rope])
            S.dve(lambda e: e.tensor_scalar(out=tmp, in0=tmp, scalar1=-MAGIC, scalar2=None, op0=ALU.add), reads=[b_rope], writes=[b_rope])
            for cst in (C1, C2, C3):
                S.dve(lambda e, cst=cst: e.scalar_tensor_tensor(out=ang, in0=tmp, scalar=-cst, in1=ang, op0=ALU.mult, op1=ALU.add),
                      reads=[b_rope], writes=[b_rope])
            S.dve(lambda e: e.tensor_scalar(out=ang, in0=ang, scalar1=3.1415925, scalar2=-3.1415925, op0=ALU.min, op1=ALU.max),
                  reads=[b_rope], writes=[b_rope])
            S.act(lambda e: e.activation(out=sinT, in_=ang, func=AF.Sin), reads=[b_rope], writes=[b_rope])
            S.dve(lambda e: e.tensor_scalar(out=tmp2, in0=ang, scalar1=-1.0, scalar2=None, op0=ALU.mult), reads=[b_rope], writes=[b_rope])
            S.dve(lambda e: e.tensor_tensor(out=tmp2, in0=tmp2, in1=ang, op=ALU.max), reads=[b_rope], writes=[b_rope])
            S.dve(lambda e: e.tensor_scalar(out=tmp2, in0=tmp2, scalar1=-1.0, scalar2=math.pi / 2, op0=ALU.mult, op1=ALU.add),
                  reads=[b_rope], writes=[b_rope])
            S.act(lambda e: e.activation(out=cosT, in_=tmp2, func=AF.Sin), reads=[b_rope], writes=[b_rope])
            S.pool(lambda e: e.memset(pmat[:], 0.0), writes=[b_pmat])
            S.pool(lambda e: e.affine_select(out=pmat[:, 0:16], in_=pmat[:, 0:16], pattern=[[-1, 16]], compare_op=ALU.not_equal,
                                             fill=-1.0, base=-16, channel_multiplier=1), reads=[b_pmat], writes=[b_pmat])
            S.pool(lambda e: e.affine_select(out=pmat[:, 16:32], in_=pmat[:, 16:32], pattern=[[-1, 16]], compare_op=ALU.not_equal,
                                             fill=1.0, base=0, channel_multiplier=1), reads=[b_pmat], writes=[b_pmat])
            w_in = Wd["ev_w_in"]
            ostg = [R2[:, 24576 + i * 2048: 24576 + (i + 1) * 2048] for i in range(2)]
            b_ostg = [Buf("ostg0"), Buf("ostg1")]
            region_switch(R2_all, b_ostg)
            cnt = {"o": 0}

            def mk_evacA(dst_scr, dst_name, scale, rope):
                def evac(ps, pbuf, nchg, tb):
                    oi = (cnt["o"] // 4) % 2
                    cnt["o"] += 1
                    od = ostg[oi][:, tb * 512:(tb + 1) * 512]
                    if not rope:
                        S.act(lambda e: e.activation(out=od, in_=ps, func=AF.Copy, scale=scale), reads=[pbuf], writes=[b_ostg[oi]])
                    else:
                        i = next_stg()
                        S.act(lambda e: e.activation(out=stg[i][:], in_=ps, func=AF.Copy, scale=scale), reads=[pbuf], writes=[b_stg[i]])
                        S.pe(lambda e: e.matmul(PS[0:32, 3072:3584], lhsT=pmat[:], rhs=stg[i][:], start=True, stop=True),
                             reads=[b_pmat, b_stg[i]], writes=[PB[6]])
                        S.act(lambda e: e.activation(out=od, in_=stg[i][:], func=AF.Copy),
                              reads=[b_stg[i]], writes=[b_ostg[oi]])
                        S.dve(lambda e: e.tensor_tensor(out=tmp[:, 0:512], in0=PS[0:32, 3072:3584], in1=sinT[:, tb * 512:(tb + 1) * 512],
                                                        op=ALU.mult), reads=[PB[6], b_rope], writes=[b_rope])
                        S.dve(lambda e: e.tensor_tensor(out=tmp2[:, 0:512], in0=stg[i][0:32, :], in1=cosT[:, tb * 512:(tb + 1) * 512],
                                                        op=ALU.mult), reads=[b_stg[i], b_rope], writes=[b_rope])
                        S.dve(lambda e: e.tensor_tensor(out=od[0:32, :], in0=tmp[:, 0:512], in1=tmp2[:, 0:512], op=ALU.add),
                              reads=[b_rope], writes=[b_ostg[oi]])
                    if tb == 3:
                        S.dma("sp", lambda e: e.dma_start(out=dst_scr[nchg * 128:(nchg + 1) * 128, :], in_=ostg[oi]),
                              reads=[b_ostg[oi]], writes=[b_scr[dst_name]])
                return evac

            def mk_evacB(dst_scr, dst_name):
                def evac(ps, pbuf, tt, cb, ex=None):
                    i = next_stg()
                    sv = stg[i][:].bitcast(BF16)[:, 0:512]
                    S.act(lambda e: e.activation(out=sv, in_=ps, func=AF.Copy), reads=[pbuf], writes=[b_stg[i]])
                    S.dma("sp", lambda e: e.dma_start(out=dst_scr[tt * 128:(tt + 1) * 128, cb * 512:(cb + 1) * 512], in_=sv),
                          reads=[b_stg[i]], writes=[b_scr[dst_name]])
                return evac

            linear_A(A, A_tb, w_in[:, 0:1024], KC, mk_evacA(sbqT, "sbqT", SCALE, False))
            linear_A(A, A_tb, w_in[:, 1024:2048], KC, mk_evacA(sbkT, "sbkT", 1.0, False))
            linear_B(A, A_tt, w_in[:, 2048:3072], KC, mk_evacB(sbv, "sbv"))
            linear_A(A, A_tb, w_in[:, 3072:4096], KC, mk_evacA(dfqT, "dfqT", SCALE, True))
            linear_A(A, A_tb, w_in[:, 4096:5120], KC, mk_evacA(dfkT, "dfkT", 1.0, True))
            linear_B(A, A_tt, w_in[:, 5120:6144], KC, mk_evacB(dfv, "dfv"))
            if stop == "qkv":
                return
            prefetch_w(Wd["ev_w_out"][:, 0:512], KC)
            prefetch_w(Wd["ev_w_out"][:, 512:1024], KC)
            attention()

        def attention():
            ef = [R1f[:, i * 6144:i * 6144 + 2048] for i in range(2)]
            spf = [R1f[:, i * 6144 + 2048:i * 6144 + 4096] for i in range(2)]
            csf = [R1f[:, i * 6144 + 4096:i * 6144 + 6144] for i in range(2)]
            wb = [R1[:, 24576 + i * 2048:24576 + (i + 1) * 2048] for i in range(2)]
            wTb = [R1[:, 28672 + i * 2048:28672 + (i + 1) * 2048].rearrange("p (c t) -> p c t", c=16) for i in range(2)]
            qT = [R2[:, i * 2048:(i + 1) * 2048] for i in range(4)]
            vv = R2[:, 8192:12288]
            oT = R2[:, 12288:16384].rearrange("p (j t) -> p j t", j=2)
            onbb = [R2[:, 16384 + i * 256:16384 + (i + 1) * 256] for i in range(2)]
            gsub = R2f[:, 8448:8704]
            lamt = R2f[:, 8704:9216]
            junk = R2f[:, 9216:11264]
            b_e = [Buf("e0"), Buf("e1")]
            b_sp = [Buf("sp0"), Buf("sp1")]
            b_cs = [Buf("cs0"), Buf("cs1")]
            b_w = [Buf("w0"), Buf("w1")]
            b_wT = [Buf("wT0"), Buf("wT1")]
            b_onb = [Buf("onb0"), Buf("onb1")]
            b_v, b_oT, b_gsub, b_junk, b_lam = (Buf(n) for n in ["v", "oT", "gsub", "junk", "lam"])
            b_q = [Buf(f"q{i}") for i in range(4)]
            region_switch(R1_all, b_e + b_sp + b_cs + b_w + b_wT)
            region_switch(R2_all, b_q + b_onb + [b_v, b_oT, b_gsub, b_junk])
            PB6s = [Buf(f"pb6_{i}") for i in range(4)]
            PB7s = [Buf(f"pb7_{i}") for i in range(2)]
            for bb in PB6s:
                bb.W = list(PB[6].W); bb.R = list(PB[6].R)
            for bb in PB7s:
                bb.W = list(PB[7].W); bb.R = list(PB[7].R)
            lambda_init = 0.8 - 0.6 * math.exp(-0.3 * 0)
            (t1, b_t1), (t2, b_t2), (nlam, b_nlam) = smalloc(), smalloc(), smalloc()
            for i, nm in enumerate(["ev_lam_q1", "ev_lam_k1", "ev_lam_q2", "ev_lam_k2"]):
                S.dma("sp", lambda e, i=i, nm=nm: e.dma_start(out=lamt[:, i * 128:(i + 1) * 128], in_=Wd[nm].partition_broadcast(128)),
                      writes=[b_gsub])
            S.dma("sp", lambda e, i=i: e.dma_start(out=gsub, in_=Wd["ev_subln"].partition_broadcast(128)), writes=[b_gsub])
            S.dve(lambda e, i=i: e.tensor_tensor(out=junk[:, 0:128], in0=lamt[:, 0:128], in1=lamt[:, 128:256], op=ALU.mult),
                  reads=[b_gsub], writes=[b_junk])
            S.dve(lambda e, i=i: e.tensor_reduce(out=t1, in_=junk[:, 0:128], axis=AX.X, op=ALU.add), reads=[b_junk], writes=[b_t1])
            S.dve(lambda e, i=i: e.tensor_tensor(out=junk[:, 0:128], in0=lamt[:, 256:384], in1=lamt[:, 384:512], op=ALU.mult),
                  reads=[b_gsub], writes=[b_junk])
            S.dve(lambda e, i=i: e.tensor_reduce(out=t2, in_=junk[:, 0:128], axis=AX.X, op=ALU.add), reads=[b_junk], writes=[b_t2])
            S.act(lambda e, i=i: e.activation(out=t1, in_=t1, func=AF.Exp), reads=[b_t1], writes=[b_t1])
            S.act(lambda e, i=i: e.activation(out=t2, in_=t2, func=AF.Exp), reads=[b_t2], writes=[b_t2])
            S.dve(lambda e, i=i: e.tensor_tensor(out=nlam, in0=t2, in1=t1, op=ALU.subtract), reads=[b_t1, b_t2], writes=[b_nlam])
            S.dve(lambda e, i=i: e.tensor_scalar(out=nlam, in0=nlam, scalar1=-lambda_init, scalar2=None, op0=ALU.add),
                  reads=[b_nlam], writes=[b_nlam])
            S.dve(lambda e, i=i: e.tensor_scalar(out=gsub, in0=gsub, scalar1=(1.0 - lambda_init), scalar2=None, op0=ALU.mult),
                  reads=[b_gsub], writes=[b_gsub])

            zps = PS[:, 0:2048]
            ZB = PB[0:4]
            it = {"n": 0}

            def transposes_w(nblk, i):
                for g0 in range(0, nblk, 8):
                    bk = 4 + (g0 // 8) % 2
                    n = min(8, nblk - g0)
                    for j in range(n):
                        c = g0 + j
                        S.pe(lambda e, i=i, bk=bk, j=j, c=c: e.transpose(out=bank_bf(bk)[:, j * 128:(j + 1) * 128],
                                                                    in_=wb[i][:, c * 128:(c + 1) * 128], identity=ident[:]),
                             reads=[b_w[i], b_ident], writes=[PB[bk]])
                    src = bank_bf(bk)[:, 0:n * 128].rearrange("p (c t) -> p c t", c=n)
                    if (g0 // 8) % 2 == 0:
                        S.act(lambda e, i=i, src=src, g0=g0, n=n: e.activation(out=wTb[i][:, g0:g0 + n, :], in_=src, func=AF.Copy),
                              reads=[PB[bk]], writes=[b_wT[i]])
                    else:
                        S.dve(lambda e, i=i, src=src, g0=g0, n=n: e.tensor_copy(out=wTb[i][:, g0:g0 + n, :], in_=src),
                              reads=[PB[bk]], writes=[b_wT[i]])

            def run_pipeline(iters):
                N = len(iters)
                for n in range(-2, N):
                    if 0 <= n + 2 < N:
                        iters[n + 2][0]()
                    if 0 <= n + 1 < N:
                        iters[n + 1][1]()
                    if 0 <= n < N:
                        iters[n][2]()

            vvb = [R2[:, 8192:12288], R2[:, 22528:26624]]
            b_vb = [b_v, Buf("v1")]
            qk2 = [R2[:, 26624:28672], R2[:, 28672:30720]]
            b_qk2 = [Buf("q1b"), Buf("k1b")]
            region_switch(R2_all, [b_vb[1]] + b_qk2)
            sb_sm = [smalloc() for _ in range(2)]
            iters = []
            for hh in range(8):
                hp = hh % 2
                qh = qT[0] if hp == 0 else qk2[0]
                kh = qT[1] if hp == 0 else qk2[1]
                b_qh = b_q[0] if hp == 0 else b_qk2[0]
                b_kh = b_q[1] if hp == 0 else b_qk2[1]
                v3 = vvb[hp][:, 0:2048].rearrange("p (c d) -> p c d", c=16)
                b_vh = b_vb[hp]
                for qt in range(NT):
                    i = it["n"] % 2
                    it["n"] += 1

                    def a1(hh=hh, qt=qt, i=i, qh=qh, kh=kh, b_qh=b_qh, b_kh=b_kh, v3=v3, b_vh=b_vh):
                        if qt == 0:
                            S.dma("sp", lambda e: e.dma_start(out=qh, in_=sbqT[hh * 128:(hh + 1) * 128, :]), reads=[b_scr["sbqT"]], writes=[b_qh])
                            S.dma("sp", lambda e: e.dma_start(out=kh, in_=sbkT[hh * 128:(hh + 1) * 128, :]), reads=[b_scr["sbkT"]], writes=[b_kh])
                            S.dma("sp", lambda e: e.dma_start(out=v3, in_=sbv[:, hh * 128:(hh + 1) * 128].rearrange("(c p) d -> p c d", p=128)),
                                  reads=[b_scr["sbv"]], writes=[b_vh])
                        e_f, sp_f = ef[i], spf[i]
                        Wk = (qt + 1) * 128
                        tq = slice(qt * 128, (qt + 1) * 128)
                        nb = (Wk + 511) // 512
                        for kb in range(nb):
                            k0, k1 = kb * 512, min((kb + 1) * 512, Wk)
                            S.pe(lambda e, k0=k0, k1=k1: e.matmul(zps[:, k0:k1], lhsT=qh[:, tq], rhs=kh[:, k0:k1], start=True, stop=True),
                                 reads=[b_qh, b_kh], writes=[ZB[kb]])
                        zb = ZB[0:nb]
                        S.act(lambda e: e.activation(out=e_f[:, 0:Wk], in_=zps[:, 0:Wk], func=AF.Exp), reads=zb, writes=[b_e[i]])
                        S.act(lambda e: e.activation(out=sp_f[:, 0:Wk], in_=e_f[:, 0:Wk], func=AF.Ln, bias=1.0),
                              reads=[b_e[i]], writes=[b_sp[i]])
                        S.pool(lambda e: e.affine_select(out=sp_f[:, Wk - 128:Wk], in_=sp_f[:, Wk - 128:Wk], pattern=[[-1, 128]],
                                                         compare_op=ALU.is_gt, fill=0.0, base=0, channel_multiplier=1),
                               reads=[b_sp[i]], writes=[b_sp[i]])

                    def a2(hh=hh, qt=qt, i=i):
                        ntot, b_nt = sb_sm[i]
                        e_f, sp_f, cs_f = ef[i], spf[i], csf[i]
                        Wk = (qt + 1) * 128
                        S.dve(lambda e: e.tensor_tensor_scan(out=cs_f[:, 0:Wk], data0=sp_f[:, 0:Wk], data1=sp_f[:, 0:Wk],
                                                             initial=0.0, op0=ALU.add, op1=ALU.bypass),
                              reads=[b_sp[i]], writes=[b_cs[i]])
                        S.dve(lambda e: e.tensor_scalar(out=ntot, in0=cs_f[:, Wk - 1:Wk], scalar1=-1.0, scalar2=None, op0=ALU.mult),
                              reads=[b_cs[i]], writes=[b_nt])
                        S.act(lambda e: e.activation(out=sp_f[:, 0:1], in_=ntot, func=AF.Exp), reads=[b_nt, b_sp[i], b_cs[i]], writes=[b_sp[i]])
                        S.act(lambda e: e.activation(out=sp_f[:, 1:Wk], in_=cs_f[:, 0:Wk - 1], func=AF.Exp, bias=ntot),
                              reads=[b_cs[i], b_nt], writes=[b_sp[i]])
                        S.dve(lambda e: e.tensor_tensor(out=wb[i][:, 0:Wk], in0=e_f[:, 0:Wk], in1=sp_f[:, 0:Wk], op=ALU.mult),
                              reads=[b_e[i], b_sp[i]], writes=[b_w[i]])
                        S.pool(lambda e: e.affine_select(out=wb[i][:, Wk - 128:Wk], in_=wb[i][:, Wk - 128:Wk], pattern=[[-1, 128]],
                                                         compare_op=ALU.is_gt, fill=0.0, base=0, channel_multiplier=1),
                               reads=[b_w[i]], writes=[b_w[i]])

                    def bst(hh=hh, qt=qt, i=i, v3=v3, b_vh=b_vh):
                        tq = slice(qt * 128, (qt + 1) * 128)
                        transposes_w(qt + 1, i)
                        sl = (hh * NT + qt) % 4
                        ops_ = PS[:, 3072 + sl * 128: 3072 + (sl + 1) * 128]
                        for c in range(qt + 1):
                            S.pe(lambda e, c=c: e.matmul(ops_, lhsT=v3[:, c, :], rhs=wTb[i][:, c, :], start=(c == 0), stop=(c == qt)),
                                 reads=[b_vh, b_wT[i]], writes=[PB6s[sl]])
                        S.act(lambda e: e.activation(out=oT[:, 0, tq], in_=ops_, func=AF.Copy), reads=[PB6s[sl]], writes=[b_oT])
                        if qt == NT - 1:
                            S.dma("sp", lambda e: e.dma_start(out=mergedT[hh * 128:(hh + 1) * 128, :], in_=oT[:, 0, :]), reads=[b_oT],
                                  writes=[b_scr["mergedT"]])
                    iters.append((a1, a2, bst))
            run_pipeline(iters)
            if stop == "sb":
                return
            df_sm = [[smalloc() for _ in range(11)] for _ in range(2)]
            iters = []
            for hh in range(4):
                hp = hh % 2
                v3 = vvb[hp].rearrange("p (c d) -> p c d", c=16)
                b_vh = b_vb[hp]
                for qt in range(NT):
                    i = it["n"] % 2
                    it["n"] += 1

                    def a1(maps=(0, 1), do_load=True, hh=hh, qt=qt, i=i, v3=v3, b_vh=b_vh):
                        if qt == 0 and do_load:
                            for m in range(2):
                                S.dma("sp", lambda e, m=m: e.dma_start(out=qT[m], in_=dfqT[(hh * 2 + m) * 128:(hh * 2 + m + 1) * 128, :]),
                                      reads=[b_scr["dfqT"]], writes=[b_q[m]])
                                S.dma("sp", lambda e, m=m: e.dma_start(out=qT[2 + m], in_=dfkT[(hh * 2 + m) * 128:(hh * 2 + m + 1) * 128, :]),
                                      reads=[b_scr["dfkT"]], writes=[b_q[2 + m]])
                            S.dma("sp", lambda e: e.dma_start(out=v3, in_=dfv[:, hh * 256:(hh + 1) * 256].rearrange("(c p) d -> p c d", p=128)),
                                  reads=[b_scr["dfv"]], writes=[b_vh])
                        sms = df_sm[i]
                        Wk = (qt + 1) * 128
                        tq = slice(qt * 128, (qt + 1) * 128)
                        nb = (Wk + 511) // 512
                        zb = ZB[0:nb]
                        pms = [ef[i], csf[i]]
                        b_pms = [b_e[i], b_cs[i]]
                        for m in maps:
                            pm, b_pm = pms[m], b_pms[m]
                            (mx, b_mx), (ll, b_ll), (lb, b_lb) = sms[m * 3], sms[m * 3 + 1], sms[m * 3 + 2]
                            for kb in range(nb):
                                k0, k1 = kb * 512, min((kb + 1) * 512, Wk)
                                S.pe(lambda e, k0=k0, k1=k1, m=m: e.matmul(zps[:, k0:k1], lhsT=qT[m][:, tq], rhs=qT[2 + m][:, k0:k1],
                                                                            start=True, stop=True),
                                     reads=[b_q[m], b_q[2 + m]], writes=[ZB[kb]])
                            S.dve(lambda e, mx=mx: e.tensor_reduce(out=mx, in_=zps[:, 0:Wk], axis=AX.X, op=ALU.max), reads=zb, writes=[b_mx])
                            S.dve(lambda e, mx=mx: e.tensor_scalar(out=mx, in0=mx, scalar1=-1.0, scalar2=None, op0=ALU.mult),
                                  reads=[b_mx], writes=[b_mx])
                            S.pool(lambda e, lb=lb: e.memset(lb, 0.0), writes=[b_lb])
                            S.act(lambda e, pm=pm, mx=mx, ll=ll: e.activation(out=pm[:, 0:Wk - 64], in_=zps[:, 0:Wk - 64], func=AF.Exp,
                                                                             bias=mx, accum_out=ll),
                                  reads=zb + [b_mx], writes=[b_pm, b_ll])
                            S.act(lambda e, pm=pm, mx=mx, lb=lb: e.activation(out=pm[64:128, Wk - 64:Wk], in_=zps[64:128, Wk - 64:Wk], func=AF.Exp,
                                                                             bias=mx[64:128, :], accum_out=lb[64:128, :]),
                                  reads=zb + [b_mx, b_lb], writes=[b_pm, b_lb])
                            S.pool(lambda e, pm=pm: e.memset(pm[0:64, Wk - 64:Wk], 0.0), reads=[b_pm], writes=[b_pm])
                            S.dve(lambda e, ll=ll, lb=lb: e.tensor_tensor(out=ll, in0=ll, in1=lb, op=ALU.add), reads=[b_ll, b_lb], writes=[b_ll])

                    def a2(hh=hh, qt=qt, i=i):
                        sms = df_sm[i]
                        Wk = (qt + 1) * 128
                        (r1, b_r1), (r2, b_r2) = sms[6], sms[7]
                        l1, b_l1 = sms[1]
                        l2, b_l2 = sms[4]
                        p1, p2 = ef[i], csf[i]
                        S.dve(lambda e: e.reciprocal(out=r1, in_=l1), reads=[b_l1], writes=[b_r1])
                        S.dve(lambda e: e.reciprocal(out=r2, in_=l2), reads=[b_l2], writes=[b_r2])
                        S.dve(lambda e: e.tensor_tensor(out=r2, in0=r2, in1=nlam, op=ALU.mult), reads=[b_r2, b_nlam], writes=[b_r2])
                        S.act(lambda e: e.activation(out=p1[:, 0:Wk], in_=p1[:, 0:Wk], func=AF.Copy, scale=r1),
                              reads=[b_e[i], b_r1], writes=[b_e[i]])
                        S.dve(lambda e: e.scalar_tensor_tensor(out=wb[i][:, 0:Wk], in0=p2[:, 0:Wk], scalar=r2, in1=p1[:, 0:Wk],
                                                               op0=ALU.mult, op1=ALU.add),
                              reads=[b_e[i], b_cs[i], b_r2], writes=[b_w[i]])

                    def bst(hh=hh, qt=qt, i=i, v3=v3, b_vh=b_vh):
                        sms = df_sm[i]
                        (ss, b_ss), (ve, b_ve), (rs, b_rs) = sms[8], sms[9], sms[10]
                        tq = slice(qt * 128, (qt + 1) * 128)
                        transposes_w(qt + 1, i)
                        ops_ = PS[:, 3072 + i * 256:3072 + (i + 1) * 256]
                        for c in range(qt + 1):
                            S.pe(lambda e, c=c: e.matmul(ops_, lhsT=wTb[i][:, c, :], rhs=v3[:, c, :], start=(c == 0), stop=(c == qt)),
                                 reads=[b_vh, b_wT[i]], writes=[PB6s[i]])
                        S.act(lambda e: e.activation(out=junk[:, i * 256:(i + 1) * 256], in_=ops_, func=AF.Square, accum_out=ss),
                              reads=[PB6s[i]], writes=[b_junk, b_ss])
                        rstd_op(ss, b_ss, ve, b_ve, rs, b_rs, 256)
                        S.dve(lambda e: e.scalar_tensor_tensor(out=onbb[i], in0=ops_, scalar=rs, in1=gsub, op0=ALU.mult, op1=ALU.mult),
                              reads=[PB6s[i], b_rs, b_gsub], writes=[b_onb[i]])
                        tps = bank_bf(7)[:, i * 256:(i + 1) * 256]
                        for j in range(2):
                            S.pe(lambda e, j=j: e.transpose(out=tps[:, j * 128:(j + 1) * 128], in_=onbb[i][:, j * 128:(j + 1) * 128],
                                                            identity=ident[:]), reads=[b_onb[i], b_ident], writes=[PB7s[i]])
                        S.act(lambda e: e.activation(out=oT[:, :, tq], in_=tps.rearrange("p (j t) -> p j t", j=2), func=AF.Copy),
                              reads=[PB7s[i]], writes=[b_oT])
                        if qt == NT - 1:
                            S.dma("sp", lambda e: e.dma_start(
                                out=mergedT[1024 + hh * 256:1024 + (hh + 1) * 256, :].rearrange("(j p) t -> p j t", p=128), in_=oT),
                                reads=[b_oT], writes=[b_scr["mergedT"]])
                    iters.append((a1, a2, bst))
            run_pipeline(iters)
            for bb in PB6s:
                PB[6].W = sorted(set(PB[6].W) | set(bb.W)); PB[6].R = sorted(set(PB[6].R) | set(bb.R))
            for bb in PB7s:
                PB[7].W = sorted(set(PB[7].W) | set(bb.W)); PB[7].R = sorted(set(PB[7].R) | set(bb.R))

        def load_feature_major(dst3, dbufs, src_scr, src_name):
            region_switch(R2_all, Bbuf)
            for k in range(KC):
                S.dma("sp", lambda e, k=k: e.dma_start(out=dst3[:, k, :], in_=src_scr[k * 128:(k + 1) * 128, :]),
                      reads=[b_scr[src_name]], writes=[dbufs[k]])

        def odd_mixer():
            w_in = Wd["od_w_in"]
            region_switch(R2_all, Bbuf)
            def evac_u(ps, pbuf, nchg, tb):
                S.act(lambda e: e.activation(out=Bb[:, nchg, tb * 512:(tb + 1) * 512], in_=ps, func=AF.Gelu_apprx_tanh),
                      reads=[pbuf], writes=[Bbuf[nchg]])
            linear_A(A, A_tb, w_in[:, 0:2048], KC, evac_u)

            def evac_v(ps, pbuf, tt, cb, ex=None):
                i = next_stg()
                S.act(lambda e: e.activation(out=stg[i][:], in_=ps, func=AF.Gelu_apprx_tanh), reads=[pbuf], writes=[b_stg[i]])
                S.dma("sp", lambda e: e.dma_start(out=vsc[tt * 128:(tt + 1) * 128, cb * 512:(cb + 1) * 512], in_=stg[i][:]),
                      reads=[b_stg[i]], writes=[b_scr["vsc"]])
            linear_B(A, A_tt, w_in[:, 2048:4096], KC, evac_v)
            wsn = R1f[:, 0:2048].rearrange("p (g s) -> p g s", g=16)
            wsb = R1[:, 4096:6144].rearrange("p (g s) -> p g s", g=16)
            wmT = R1[:, 6144:8192].rearrange("p (g t) -> p g t", g=16)
            bsb = R1f[:, 4096:6144].rearrange("p (g t) -> p g t", g=16)
            lng = R1f[:, 6144:8192]
            lnb = R1f[:, 8192:10240]
            vnb = R1[:, 20480:22528]
            tmpf = R1f[:, 11264:11776]
            b_ws, b_wmT, b_bsb, b_ln, b_vnb, b_tmpf = (Buf(n) for n in ["wsn", "wmT", "bsb", "ln", "vnb", "tmpf"])
            region_switch(R1_all, [b_ws, b_wmT, b_bsb, b_ln, b_vnb, b_tmpf])
            S.dma("sp", lambda e: e.dma_start(out=wsn, in_=Wd["od_w_s"].rearrange("g t s -> t g s")), writes=[b_ws])
            S.dma("sp", lambda e: e.dma_start(out=bsb.rearrange("p g t -> p (g t)"), in_=Wd["od_b_s"].partition_broadcast(128)), writes=[b_bsb])
            S.dma("sp", lambda e: e.dma_start(out=lng, in_=Wd["od_ln_g"].partition_broadcast(128)), writes=[b_ln])
            S.dma("sp", lambda e: e.dma_start(out=lnb, in_=Wd["od_ln_b"].partition_broadcast(128)), writes=[b_ln])
            S.dve(lambda e: e.tensor_copy(out=wsb, in_=wsn), reads=[b_ws], writes=[b_ws])
            S.pool(lambda e: e.memset(wsb[0:64, :, 64:128], 0.0), reads=[b_ws], writes=[b_ws])
            for half in range(2):
                for j in range(8):
                    g = half * 8 + j
                    S.pe(lambda e, half=half, j=j, g=g: e.transpose(out=bank_bf(4 + half)[:, j * 128:(j + 1) * 128], in_=wsb[:, g, :],
                                                                    identity=ident[:]), reads=[b_ws, b_ident], writes=[PB[4 + half]])
                S.dve(lambda e, half=half: e.tensor_copy(out=wmT[:, half * 8:(half + 1) * 8, :],
                                                         in_=bank_bf(4 + half).rearrange("p (g t) -> p g t", g=8)),
                      reads=[PB[4 + half]], writes=[b_wmT])
            FMAX = 512
            stats = R1f[:, 12288:12288 + 4 * 6]
            mv = R1f[:, 12320:12322]
            prefetch_w(Wd["od_w_out"][:, 0:512], KC)
            prefetch_w(Wd["od_w_out"][:, 512:1024], KC)
            vbuf = [hst, mst]
            b_vbuf = [b_hst, b_mst]
            vnbb = [R1[:, 20480:22528], R1[:, 26624:28672]]
            b_vnbb = [b_vnb, Buf("vnb1")]
            region_switch(R1_all, [b_vnbb[1]])
            sp_sm = [[smalloc() for _ in range(3)] for _ in range(2)]
            statsb = [R1f[:, 12288 + i * 32:12288 + i * 32 + 24] for i in range(2)]
            mvb = [R1f[:, 12352 + i * 4:12352 + i * 4 + 2] for i in range(2)]
            b_stats = [Buf("stats0"), Buf("stats1")]
            region_switch(R1_all, b_stats)

            def g1(n):
                i = n % 2
                vt = vbuf[i]
                stats, mv = statsb[i], mvb[i]
                (ve, b_ve), (rs, b_rs), (nmr, b_nmr) = sp_sm[i]
                rows = slice(n * 128, (n + 1) * 128)
                S.dma("sp", lambda e: e.dma_start(out=vt[:], in_=vsc[rows, :]), reads=[b_scr["vsc"]], writes=[b_vbuf[i]])
                for c in range(4):
                    S.dve(lambda e, c=c: e.bn_stats(out=stats[:, c * 6:(c + 1) * 6], in_=vt[:, c * 512:(c + 1) * 512]), reads=[b_vbuf[i]], writes=[b_stats[i]])
                S.dve(lambda e: e.bn_aggr(out=mv, in_=stats), reads=[b_stats[i]], writes=[b_stats[i]])
                S.dve(lambda e: e.tensor_scalar(out=ve, in0=mv[:, 1:2], scalar1=EPS, scalar2=None, op0=ALU.add), reads=[b_stats[i]], writes=[b_ve])
                S.pool(lambda e: e.tensor_tensor(out=rs, in0=ve, in1=negh, op=ALU.pow), reads=[b_ve, b_negh], writes=[b_rs])
                S.dve(lambda e: e.scalar_tensor_tensor(out=nmr, in0=mv[:, 0:1], scalar=-1.0, in1=rs, op0=ALU.mult, op1=ALU.mult),
                      reads=[b_stats[i], b_rs], writes=[b_nmr])
                S.act(lambda e: e.activation(out=vt[:], in_=vt[:], func=AF.Identity, scale=rs, bias=nmr), reads=[b_vbuf[i], b_rs, b_nmr], writes=[b_vbuf[i]])
                S.pool(lambda e: e.tensor_tensor(out=vt[:], in0=vt[:], in1=lng, op=ALU.mult), reads=[b_vbuf[i], b_ln], writes=[b_vbuf[i]])
                S.dve(lambda e: e.tensor_tensor(out=vnbb[i], in0=vt[:], in1=lnb, op=ALU.add), reads=[b_vbuf[i], b_ln], writes=[b_vnbb[i]])

            def g2(n):
                i = n % 2
                vn = vnbb[i]
                for g4 in range(4):
                    b = next_pb()
                    for j in range(4):
                        g = g4 * 4 + j
                        S.pe(lambda e, b=b, j=j, g=g: e.matmul(bank(b)[:, j * 128:(j + 1) * 128], lhsT=vn[:, g * 128:(g + 1) * 128], rhs=wmT[:, g, :],
                                                               start=True, stop=True), reads=[b_vnbb[i], b_wmT], writes=[PB[b]])
                    tf = R1f[:, 11264 + (g4 % 2) * 512:11264 + (g4 % 2 + 1) * 512]
                    b_tf = [b_tmpf, b_tmpf2][g4 % 2]
                    S.dve(lambda e, b=b, g4=g4, tf=tf: e.tensor_tensor(out=tf.rearrange("p (g t) -> p g t", g=4),
                                                                       in0=bank(b).rearrange("p (g t) -> p g t", g=4),
                                                                       in1=bsb[:, g4 * 4:(g4 + 1) * 4, :], op=ALU.add),
                          reads=[PB[b], b_bsb], writes=[b_tf])
                    yv = Bb[:, g4 * 4:(g4 + 1) * 4, n * 128:(n + 1) * 128]
                    eng = S.pool if g4 % 2 == 0 else S.dve
                    eng(lambda e, yv=yv, tf=tf: e.tensor_tensor(out=yv, in0=yv, in1=tf.rearrange("p (g t) -> p g t", g=4), op=ALU.mult),
                        reads=[b_tf] + Bbuf[g4 * 4:(g4 + 1) * 4], writes=Bbuf[g4 * 4:(g4 + 1) * 4])

            b_tmpf2 = Buf("tmpf2")
            region_switch(R1_all, [b_tmpf2])
            for n in range(-1, NT):
                if n + 1 < NT:
                    g1(n + 1)
                if n >= 0:
                    g2(n)
            linear_B(Bb, B_all, Wd["od_w_out"], KC, evac_to_ms)

        prefetch_w(Wd["ev_w_in"][:, 0:1024][:, 0:512], KC)
        prefetch_w(Wd["ev_w_in"][:, 0:1024][:, 512:1024], KC)
        upd_pass(x_in, None, None, None, "norm", GC_EV)
        even_mixer()
        if stop in ("qkv", "sb"):
            pass
        else:
            load_feature_major(Bb, Bbuf, mergedT, "mergedT")
            linear_B(Bb, B_all, Wd["ev_w_out"], KC, evac_to_ms)
            if stop == "mix0":
                upd_pass(x_in, ms, Wd["ev_norm_post"], out, None, None, final=True)
            else:
                prefetch_w(Wd["ffn_w1"][0][:, 0:2048][:, 0:512], KC)
                prefetch_w(Wd["ffn_w1"][0][:, 0:2048][:, 512:1024], KC)
                upd_pass(x_in, ms, Wd["ev_norm_post"], hs, "norm", GC_F0)
                ffn(0)
                if stop == "ffn0":
                    upd_pass(hs, ms, Wd["ffn_norm_post"][0:1, :], out, None, None, final=True)
                else:
                    prefetch_w(Wd["ple_w_proj"][0][:, 0:512], 2)
                    prefetch_w(Wd["ple_w_gate"][0][:, 0:512], KC)
                    upd_pass(hs, ms, Wd["ffn_norm_post"][0:1, :], hs, "raw", None)
                    ple(0)
                    if stop == "l0":
                        upd_pass(hs, ms, Wd["ple_norm"][0:1, :], out, None, None, final=True)
                    else:
                        prefetch_w(Wd["od_w_in"][:, 0:2048][:, 0:512], KC)
                        prefetch_w(Wd["od_w_in"][:, 0:2048][:, 512:1024], KC)
                        upd_pass(hs, ms, Wd["ple_norm"][0:1, :], hs, "norm", GC_OD)
                        odd_mixer()
                        if stop == "mix1":
                            upd_pass(hs, ms, Wd["od_norm_post"], out, None, None, final=True)
                        else:
                            prefetch_w(Wd["ffn_w1"][1][:, 0:2048][:, 0:512], KC)
                            prefetch_w(Wd["ffn_w1"][1][:, 0:2048][:, 512:1024], KC)
                            upd_pass(hs, ms, Wd["od_norm_post"], hs, "norm", GC_F1)
                            ffn(1)
                            prefetch_w(Wd["ple_w_proj"][1][:, 0:512], 2)
                            prefetch_w(Wd["ple_w_gate"][1][:, 0:512], KC)
                            upd_pass(hs, ms, Wd["ffn_norm_post"][1:2, :], hs, "raw", None)
                            ple(1)
                            upd_pass(hs, ms, Wd["ple_norm"][1:2, :], out, None, None, final=True)
        if stop in ("qkv", "sb"):
            o = S.dma("sp", lambda e: e.dma_start(out=out[0:128, :], in_=hst[:]), reads=[b_hst], writes=[b_out])
            out_ops.append(o)
        assert not pref, list(pref)
        S.emit(final_wait_ops=out_ops)
    return nc


def make_in_maps(inputs):
    f = lambda a: np.ascontiguousarray(np.asarray(a))
    shared = {}
    for name, shape in W_SPECS:
        shared[name] = f(inputs[name]).reshape(shape)
    maps = []
    for b in range(8):
        m = dict(shared)
        m["x"] = f(inputs["x"][b])
        m["p"] = f(inputs["p"][:, b])
        m["positions"] = f(inputs["positions"][b:b + 1]).astype(np.int32)
        maps.append(m)
    return maps


_NC_CACHE = {}


def kernel(**inputs):
    if "nc" not in _NC_CACHE:
        _NC_CACHE["nc"] = build()
    nc = _NC_CACHE["nc"]
    maps = make_in_maps(inputs)
    res = run_bass_kernel_spmd(nc, maps, core_ids=list(range(8)))
    return np.stack([np.asarray(r["out"]) for r in res.results], axis=0).astype(np.float32)
```

```python
import math
from contextlib import ExitStack

import numpy as np
import concourse.bass as bass
import concourse.mybir as mybir
from concourse.bass_utils import run_bass_kernel_spmd

F32 = mybir.dt.float32
BF16 = mybir.dt.bfloat16
I32 = mybir.dt.int32
AF = mybir.ActivationFunctionType
ALU = mybir.AluOpType
AX = mybir.AxisListType

ENGINES = ("pe", "act", "dve", "pool", "sp")
DMA_SEMS_PER_QUEUE = 8


class Buf:
    __slots__ = ("name", "W", "R")

    def __init__(self, name):
        self.name = name
        self.W = []
        self.R = []


class _Op:
    __slots__ = ("eng", "fn", "deps", "is_dma", "idx", "signal", "ticket", "dsem")

    def __init__(self, eng, fn, is_dma, idx):
        self.eng = eng
        self.fn = fn
        self.is_dma = is_dma
        self.idx = idx
        self.deps = set()
        self.signal = False
        self.ticket = None
        self.dsem = None


class Sched:
    def __init__(self, nc):
        self.nc = nc
        self.ops = []

    def op(self, eng, fn, reads=(), writes=(), dma=False):
        o = _Op(eng, fn, dma, len(self.ops))
        deps = set()
        for b in reads:
            deps.update(b.W)
        for b in writes:
            deps.update(b.W)
            deps.update(b.R)
        o.deps = deps
        for b in writes:
            b.W = [o.idx]
            b.R = []
        for b in reads:
            if b in writes:
                continue
            if not dma:
                b.R = [i for i in b.R if not (self.ops[i].eng == eng and not self.ops[i].is_dma)]
            b.R.append(o.idx)
        self.ops.append(o)
        return o

    def pe(self, fn, reads=(), writes=()):
        return self.op("pe", fn, reads, writes)

    def act(self, fn, reads=(), writes=()):
        return self.op("act", fn, reads, writes)

    def dve(self, fn, reads=(), writes=()):
        return self.op("dve", fn, reads, writes)

    def pool(self, fn, reads=(), writes=()):
        return self.op("pool", fn, reads, writes)

    def dma(self, eng, fn, reads=(), writes=()):
        return self.op(eng, fn, reads, writes, dma=True)

    def emit(self, final_wait_ops=()):
        nc = self.nc
        ops = self.ops
        for o in ops:
            for d in o.deps:
                p = ops[d]
                if p.is_dma:
                    continue
                if p.eng != o.eng or o.is_dma or o.eng != "pe":
                    p.signal = True
        with ExitStack() as es:
            esem = {e: es.enter_context(nc.semaphore("s_" + e)) for e in ENGINES}
            dsems = {}
            for e in ("sp", "act", "pool"):
                dsems[e] = [es.enter_context(nc.semaphore(f"d_{e}_{i}")) for i in range(DMA_SEMS_PER_QUEUE)]
            cnt = {e: 0 for e in ENGINES}
            dcnt = {e: 0 for e in dsems}
            dtot = {e: [0] * DMA_SEMS_PER_QUEUE for e in dsems}
            for o in ops:
                if o.is_dma:
                    j = dcnt[o.eng] % DMA_SEMS_PER_QUEUE
                    dcnt[o.eng] += 1
                    o.dsem = (o.eng, j, dtot[o.eng][j])
                    dtot[o.eng][j] += 16
                    o.ticket = (dsems[o.eng][j], dtot[o.eng][j], ("d", o.eng, j))
                elif o.signal:
                    cnt[o.eng] += 1
                    o.ticket = (esem[o.eng], cnt[o.eng], ("e", o.eng))
            self.max_ticket = dict(cnt)
            per_eng = {e: [o for o in ops if o.eng == e] for e in ENGINES}
            final = list(final_wait_ops)
            blk = es.enter_context(nc.Block())

            def run(e, eng):
                seen = {}
                for o in per_eng[e]:
                    waits = {}
                    for d in o.deps:
                        p = ops[d]
                        if p.ticket is None:
                            continue
                        if (not p.is_dma) and (not o.is_dma) and p.eng == e and e == "pe":
                            continue
                        sem, val, key = p.ticket
                        if seen.get(key, 0) >= val:
                            continue
                        if key not in waits or waits[key][1] < val:
                            waits[key] = (sem, val)
                    if o.is_dma:
                        qe, j, prev = o.dsem
                        key = ("d", qe, j)
                        if prev > 0 and seen.get(key, 0) < prev:
                            if key not in waits or waits[key][1] < prev:
                                waits[key] = (dsems[qe][j], prev)
                    for key, (sem, val) in waits.items():
                        eng.wait_ge(sem, val)
                        seen[key] = val
                    ins = o.fn(eng)
                    if o.ticket is not None:
                        sem, val, key = o.ticket
                        ins.then_inc(sem, 16 if o.is_dma else 1)
                if e == "sp":
                    for o in final:
                        sem, val, key = o.ticket
                        eng.wait_ge(sem, val)

            @blk.tensor
            def _(eng):
                run("pe", eng)

            @blk.scalar
            def _(eng):
                run("act", eng)

            @blk.vector
            def _(eng):
                run("dve", eng)

            @blk.gpsimd
            def _(eng):
                run("pool", eng)

            @blk.sync
            def _(eng):
                run("sp", eng)


D = 2048
T = 2048
NT = 16
KC = 16
HD = 128
DFF = 8192
PLE = 256
EPS = 1e-6
ROPE_THETA = 500000.0
ROT = 32
SCALE = HD ** -0.5
MAGIC = 12582912.0
TWO_PI = 2.0 * math.pi
C1 = 6.28125
C2 = 1015.0 / 524288.0
C3 = TWO_PI - C1 - C2

W_SPECS = [
    ("ev_norm_pre", [1, D]), ("ev_w_in", [D, 6144]),
    ("ev_lam_q1", [1, HD]), ("ev_lam_k1", [1, HD]), ("ev_lam_q2", [1, HD]), ("ev_lam_k2", [1, HD]),
    ("ev_subln", [1, 256]), ("ev_w_out", [D, D]), ("ev_norm_post", [1, D]),
    ("od_norm_pre", [1, D]), ("od_w_in", [D, 4096]), ("od_ln_g", [1, D]), ("od_ln_b", [1, D]),
    ("od_w_s", [16, 128, 128]), ("od_b_s", [1, 2048]), ("od_w_out", [D, D]), ("od_norm_post", [1, D]),
    ("ffn_norm_pre", [2, D]), ("ffn_w1", [2, D, DFF]), ("ffn_w2", [2, DFF, D]), ("ffn_norm_post", [2, D]),
    ("ple_w_proj", [2, PLE, D]), ("ple_w_gate", [2, D, D]), ("ple_norm", [2, D]),
]


def build(stop=None, debug=False):
    nc = bass.Bass("TRN2", target_bir_lowering=False)
    x_in = nc.dram_tensor("x", [T, D], F32, kind="ExternalInput").ap()
    p_in = nc.dram_tensor("p", [2, T, PLE], F32, kind="ExternalInput").ap()
    pos_in = nc.dram_tensor("positions", [1, T], I32, kind="ExternalInput").ap()
    Wd = {}
    for name, shape in W_SPECS:
        Wd[name] = nc.dram_tensor(name, shape, F32, kind="ExternalInput").ap()
    out = nc.dram_tensor("out", [T, D], F32, kind="ExternalOutput").ap()
    dk = "ExternalOutput" if debug else "Internal"
    hs = nc.dram_tensor("hs", [T, D], F32, kind=dk).ap()
    ms = nc.dram_tensor("ms", [T, D], F32, kind=dk).ap()
    sbqT = nc.dram_tensor("sbqT", [1024, T], BF16, kind=dk).ap()
    sbkT = nc.dram_tensor("sbkT", [1024, T], BF16, kind=dk).ap()
    sbv = nc.dram_tensor("sbv", [T, 1024], BF16, kind=dk).ap()
    dfqT = nc.dram_tensor("dfqT", [1024, T], BF16, kind=dk).ap()
    dfkT = nc.dram_tensor("dfkT", [1024, T], BF16, kind=dk).ap()
    dfv = nc.dram_tensor("dfv", [T, 1024], BF16, kind=dk).ap()
    mergedT = nc.dram_tensor("mergedT", [D, T], BF16, kind=dk).ap()
    vsc = nc.dram_tensor("vsc", [T, D], F32, kind=dk).ap()

    es = ExitStack()
    with es:
        def sb(name, shape, dt):
            return es.enter_context(nc.sbuf_tensor(name, shape, dt))

        R1 = sb("R1", [128, 32768], BF16)
        R2 = sb("R2", [128, 32768], BF16)
        WS = [sb(f"ws{i}", [128, 16, 512], BF16) for i in range(2)]
        hst = sb("hst", [128, D], F32)
        mst = sb("mst", [128, D], F32)
        gpb = sb("gpb", [128, D], F32)
        xnb = sb("xnb", [128, D], BF16)
        stg = [sb(f"stg{i}", [128, 512], F32) for i in range(4)]
        gcols = sb("gcols", [128, 16 * 5], F32)
        ident = sb("ident", [128, 128], BF16)
        identf = sb("identf", [128, 128], F32)
        sm = sb("sm", [128, 128], F32)
        pmat = sb("pmat", [128, 32], F32)
        g16 = sb("g16", [16, 5 * 128], F32)
        PS = es.enter_context(nc.psum_tensor("PS", [128, 4096], F32))

        S = Sched(nc)
        A = R1[:].rearrange("p (k t) -> p k t", k=16)
        Bb = R2[:].rearrange("p (k t) -> p k t", k=16)
        R1f = R1[:].bitcast(F32)
        R2f = R2[:].bitcast(F32)

        def bank(b):
            return PS[:, b * 512:(b + 1) * 512]

        def bank_bf(b):
            return PS[:, b * 512:(b + 1) * 512].bitcast(BF16)

        PB = [Buf(f"pb{b}") for b in range(8)]
        Abuf = [Buf(f"A{t}") for t in range(NT)]
        Bbuf = [Buf(f"B{k}") for k in range(KC)]
        WSb = [Buf("ws0"), Buf("ws1")]
        b_hst, b_mst, b_gpb, b_xnb = Buf("hst"), Buf("mst"), Buf("gpb"), Buf("xnb")
        b_stg = [Buf(f"stg{i}") for i in range(4)]
        b_gcols, b_ident, b_sm, b_pmat, b_g16 = Buf("gcols"), Buf("ident"), Buf("sm"), Buf("pmat"), Buf("g16")
        b_hs = [Buf(f"hs{t}") for t in range(NT)]
        b_ms = [Buf(f"ms{t}") for t in range(NT)]
        b_out = Buf("out")
        b_R1 = Buf("R1")
        b_R2 = Buf("R2")
        b_scr = {n: Buf(n) for n in ["sbqT", "sbkT", "sbv", "dfqT", "dfkT", "dfv", "mergedT", "vsc"]}
        out_ops = []
        st = {"stg": 0, "ws": 0, "pb": 0, "sm": 20}
        R1_all = set(Abuf)
        R2_all = set(Bbuf)

        def region_switch(region_all, new_bufs):
            Wu, Ru = set(), set()
            for b in region_all:
                Wu.update(b.W)
                Ru.update(b.R)
            for b in new_bufs:
                b.W = sorted(set(b.W) | Wu)
                b.R = sorted(set(b.R) | Ru)
            region_all.update(new_bufs)

        SM_SS, SM_VE, SM_RSTD, SM_NEGH, SM_EPS, SM_SS2, SM_RSTD2 = 0, 1, 2, 3, 4, 5, 6
        SM_MX, SM_L1, SM_L2, SM_LB, SM_R1, SM_R2, SM_NLAM, SM_TMP, SM_TMP2 = 8, 9, 10, 11, 12, 13, 14, 15, 16
        SM_NTOT = 17

        def smc(c, n=128):
            return sm[0:n, c:c + 1]

        S.pool(lambda e: e.memset(ident[:], 1.0), writes=[b_ident])
        S.pool(lambda e: e.affine_select(out=ident[:], in_=ident[:], pattern=[[-1, 128]], compare_op=ALU.is_equal,
                                         fill=0.0, base=0, channel_multiplier=1), reads=[b_ident], writes=[b_ident])
        S.pool(lambda e: e.memset(identf[:], 1.0), writes=[b_ident])
        S.pool(lambda e: e.affine_select(out=identf[:], in_=identf[:], pattern=[[-1, 128]], compare_op=ALU.is_equal,
                                         fill=0.0, base=0, channel_multiplier=1), reads=[b_ident], writes=[b_ident])
        S.pool(lambda e: e.memset(sm[:], 0.0), writes=[b_sm])
        S.pool(lambda e: e.memset(smc(SM_NEGH), -0.5), writes=[b_sm])
        S.pool(lambda e: e.memset(smc(SM_EPS), EPS), writes=[b_sm])
        gnames = [("ev_norm_pre", 0), ("ffn_norm_pre", 0), ("od_norm_pre", 0), ("ffn_norm_pre", 1)]
        for j, (nm, li) in enumerate(gnames):
            src = Wd[nm][li:li + 1, :].rearrange("o (k p) -> (o k) p", p=128)
            S.dma("sp", lambda e, j=j, src=src: e.dma_start(out=g16[:, j * 128:(j + 1) * 128], in_=src), writes=[b_g16])
        for j in range(len(gnames)):
            S.pe(lambda e, j=j: e.matmul(PS[:, 3584 + j * 16:3584 + (j + 1) * 16], lhsT=g16[:, j * 128:(j + 1) * 128],
                                          rhs=identf[0:16, 0:16], start=True, stop=True),
                 reads=[b_g16, b_ident], writes=[PB[7]])
        S.dve(lambda e: e.tensor_copy(out=gcols[:, 0:64], in_=PS[:, 3584:3584 + 64]), reads=[PB[7]], writes=[b_gcols])
        GC_EV, GC_F0, GC_OD, GC_F1 = 0, 1, 2, 3

        def rstd_from_ss(c_ss, c_out, n_feat, np_=128):
            S.dve(lambda e: e.tensor_scalar(out=smc(SM_VE, np_), in0=smc(c_ss, np_), scalar1=1.0 / n_feat, scalar2=EPS,
                                            op0=ALU.mult, op1=ALU.add), reads=[b_sm], writes=[b_sm])
            S.pool(lambda e: e.tensor_tensor(out=smc(c_out, np_), in0=smc(SM_VE, np_), in1=smc(SM_NEGH, np_), op=ALU.pow),
                   reads=[b_sm], writes=[b_sm])

        b_negh = Buf("negh")
        negh = sm[:, 120:121]
        S.pool(lambda e: e.memset(negh, -0.5), writes=[b_negh])
        st["sm"] = 20

        def smalloc():
            c = st["sm"]
            st["sm"] += 1
            assert c < 120
            return sm[:, c:c + 1], Buf(f"sm{c}")

        def rstd_op(ss, b_ss, ve, b_ve, rs, b_rs, n_feat):
            S.dve(lambda e: e.tensor_scalar(out=ve, in0=ss, scalar1=1.0 / n_feat, scalar2=EPS, op0=ALU.mult, op1=ALU.add),
                  reads=[b_ss], writes=[b_ve])
            S.pool(lambda e: e.tensor_tensor(out=rs, in0=ve, in1=negh, op=ALU.pow), reads=[b_ve, b_negh], writes=[b_rs])

        upd_sm = [[smalloc() for _ in range(6)] for _ in range(2)]

        def upd_pass(h_src, m_src, g_post, h_dst, a_mode, gcol_idx, final=False):
            NB3 = 3
            hstb = [R2f[:, i * 2048:(i + 1) * 2048] for i in range(NB3)]
            mstb = [R2f[:, 6144 + i * 2048:6144 + (i + 1) * 2048] for i in range(NB3)]
            xnbb = [R2[:, 24576 + i * 2048:24576 + (i + 1) * 2048] for i in range(2)]
            junkb = R2[:, 28672:30720]
            bh = [Buf(f"uh{i}") for i in range(NB3)]
            bm = [Buf(f"um{i}") for i in range(NB3)]
            bx = [Buf("ux0"), Buf("ux1")]
            bj = Buf("ujunk")
            region_switch(R2_all, bh + bm + bx + [bj])
            if m_src is not None:
                S.dma("sp", lambda e: e.dma_start(out=gpb[:], in_=g_post.partition_broadcast(128)), writes=[b_gpb])
            if a_mode is not None:
                region_switch(R1_all, Abuf)
            def s1(tt):
                i = tt % NB3
                hs_, ms_ = hstb[i], mstb[i]
                (ssm, b_ssm), (vem, b_vem), (rsm, b_rsm) = upd_sm[tt % 2][0:3]
                rows = slice(tt * 128, (tt + 1) * 128)
                rd = [b_hs[tt]] if h_src is hs else []
                S.dma("sp", lambda e: e.dma_start(out=hs_, in_=h_src[rows, :]), reads=rd, writes=[bh[i]])
                if m_src is not None:
                    S.dma("sp", lambda e: e.dma_start(out=ms_, in_=m_src[rows, :]), reads=[b_ms[tt]], writes=[bm[i]])
                    S.act(lambda e: e.activation(out=junkb, in_=ms_, func=AF.Square, accum_out=ssm),
                          reads=[bm[i]], writes=[bj, b_ssm])
                    rstd_op(ssm, b_ssm, vem, b_vem, rsm, b_rsm, D)
                    S.dve(lambda e: e.scalar_tensor_tensor(out=ms_, in0=ms_, scalar=rsm, in1=gpb[:], op0=ALU.mult, op1=ALU.mult),
                          reads=[bm[i], b_rsm, b_gpb], writes=[bm[i]])
                    S.dve(lambda e: e.tensor_tensor(out=hs_, in0=hs_, in1=ms_, op=ALU.add),
                          reads=[bh[i], bm[i]], writes=[bh[i]])
                if h_dst is not None:
                    o = S.dma("pool", lambda e: e.dma_start(out=h_dst[rows, :], in_=hs_), reads=[bh[i]],
                              writes=[b_hs[tt]] if h_dst is hs else [b_out])
                    if final:
                        out_ops.append(o)

            def s2(tt):
                if a_mode is None:
                    return
                i = tt % NB3
                ix = tt % 2
                hs_, xn_ = hstb[i], xnbb[ix]
                (ssh, b_ssh), (veh, b_veh), (rsh, b_rsh) = upd_sm[ix][3:6]
                if a_mode == "norm":
                    S.act(lambda e: e.activation(out=junkb, in_=hs_, func=AF.Square, accum_out=ssh),
                          reads=[bh[i]], writes=[bj, b_ssh])
                    rstd_op(ssh, b_ssh, veh, b_veh, rsh, b_rsh, D)
                    S.act(lambda e: e.activation(out=xn_, in_=hs_, func=AF.Copy, scale=rsh),
                          reads=[bh[i], b_rsh], writes=[bx[ix]])
                else:
                    S.act(lambda e: e.activation(out=xn_, in_=hs_, func=AF.Copy), reads=[bh[i]], writes=[bx[ix]])
                for half in range(2):
                    bk = 4 + half
                    for j in range(8):
                        k = half * 8 + j
                        S.pe(lambda e, bk=bk, j=j, k=k: e.transpose(out=bank_bf(bk)[:, j * 128:(j + 1) * 128],
                                                                    in_=xn_[:, k * 128:(k + 1) * 128], identity=ident[:]),
                             reads=[bx[ix], b_ident], writes=[PB[bk]])
                    dst = A[:, half * 8:(half + 1) * 8, tt * 128:(tt + 1) * 128]
                    src = bank_bf(bk).rearrange("p (k t) -> p k t", k=8)
                    if a_mode == "norm":
                        gc = gcols[:, gcol_idx * 16 + half * 8: gcol_idx * 16 + half * 8 + 8].unsqueeze(2).to_broadcast([128, 8, 128])
                        S.dve(lambda e, dst=dst, src=src, gc=gc: e.tensor_tensor(out=dst, in0=src, in1=gc, op=ALU.mult),
                              reads=[PB[bk], b_gcols], writes=[Abuf[tt]])
                    else:
                        S.dve(lambda e, dst=dst, src=src: e.tensor_copy(out=dst, in_=src), reads=[PB[bk]], writes=[Abuf[tt]])

            for n in range(-1, NT):
                if n + 1 < NT:
                    s1(n + 1)
                if n >= 0:
                    s2(n)

        pref = {}

        def prefetch_w(wap, kc):
            k = repr(wap)
            if k in pref:
                return
            pref[k] = load_w(wap, kc, _nopref=True)

        def load_w(wap, kc, eng="pool", _nopref=False):
            if not _nopref:
                k = repr(wap)
                if k in pref:
                    return pref.pop(k)
            i = st["ws"] % 2
            st["ws"] += 1
            ncols = wap.shape[1]
            src = wap.rearrange("(k p) n -> p k n", p=128)
            S.dma(eng, lambda e: e.dma_start(out=WS[i][:, 0:kc, 0:ncols], in_=src), writes=[WSb[i]])
            return i

        def next_pb(n=4):
            b = st["pb"] % n
            st["pb"] += 1
            return b

        def next_stg():
            i = st["stg"] % 4
            st["stg"] += 1
            return i

        def linear_A(X, Xbufs_for_tb, wap, kc, evac):
            N = wap.shape[1]
            for cb in range(N // 512):
                wi = load_w(wap[:, cb * 512:(cb + 1) * 512], kc)
                for nch in range(4):
                    for tb in range(4):
                        b = next_pb()
                        for k in range(kc):
                            S.pe(lambda e, b=b, wi=wi, k=k, nch=nch, tb=tb: e.matmul(
                                bank(b), lhsT=WS[wi][:, k, nch * 128:(nch + 1) * 128], rhs=X[:, k, tb * 512:(tb + 1) * 512],
                                start=(k == 0), stop=(k == kc - 1)),
                                 reads=[WSb[wi]] + Xbufs_for_tb(tb), writes=[PB[b]])
                        evac(bank(b), PB[b], cb * 4 + nch, tb)

        def linear_B(X, Xbufs_for_tt, wap, kc, evac, extra=None):
            N = wap.shape[1]
            for cb in range(N // 512):
                wi = load_w(wap[:, cb * 512:(cb + 1) * 512], kc)
                ex = extra(cb) if extra is not None else None
                for tt in range(NT):
                    b = next_pb()
                    for k in range(kc):
                        S.pe(lambda e, b=b, wi=wi, k=k, tt=tt: e.matmul(
                            bank(b), lhsT=X[:, k, tt * 128:(tt + 1) * 128], rhs=WS[wi][:, k, 0:512],
                            start=(k == 0), stop=(k == kc - 1)),
                             reads=[WSb[wi]] + Xbufs_for_tt(tt), writes=[PB[b]])
                    evac(bank(b), PB[b], tt, cb, ex)

        def A_tb(tb):
            return Abuf[tb * 4:(tb + 1) * 4]

        def A_tt(tt):
            return [Abuf[tt]]

        def B_all(_):
            return Bbuf

        def evac_to_ms(ps, pbuf, tt, cb, ex=None):
            i = next_stg()
            S.act(lambda e: e.activation(out=stg[i][:], in_=ps, func=AF.Copy), reads=[pbuf], writes=[b_stg[i]])
            S.dma("sp", lambda e: e.dma_start(out=ms[tt * 128:(tt + 1) * 128, cb * 512:(cb + 1) * 512], in_=stg[i][:]),
                  reads=[b_stg[i]], writes=[b_ms[tt]])

        def ffn(li):
            w1 = Wd["ffn_w1"][li]
            w2 = Wd["ffn_w2"][li]
            region_switch(R2_all, Bbuf)
            for fb in range(4):
                def evac1(ps, pbuf, nchg, tb):
                    i = next_stg()
                    S.act(lambda e: e.activation(out=stg[i][:], in_=ps, func=AF.Relu), reads=[pbuf], writes=[b_stg[i]])
                    S.dve(lambda e: e.scalar_tensor_tensor(out=Bb[:, nchg, tb * 512:(tb + 1) * 512], in0=ps, scalar=0.0,
                                                           in1=stg[i][:], op0=ALU.max, op1=ALU.mult),
                          reads=[pbuf, b_stg[i]], writes=[Bbuf[nchg]])
                linear_A(A, A_tb, w1[:, fb * 2048:(fb + 1) * 2048], KC, evac1)

                def evac2(ps, pbuf, tt, cb, ex=None, fb=fb):
                    i = next_stg()
                    dst = ms[tt * 128:(tt + 1) * 128, cb * 512:(cb + 1) * 512]
                    if fb == 0:
                        S.act(lambda e: e.activation(out=stg[i][:], in_=ps, func=AF.Copy), reads=[pbuf], writes=[b_stg[i]])
                    else:
                        S.dma("sp", lambda e: e.dma_start(out=stg[i][:], in_=dst), reads=[b_ms[tt]], writes=[b_stg[i]])
                        S.dve(lambda e: e.tensor_tensor(out=stg[i][:], in0=ps, in1=stg[i][:], op=ALU.add),
                              reads=[pbuf, b_stg[i]], writes=[b_stg[i]])
                    S.dma("sp", lambda e: e.dma_start(out=dst, in_=stg[i][:]), reads=[b_stg[i]], writes=[b_ms[tt]])
                linear_B(Bb, B_all, w2[fb * 2048:(fb + 1) * 2048, :], KC, evac2)

        def ple(li):
            pT = R2[:, 0:4096].rearrange("p (k t) -> p k t", k=2)
            b_pT = Buf("pT")
            region_switch(R2_all, [b_pT])
            for tt in range(NT):
                i = next_stg()
                S.dma("sp", lambda e, i=i, tt=tt: e.dma_start(out=stg[i][:, 0:256], in_=p_in[li, tt * 128:(tt + 1) * 128, :]),
                      writes=[b_stg[i]])
                S.act(lambda e, i=i: e.activation(out=xnb[:, 0:256], in_=stg[i][:, 0:256], func=AF.Copy),
                      reads=[b_stg[i]], writes=[b_xnb])
                for k in range(2):
                    S.pe(lambda e, k=k: e.transpose(out=bank_bf(4)[:, k * 128:(k + 1) * 128], in_=xnb[:, k * 128:(k + 1) * 128],
                                                    identity=ident[:]), reads=[b_xnb, b_ident], writes=[PB[4]])
                S.dve(lambda e, tt=tt: e.tensor_copy(out=pT[:, :, tt * 128:(tt + 1) * 128],
                                                     in_=bank_bf(4)[:, 0:256].rearrange("p (k t) -> p k t", k=2)),
                      reads=[PB[4]], writes=[b_pT])
            wg = Wd["ple_w_gate"][li]
            wp = Wd["ple_w_proj"][li]

            def extra(cb):
                return load_w(wp[:, cb * 512:(cb + 1) * 512], 2)

            def evac(ps, pbuf, tt, cb, wpi):
                S.pe(lambda e: e.matmul(bank(6), lhsT=pT[:, 0, tt * 128:(tt + 1) * 128], rhs=WS[wpi][:, 0, 0:512], start=True, stop=False),
                     reads=[b_pT, WSb[wpi]], writes=[PB[6]])
                S.pe(lambda e: e.matmul(bank(6), lhsT=pT[:, 1, tt * 128:(tt + 1) * 128], rhs=WS[wpi][:, 1, 0:512], start=False, stop=True),
                     reads=[b_pT, WSb[wpi]], writes=[PB[6]])
                i = next_stg()
                S.act(lambda e: e.activation(out=stg[i][:], in_=ps, func=AF.Sigmoid), reads=[pbuf], writes=[b_stg[i]])
                S.dve(lambda e: e.tensor_tensor(out=stg[i][:], in0=bank(6), in1=stg[i][:], op=ALU.mult),
                      reads=[PB[6], b_stg[i]], writes=[b_stg[i]])
                S.dma("sp", lambda e: e.dma_start(out=ms[tt * 128:(tt + 1) * 128, cb * 512:(cb + 1) * 512], in_=stg[i][:]),
                      reads=[b_stg[i]], writes=[b_ms[tt]])
            N = D
            for cb in range(N // 512):
                wpi = extra(cb)
                wi = load_w(wg[:, cb * 512:(cb + 1) * 512], KC)
                for tt in range(NT):
                    b = next_pb()
                    for k in range(KC):
                        S.pe(lambda e, b=b, wi=wi, k=k, tt=tt: e.matmul(
                            bank(b), lhsT=A[:, k, tt * 128:(tt + 1) * 128], rhs=WS[wi][:, k, 0:512],
                            start=(k == 0), stop=(k == KC - 1)), reads=[WSb[wi], Abuf[tt]], writes=[PB[b]])
                    evac(bank(b), PB[b], tt, cb, wpi)

        def even_mixer():
            cosT = R2f[0:32, 0:2048]
            sinT = R2f[0:32, 2048:4096]
            ang = R2f[0:32, 4096:6144]
            tmp = R2f[0:32, 6144:8192]
            tmp2 = R2f[0:32, 8192:10240]
            posi = R2f[0:32, 10240:12288].bitcast(I32)
            frow = R2f[0:1, 15360:15360 + 32]
            one2 = R2f[0:1, 15392:15394]
            b_rope = Buf("rope")
            region_switch(R2_all, [b_rope])
            S.dma("sp", lambda e: e.dma_start(out=posi, in_=pos_in.partition_broadcast(32)), writes=[b_rope])
            S.dve(lambda e: e.tensor_copy(out=ang, in_=posi), reads=[b_rope], writes=[b_rope])
            inv = (np.float32(ROPE_THETA) ** (-(np.arange(0, ROT, 2, dtype=np.float32)) / np.float32(ROT))).astype(np.float32)
            for i in range(16):
                for hh in range(2):
                    c = hh * 16 + i
                    S.pool(lambda e, c=c, v=float(inv[i]): e.memset(frow[:, c:c + 1], v), writes=[b_rope])
            S.pool(lambda e: e.memset(one2, 1.0), writes=[b_rope])
            S.pe(lambda e: e.matmul(PS[0:32, 3584:3586], lhsT=frow, rhs=one2, start=True, stop=True), reads=[b_rope], writes=[PB[7]])
            S.dve(lambda e: e.tensor_copy(out=sm[0:32, SM_TMP:SM_TMP + 1], in_=PS[0:32, 3584:3585]), reads=[PB[7]], writes=[b_sm])
            S.dve(lambda e: e.tensor_scalar(out=ang, in0=ang, scalar1=sm[0:32, SM_TMP:SM_TMP + 1], scalar2=None, op0=ALU.mult),
                  reads=[b_rope, b_sm], writes=[b_rope])
            S.dve(lambda e: e.tensor_scalar(out=tmp, in0=ang, scalar1=1.0 / TWO_PI, scalar2=MAGIC, op0=ALU.mult, op1=ALU.add),
                  reads=[b_rope], writes=[b_rope])
            S.dve(lambda e: e.tensor_scalar(out=tmp, in0=tmp, scalar1=-MAGIC, scalar2=None, op0=ALU.add), reads=[b_rope], writes=[b_rope])
            for cst in (C1, C2, C3):
                S.dve(lambda e, cst=cst: e.scalar_tensor_tensor(out=ang, in0=tmp, scalar=-cst, in1=ang, op0=ALU.mult, op1=ALU.add),
                      reads=[b_rope], writes=[b_rope])
            S.dve(lambda e: e.tensor_scalar(out=ang, in0=ang, scalar1=3.1415925, scalar2=-3.1415925, op0=ALU.min, op1=ALU.max),
                  reads=[b_rope], writes=[b_rope])
            S.act(lambda e: e.activation(out=sinT, in_=ang, func=AF.Sin), reads=[b_rope], writes=[b_rope])
            S.dve(lambda e: e.tensor_scalar(out=tmp2, in0=ang, scalar1=-1.0, scalar2=None, op0=ALU.mult), reads=[b_rope], writes=[b_rope])
            S.dve(lambda e: e.tensor_tensor(out=tmp2, in0=tmp2, in1=ang, op=ALU.max), reads=[b_rope], writes=[b_rope])
            S.dve(lambda e: e.tensor_scalar(out=tmp2, in0=tmp2, scalar1=-1.0, scalar2=math.pi / 2, op0=ALU.mult, op1=ALU.add),
                  reads=[b_rope], writes=[b_rope])
            S.act(lambda e: e.activation(out=cosT, in_=tmp2, func=AF.Sin), reads=[b_rope], writes=[b_rope])
            S.pool(lambda e: e.memset(pmat[:], 0.0), writes=[b_pmat])
            S.pool(lambda e: e.affine_select(out=pmat[:, 0:16], in_=pmat[:, 0:16], pattern=[[-1, 16]], compare_op=ALU.not_equal,
                                             fill=-1.0, base=-16, channel_multiplier=1), reads=[b_pmat], writes=[b_pmat])
            S.pool(lambda e: e.affine_select(out=pmat[:, 16:32], in_=pmat[:, 16:32], pattern=[[-1, 16]], compare_op=ALU.not_equal,
                                             fill=1.0, base=0, channel_multiplier=1), reads=[b_pmat], writes=[b_pmat])
            w_in = Wd["ev_w_in"]
            ostg = [R2[:, 24576 + i * 2048: 24576 + (i + 1) * 2048] for i in range(2)]
            b_ostg = [Buf("ostg0"), Buf("ostg1")]
            region_switch(R2_all, b_ostg)
            cnt = {"o": 0}

            def mk_evacA(dst_scr, dst_name, scale, rope):
                def evac(ps, pbuf, nchg, tb):
                    oi = (cnt["o"] // 4) % 2
                    cnt["o"] += 1
                    od = ostg[oi][:, tb * 512:(tb + 1) * 512]
                    if not rope:
                        S.act(lambda e: e.activation(out=od, in_=ps, func=AF.Copy, scale=scale), reads=[pbuf], writes=[b_ostg[oi]])
                    else:
                        i = next_stg()
                        S.act(lambda e: e.activation(out=stg[i][:], in_=ps, func=AF.Copy, scale=scale), reads=[pbuf], writes=[b_stg[i]])
                        S.pe(lambda e: e.matmul(PS[0:32, 3072:3584], lhsT=pmat[:], rhs=stg[i][:], start=True, stop=True),
                             reads=[b_pmat, b_stg[i]], writes=[PB[6]])
                        S.act(lambda e: e.activation(out=od, in_=stg[i][:], func=AF.Copy),
                              reads=[b_stg[i]], writes=[b_ostg[oi]])
                        S.dve(lambda e: e.tensor_tensor(out=tmp[:, 0:512], in0=PS[0:32, 3072:3584], in1=sinT[:, tb * 512:(tb + 1) * 512],
                                                        op=ALU.mult), reads=[PB[6], b_rope], writes=[b_rope])
                        S.dve(lambda e: e.tensor_tensor(out=tmp2[:, 0:512], in0=stg[i][0:32, :], in1=cosT[:, tb * 512:(tb + 1) * 512],
                                                        op=ALU.mult), reads=[b_stg[i], b_rope], writes=[b_rope])
                        S.dve(lambda e: e.tensor_tensor(out=od[0:32, :], in0=tmp[:, 0:512], in1=tmp2[:, 0:512], op=ALU.add),
                              reads=[b_rope], writes=[b_ostg[oi]])
                    if tb == 3:
                        S.dma("sp", lambda e: e.dma_start(out=dst_scr[nchg * 128:(nchg + 1) * 128, :], in_=ostg[oi]),
                              reads=[b_ostg[oi]], writes=[b_scr[dst_name]])
                return evac

            def mk_evacB(dst_scr, dst_name):
                def evac(ps, pbuf, tt, cb, ex=None):
                    i = next_stg()
                    sv = stg[i][:].bitcast(BF16)[:, 0:512]
                    S.act(lambda e: e.activation(out=sv, in_=ps, func=AF.Copy), reads=[pbuf], writes=[b_stg[i]])
                    S.dma("sp", lambda e: e.dma_start(out=dst_scr[tt * 128:(tt + 1) * 128, cb * 512:(cb + 1) * 512], in_=sv),
                          reads=[b_stg[i]], writes=[b_scr[dst_name]])
                return evac

            linear_A(A, A_tb, w_in[:, 0:1024], KC, mk_evacA(sbqT, "sbqT", SCALE, False))
            linear_A(A, A_tb, w_in[:, 1024:2048], KC, mk_evacA(sbkT, "sbkT", 1.0, False))
            linear_B(A, A_tt, w_in[:, 2048:3072], KC, mk_evacB(sbv, "sbv"))
            linear_A(A, A_tb, w_in[:, 3072:4096], KC, mk_evacA(dfqT, "dfqT", SCALE, True))
            linear_A(A, A_tb, w_in[:, 4096:5120], KC, mk_evacA(dfkT, "dfkT", 1.0, True))
            linear_B(A, A_tt, w_in[:, 5120:6144], KC, mk_evacB(dfv, "dfv"))
            if stop == "qkv":
                return
            prefetch_w(Wd["ev_w_out"][:, 0:512], KC)
            prefetch_w(Wd["ev_w_out"][:, 512:1024], KC)
            attention()

        def attention():
            ef = [R1f[:, i * 6144:i * 6144 + 2048] for i in range(2)]
            spf = [R1f[:, i * 6144 + 2048:i * 6144 + 4096] for i in range(2)]
            csf = [R1f[:, i * 6144 + 4096:i * 6144 + 6144] for i in range(2)]
            wb = [R1[:, 24576 + i * 2048:24576 + (i + 1) * 2048] for i in range(2)]
            wTb = [R1[:, 28672 + i * 2048:28672 + (i + 1) * 2048].rearrange("p (c t) -> p c t", c=16) for i in range(2)]
            qT = [R2[:, i * 2048:(i + 1) * 2048] for i in range(4)]
            vv = R2[:, 8192:12288]
            oT = R2[:, 12288:16384].rearrange("p (j t) -> p j t", j=2)
            onbb = [R2[:, 16384 + i * 256:16384 + (i + 1) * 256] for i in range(2)]
            gsub = R2f[:, 8448:8704]
            lamt = R2f[:, 8704:9216]
            junk = R2f[:, 9216:11264]
            b_e = [Buf("e0"), Buf("e1")]
            b_sp = [Buf("sp0"), Buf("sp1")]
            b_cs = [Buf("cs0"), Buf("cs1")]
            b_w = [Buf("w0"), Buf("w1")]
            b_wT = [Buf("wT0"), Buf("wT1")]
            b_onb = [Buf("onb0"), Buf("onb1")]
            b_v, b_oT, b_gsub, b_junk, b_lam = (Buf(n) for n in ["v", "oT", "gsub", "junk", "lam"])
            b_q = [Buf(f"q{i}") for i in range(4)]
            region_switch(R1_all, b_e + b_sp + b_cs + b_w + b_wT)
            region_switch(R2_all, b_q + b_onb + [b_v, b_oT, b_gsub, b_junk])
            PB6s = [Buf(f"pb6_{i}") for i in range(4)]
            PB7s = [Buf(f"pb7_{i}") for i in range(2)]
            for bb in PB6s:
                bb.W = list(PB[6].W); bb.R = list(PB[6].R)
            for bb in PB7s:
                bb.W = list(PB[7].W); bb.R = list(PB[7].R)
            lambda_init = 0.8 - 0.6 * math.exp(-0.3 * 0)
            (t1, b_t1), (t2, b_t2), (nlam, b_nlam) = smalloc(), smalloc(), smalloc()
            for i, nm in enumerate(["ev_lam_q1", "ev_lam_k1", "ev_lam_q2", "ev_lam_k2"]):
                S.dma("sp", lambda e, i=i, nm=nm: e.dma_start(out=lamt[:, i * 128:(i + 1) * 128], in_=Wd[nm].partition_broadcast(128)),
                      writes=[b_gsub])
            S.dma("sp", lambda e, i=i: e.dma_start(out=gsub, in_=Wd["ev_subln"].partition_broadcast(128)), writes=[b_gsub])
            S.dve(lambda e, i=i: e.tensor_tensor(out=junk[:, 0:128], in0=lamt[:, 0:128], in1=lamt[:, 128:256], op=ALU.mult),
                  reads=[b_gsub], writes=[b_junk])
            S.dve(lambda e, i=i: e.tensor_reduce(out=t1, in_=junk[:, 0:128], axis=AX.X, op=ALU.add), reads=[b_junk], writes=[b_t1])
            S.dve(lambda e, i=i: e.tensor_tensor(out=junk[:, 0:128], in0=lamt[:, 256:384], in1=lamt[:, 384:512], op=ALU.mult),
                  reads=[b_gsub], writes=[b_junk])
            S.dve(lambda e, i=i: e.tensor_reduce(out=t2, in_=junk[:, 0:128], axis=AX.X, op=ALU.add), reads=[b_junk], writes=[b_t2])
            S.act(lambda e, i=i: e.activation(out=t1, in_=t1, func=AF.Exp), reads=[b_t1], writes=[b_t1])
            S.act(lambda e, i=i: e.activation(out=t2, in_=t2, func=AF.Exp), reads=[b_t2], writes=[b_t2])
            S.dve(lambda e, i=i: e.tensor_tensor(out=nlam, in0=t2, in1=t1, op=ALU.subtract), reads=[b_t1, b_t2], writes=[b_nlam])
            S.dve(lambda e, i=i: e.tensor_scalar(out=nlam, in0=nlam, scalar1=-lambda_init, scalar2=None, op0=ALU.add),
                  reads=[b_nlam], writes=[b_nlam])
            S.dve(lambda e, i=i: e.tensor_scalar(out=gsub, in0=gsub, scalar1=(1.0 - lambda_init), scalar2=None, op0=ALU.mult),
                  reads=[b_gsub], writes=[b_gsub])

            zps = PS[:, 0:2048]
            ZB = PB[0:4]
            it = {"n": 0}

            def transposes_w(nblk, i):
                for g0 in range(0, nblk, 8):
                    bk = 4 + (g0 // 8) % 2
                    n = min(8, nblk - g0)
                    for j in range(n):
                        c = g0 + j
                        S.pe(lambda e, i=i, bk=bk, j=j, c=c: e.transpose(out=bank_bf(bk)[:, j * 128:(j + 1) * 128],
                                                                    in_=wb[i][:, c * 128:(c + 1) * 128], identity=ident[:]),
                             reads=[b_w[i], b_ident], writes=[PB[bk]])
                    src = bank_bf(bk)[:, 0:n * 128].rearrange("p (c t) -> p c t", c=n)
                    if (g0 // 8) % 2 == 0:
                        S.act(lambda e, i=i, src=src, g0=g0, n=n: e.activation(out=wTb[i][:, g0:g0 + n, :], in_=src, func=AF.Copy),
                              reads=[PB[bk]], writes=[b_wT[i]])
                    else:
                        S.dve(lambda e, i=i, src=src, g0=g0, n=n: e.tensor_copy(out=wTb[i][:, g0:g0 + n, :], in_=src),
                              reads=[PB[bk]], writes=[b_wT[i]])

            def run_pipeline(iters):
                N = len(iters)
                for n in range(-2, N):
                    if 0 <= n + 2 < N:
                        iters[n + 2][0]()
                    if 0 <= n + 1 < N:
                        iters[n + 1][1]()
                    if 0 <= n < N:
                        iters[n][2]()

            vvb = [R2[:, 8192:12288], R2[:, 22528:26624]]
            b_vb = [b_v, Buf("v1")]
            qk2 = [R2[:, 26624:28672], R2[:, 28672:30720]]
            b_qk2 = [Buf("q1b"), Buf("k1b")]
            region_switch(R2_all, [b_vb[1]] + b_qk2)
            sb_sm = [smalloc() for _ in range(2)]
            iters = []
            for hh in range(8):
                hp = hh % 2
                qh = qT[0] if hp == 0 else qk2[0]
                kh = qT[1] if hp == 0 else qk2[1]
                b_qh = b_q[0] if hp == 0 else b_qk2[0]
                b_kh = b_q[1] if hp == 0 else b_qk2[1]
                v3 = vvb[hp][:, 0:2048].rearrange("p (c d) -> p c d", c=16)
                b_vh = b_vb[hp]
                for qt in range(NT):
                    i = it["n"] % 2
                    it["n"] += 1

                    def a1(hh=hh, qt=qt, i=i, qh=qh, kh=kh, b_qh=b_qh, b_kh=b_kh, v3=v3, b_vh=b_vh):
                        if qt == 0:
                            S.dma("sp", lambda e: e.dma_start(out=qh, in_=sbqT[hh * 128:(hh + 1) * 128, :]), reads=[b_scr["sbqT"]], writes=[b_qh])
                            S.dma("sp", lambda e: e.dma_start(out=kh, in_=sbkT[hh * 128:(hh + 1) * 128, :]), reads=[b_scr["sbkT"]], writes=[b_kh])
                            S.dma("sp", lambda e: e.dma_start(out=v3, in_=sbv[:, hh * 128:(hh + 1) * 128].rearrange("(c p) d -> p c d", p=128)),
                                  reads=[b_scr["sbv"]], writes=[b_vh])
                        e_f, sp_f = ef[i], spf[i]
                        Wk = (qt + 1) * 128
                        tq = slice(qt * 128, (qt + 1) * 128)
                        nb = (Wk + 511) // 512
                        for kb in range(nb):
                            k0, k1 = kb * 512, min((kb + 1) * 512, Wk)
                            S.pe(lambda e, k0=k0, k1=k1: e.matmul(zps[:, k0:k1], lhsT=qh[:, tq], rhs=kh[:, k0:k1], start=True, stop=True),
                                 reads=[b_qh, b_kh], writes=[ZB[kb]])
                        zb = ZB[0:nb]
                        S.act(lambda e: e.activation(out=e_f[:, 0:Wk], in_=zps[:, 0:Wk], func=AF.Exp), reads=zb, writes=[b_e[i]])
                        S.act(lambda e: e.activation(out=sp_f[:, 0:Wk], in_=e_f[:, 0:Wk], func=AF.Ln, bias=1.0),
                              reads=[b_e[i]], writes=[b_sp[i]])
                        S.pool(lambda e: e.affine_select(out=sp_f[:, Wk - 128:Wk], in_=sp_f[:, Wk - 128:Wk], pattern=[[-1, 128]],
                                                         compare_op=ALU.is_gt, fill=0.0, base=0, channel_multiplier=1),
                               reads=[b_sp[i]], writes=[b_sp[i]])

                    def a2(hh=hh, qt=qt, i=i):
                        ntot, b_nt = sb_sm[i]
                        e_f, sp_f, cs_f = ef[i], spf[i], csf[i]
                        Wk = (qt + 1) * 128
                        S.dve(lambda e: e.tensor_tensor_scan(out=cs_f[:, 0:Wk], data0=sp_f[:, 0:Wk], data1=sp_f[:, 0:Wk],
                                                             initial=0.0, op0=ALU.add, op1=ALU.bypass),
                              reads=[b_sp[i]], writes=[b_cs[i]])
                        S.dve(lambda e: e.tensor_scalar(out=ntot, in0=cs_f[:, Wk - 1:Wk], scalar1=-1.0, scalar2=None, op0=ALU.mult),
                              reads=[b_cs[i]], writes=[b_nt])
                        S.act(lambda e: e.activation(out=sp_f[:, 0:1], in_=ntot, func=AF.Exp), reads=[b_nt, b_sp[i], b_cs[i]], writes=[b_sp[i]])
                        S.act(lambda e: e.activation(out=sp_f[:, 1:Wk], in_=cs_f[:, 0:Wk - 1], func=AF.Exp, bias=ntot),
                              reads=[b_cs[i], b_nt], writes=[b_sp[i]])
                        S.dve(lambda e: e.tensor_tensor(out=wb[i][:, 0:Wk], in0=e_f[:, 0:Wk], in1=sp_f[:, 0:Wk], op=ALU.mult),
                              reads=[b_e[i], b_sp[i]], writes=[b_w[i]])
                        S.pool(lambda e: e.affine_select(out=wb[i][:, Wk - 128:Wk], in_=wb[i][:, Wk - 128:Wk], pattern=[[-1, 128]],
                                                         compare_op=ALU.is_gt, fill=0.0, base=0, channel_multiplier=1),
                               reads=[b_w[i]], writes=[b_w[i]])

                    def bst(hh=hh, qt=qt, i=i, v3=v3, b_vh=b_vh):
                        tq = slice(qt * 128, (qt + 1) * 128)
                        transposes_w(qt + 1, i)
                        sl = (hh * NT + qt) % 4
                        ops_ = PS[:, 3072 + sl * 128: 3072 + (sl + 1) * 128]
                        for c in range(qt + 1):
                            S.pe(lambda e, c=c: e.matmul(ops_, lhsT=v3[:, c, :], rhs=wTb[i][:, c, :], start=(c == 0), stop=(c == qt)),
                                 reads=[b_vh, b_wT[i]], writes=[PB6s[sl]])
                        S.act(lambda e: e.activation(out=oT[:, 0, tq], in_=ops_, func=AF.Copy), reads=[PB6s[sl]], writes=[b_oT])
                        if qt == NT - 1:
                            S.dma("sp", lambda e: e.dma_start(out=mergedT[hh * 128:(hh + 1) * 128, :], in_=oT[:, 0, :]), reads=[b_oT],
                                  writes=[b_scr["mergedT"]])
                    iters.append((a1, a2, bst))
            run_pipeline(iters)
            if stop == "sb":
                return
            df_sm = [[smalloc() for _ in range(11)] for _ in range(2)]
            dit = []
            for hh in range(4):
                hp = hh % 2
                v3 = vvb[hp].rearrange("p (c d) -> p c d", c=16)
                b_vh = b_vb[hp]
                for qt in range(NT):
                    i = it["n"] % 2
                    it["n"] += 1
                    dit.append(dict(hh=hh, qt=qt, i=i, v3=v3, b_vh=b_vh))

            def f_z(d, m):
                hh, qt, i, v3, b_vh = d["hh"], d["qt"], d["i"], d["v3"], d["b_vh"]
                if qt == 0 and m == 0:
                    for mm in range(2):
                        S.dma("sp", lambda e, mm=mm: e.dma_start(out=qT[mm], in_=dfqT[(hh * 2 + mm) * 128:(hh * 2 + mm + 1) * 128, :]),
                              reads=[b_scr["dfqT"]], writes=[b_q[mm]])
                        S.dma("sp", lambda e, mm=mm: e.dma_start(out=qT[2 + mm], in_=dfkT[(hh * 2 + mm) * 128:(hh * 2 + mm + 1) * 128, :]),
                              reads=[b_scr["dfkT"]], writes=[b_q[2 + mm]])
                    S.dma("sp", lambda e: e.dma_start(out=v3, in_=dfv[:, hh * 256:(hh + 1) * 256].rearrange("(c p) d -> p c d", p=128)),
                          reads=[b_scr["dfv"]], writes=[b_vh])
                sms = df_sm[i]
                Wk = (qt + 1) * 128
                tq = slice(qt * 128, (qt + 1) * 128)
                nb = (Wk + 511) // 512
                zb = ZB[0:nb]
                pm = [ef[i], csf[i]][m]
                b_pm = [b_e[i], b_cs[i]][m]
                (mx, b_mx), (ll, b_ll), (lb, b_lb) = sms[m * 3], sms[m * 3 + 1], sms[m * 3 + 2]
                for kb in range(nb):
                    k0, k1 = kb * 512, min((kb + 1) * 512, Wk)
                    S.pe(lambda e, k0=k0, k1=k1: e.matmul(zps[:, k0:k1], lhsT=qT[m][:, tq], rhs=qT[2 + m][:, k0:k1], start=True, stop=True),
                         reads=[b_q[m], b_q[2 + m]], writes=[ZB[kb]])
                S.dve(lambda e: e.tensor_reduce(out=mx, in_=zps[:, 0:Wk], axis=AX.X, op=ALU.max), reads=zb, writes=[b_mx])
                S.dve(lambda e: e.tensor_scalar(out=mx, in0=mx, scalar1=-1.0, scalar2=None, op0=ALU.mult), reads=[b_mx], writes=[b_mx])
                S.pool(lambda e: e.memset(lb, 0.0), writes=[b_lb])
                S.act(lambda e: e.activation(out=pm[:, 0:Wk - 64], in_=zps[:, 0:Wk - 64], func=AF.Exp, bias=mx, accum_out=ll),
                      reads=zb + [b_mx], writes=[b_pm, b_ll])
                S.act(lambda e: e.activation(out=pm[64:128, Wk - 64:Wk], in_=zps[64:128, Wk - 64:Wk], func=AF.Exp,
                                             bias=mx[64:128, :], accum_out=lb[64:128, :]),
                      reads=zb + [b_mx, b_lb], writes=[b_pm, b_lb])
                S.pool(lambda e: e.memset(pm[0:64, Wk - 64:Wk], 0.0), reads=[b_pm], writes=[b_pm])
                if m == 1:
                    for mm in range(2):
                        (ll2, b_ll2), (lb2, b_lb2) = sms[mm * 3 + 1], sms[mm * 3 + 2]
                        S.dve(lambda e, ll2=ll2, lb2=lb2: e.tensor_tensor(out=ll2, in0=ll2, in1=lb2, op=ALU.add), reads=[b_ll2, b_lb2], writes=[b_ll2])

            def f_a2a(d):
                i = d["i"]
                sms = df_sm[i]
                Wk = (d["qt"] + 1) * 128
                (r1, b_r1), (r2, b_r2) = sms[6], sms[7]
                l1, b_l1 = sms[1]
                l2, b_l2 = sms[4]
                p1 = ef[i]
                S.dve(lambda e: e.reciprocal(out=r1, in_=l1), reads=[b_l1], writes=[b_r1])
                S.dve(lambda e: e.reciprocal(out=r2, in_=l2), reads=[b_l2], writes=[b_r2])
                S.dve(lambda e: e.tensor_tensor(out=r2, in0=r2, in1=nlam, op=ALU.mult), reads=[b_r2, b_nlam], writes=[b_r2])
                S.act(lambda e: e.activation(out=p1[:, 0:Wk], in_=p1[:, 0:Wk], func=AF.Copy, scale=r1), reads=[b_e[i], b_r1], writes=[b_e[i]])

            def f_a2b(d):
                i = d["i"]
                sms = df_sm[i]
                Wk = (d["qt"] + 1) * 128
                r2, b_r2 = sms[7]
                p1, p2 = ef[i], csf[i]
                S.dve(lambda e: e.scalar_tensor_tensor(out=wb[i][:, 0:Wk], in0=p2[:, 0:Wk], scalar=r2, in1=p1[:, 0:Wk], op0=ALU.mult, op1=ALU.add),
                      reads=[b_e[i], b_cs[i], b_r2], writes=[b_w[i]])

            def f_T(d):
                transposes_w(d["qt"] + 1, d["i"])

            def f_AV(d):
                qt, i, v3, b_vh = d["qt"], d["i"], d["v3"], d["b_vh"]
                sms = df_sm[i]
                (ss, b_ss), (ve, b_ve), (rs, b_rs) = sms[8], sms[9], sms[10]
                ops_ = PS[:, 3072 + i * 256:3072 + (i + 1) * 256]
                for c in range(qt + 1):
                    S.pe(lambda e, c=c: e.matmul(ops_, lhsT=wTb[i][:, c, :], rhs=v3[:, c, :], start=(c == 0), stop=(c == qt)),
                         reads=[b_vh, b_wT[i]], writes=[PB6s[i]])
                S.act(lambda e: e.activation(out=junk[:, i * 256:(i + 1) * 256], in_=ops_, func=AF.Square, accum_out=ss),
                      reads=[PB6s[i]], writes=[b_junk, b_ss])
                rstd_op(ss, b_ss, ve, b_ve, rs, b_rs, 256)
                S.dve(lambda e: e.scalar_tensor_tensor(out=onbb[i], in0=ops_, scalar=rs, in1=gsub, op0=ALU.mult, op1=ALU.mult),
                      reads=[PB6s[i], b_rs, b_gsub], writes=[b_onb[i]])

            def f_O(d):
                hh, qt, i = d["hh"], d["qt"], d["i"]
                tq = slice(qt * 128, (qt + 1) * 128)
                tps = bank_bf(7)[:, i * 256:(i + 1) * 256]
                for j in range(2):
                    S.pe(lambda e, j=j: e.transpose(out=tps[:, j * 128:(j + 1) * 128], in_=onbb[i][:, j * 128:(j + 1) * 128], identity=ident[:]),
                         reads=[b_onb[i], b_ident], writes=[PB7s[i]])
                S.act(lambda e: e.activation(out=oT[:, :, tq], in_=tps.rearrange("p (j t) -> p j t", j=2), func=AF.Copy),
                      reads=[PB7s[i]], writes=[b_oT])
                if qt == NT - 1:
                    S.dma("sp", lambda e: e.dma_start(
                        out=mergedT[1024 + hh * 256:1024 + (hh + 1) * 256, :].rearrange("(j p) t -> p j t", p=128), in_=oT),
                        reads=[b_oT], writes=[b_scr["mergedT"]])

            Nn = len(dit)
            for n in range(-2, Nn):
                v0 = 0 <= n < Nn
                v1 = 0 <= n + 1 < Nn
                v2 = 0 <= n + 2 < Nn
                if v1:
                    f_a2a(dit[n + 1])
                if v0:
                    f_T(dit[n])
                if v2:
                    f_z(dit[n + 2], 0)
                if v1:
                    f_a2b(dit[n + 1])
                if v0:
                    f_AV(dit[n])
                if v2:
                    f_z(dit[n + 2], 1)
                if v0:
                    f_O(dit[n])
            for bb in PB6s:
                PB[6].W = sorted(set(PB[6].W) | set(bb.W)); PB[6].R = sorted(set(PB[6].R) | set(bb.R))
            for bb in PB7s:
                PB[7].W = sorted(set(PB[7].W) | set(bb.W)); PB[7].R = sorted(set(PB[7].R) | set(bb.R))

        def load_feature_major(dst3, dbufs, src_scr, src_name):
            region_switch(R2_all, Bbuf)
            for k in range(KC):
                S.dma("sp", lambda e, k=k: e.dma_start(out=dst3[:, k, :], in_=src_scr[k * 128:(k + 1) * 128, :]),
                      reads=[b_scr[src_name]], writes=[dbufs[k]])

        def odd_mixer():
            w_in = Wd["od_w_in"]
            region_switch(R2_all, Bbuf)
            def evac_u(ps, pbuf, nchg, tb):
                S.act(lambda e: e.activation(out=Bb[:, nchg, tb * 512:(tb + 1) * 512], in_=ps, func=AF.Gelu_apprx_tanh),
                      reads=[pbuf], writes=[Bbuf[nchg]])
            linear_A(A, A_tb, w_in[:, 0:2048], KC, evac_u)

            def evac_v(ps, pbuf, tt, cb, ex=None):
                i = next_stg()
                S.act(lambda e: e.activation(out=stg[i][:], in_=ps, func=AF.Gelu_apprx_tanh), reads=[pbuf], writes=[b_stg[i]])
                S.dma("sp", lambda e: e.dma_start(out=vsc[tt * 128:(tt + 1) * 128, cb * 512:(cb + 1) * 512], in_=stg[i][:]),
                      reads=[b_stg[i]], writes=[b_scr["vsc"]])
            linear_B(A, A_tt, w_in[:, 2048:4096], KC, evac_v)
            wsn = R1f[:, 0:2048].rearrange("p (g s) -> p g s", g=16)
            wsb = R1[:, 4096:6144].rearrange("p (g s) -> p g s", g=16)
            wmT = R1[:, 6144:8192].rearrange("p (g t) -> p g t", g=16)
            bsb = R1f[:, 4096:6144].rearrange("p (g t) -> p g t", g=16)
            lng = R1f[:, 6144:8192]
            lnb = R1f[:, 8192:10240]
            vnb = R1[:, 20480:22528]
            tmpf = R1f[:, 11264:11776]
            b_ws, b_wmT, b_bsb, b_ln, b_vnb, b_tmpf = (Buf(n) for n in ["wsn", "wmT", "bsb", "ln", "vnb", "tmpf"])
            region_switch(R1_all, [b_ws, b_wmT, b_bsb, b_ln, b_vnb, b_tmpf])
            S.dma("sp", lambda e: e.dma_start(out=wsn, in_=Wd["od_w_s"].rearrange("g t s -> t g s")), writes=[b_ws])
            S.dma("sp", lambda e: e.dma_start(out=bsb.rearrange("p g t -> p (g t)"), in_=Wd["od_b_s"].partition_broadcast(128)), writes=[b_bsb])
            S.dma("sp", lambda e: e.dma_start(out=lng, in_=Wd["od_ln_g"].partition_broadcast(128)), writes=[b_ln])
            S.dma("sp", lambda e: e.dma_start(out=lnb, in_=Wd["od_ln_b"].partition_broadcast(128)), writes=[b_ln])
            S.dve(lambda e: e.tensor_copy(out=wsb, in_=wsn), reads=[b_ws], writes=[b_ws])
            S.pool(lambda e: e.memset(wsb[0:64, :, 64:128], 0.0), reads=[b_ws], writes=[b_ws])
            for half in range(2):
                for j in range(8):
                    g = half * 8 + j
                    S.pe(lambda e, half=half, j=j, g=g: e.transpose(out=bank_bf(4 + half)[:, j * 128:(j + 1) * 128], in_=wsb[:, g, :],
                                                                    identity=ident[:]), reads=[b_ws, b_ident], writes=[PB[4 + half]])
                S.dve(lambda e, half=half: e.tensor_copy(out=wmT[:, half * 8:(half + 1) * 8, :],
                                                         in_=bank_bf(4 + half).rearrange("p (g t) -> p g t", g=8)),
                      reads=[PB[4 + half]], writes=[b_wmT])
            FMAX = 512
            stats = R1f[:, 12288:12288 + 4 * 6]
            mv = R1f[:, 12320:12322]
            prefetch_w(Wd["od_w_out"][:, 0:512], KC)
            prefetch_w(Wd["od_w_out"][:, 512:1024], KC)
            vbuf = [hst, mst]
            b_vbuf = [b_hst, b_mst]
            vnbb = [R1[:, 20480:22528], R1[:, 26624:28672]]
            b_vnbb = [b_vnb, Buf("vnb1")]
            region_switch(R1_all, [b_vnbb[1]])
            sp_sm = [[smalloc() for _ in range(3)] for _ in range(2)]
            statsb = [R1f[:, 12288 + i * 32:12288 + i * 32 + 24] for i in range(2)]
            mvb = [R1f[:, 12352 + i * 4:12352 + i * 4 + 2] for i in range(2)]
            b_stats = [Buf("stats0"), Buf("stats1")]
            region_switch(R1_all, b_stats)

            def g1(n):
                i = n % 2
                vt = vbuf[i]
                stats, mv = statsb[i], mvb[i]
                (ve, b_ve), (rs, b_rs), (nmr, b_nmr) = sp_sm[i]
                rows = slice(n * 128, (n + 1) * 128)
                S.dma("sp", lambda e: e.dma_start(out=vt[:], in_=vsc[rows, :]), reads=[b_scr["vsc"]], writes=[b_vbuf[i]])
                for c in range(4):
                    S.dve(lambda e, c=c: e.bn_stats(out=stats[:, c * 6:(c + 1) * 6], in_=vt[:, c * 512:(c + 1) * 512]), reads=[b_vbuf[i]], writes=[b_stats[i]])
                S.dve(lambda e: e.bn_aggr(out=mv, in_=stats), reads=[b_stats[i]], writes=[b_stats[i]])
                S.dve(lambda e: e.tensor_scalar(out=ve, in0=mv[:, 1:2], scalar1=EPS, scalar2=None, op0=ALU.add), reads=[b_stats[i]], writes=[b_ve])
                S.pool(lambda e: e.tensor_tensor(out=rs, in0=ve, in1=negh, op=ALU.pow), reads=[b_ve, b_negh], writes=[b_rs])
                S.dve(lambda e: e.scalar_tensor_tensor(out=nmr, in0=mv[:, 0:1], scalar=-1.0, in1=rs, op0=ALU.mult, op1=ALU.mult),
                      reads=[b_stats[i], b_rs], writes=[b_nmr])
                S.act(lambda e: e.activation(out=vt[:], in_=vt[:], func=AF.Identity, scale=rs, bias=nmr), reads=[b_vbuf[i], b_rs, b_nmr], writes=[b_vbuf[i]])
                S.pool(lambda e: e.tensor_tensor(out=vt[:], in0=vt[:], in1=lng, op=ALU.mult), reads=[b_vbuf[i], b_ln], writes=[b_vbuf[i]])
                S.dve(lambda e: e.tensor_tensor(out=vnbb[i], in0=vt[:], in1=lnb, op=ALU.add), reads=[b_vbuf[i], b_ln], writes=[b_vnbb[i]])

            def g2(n):
                i = n % 2
                vn = vnbb[i]
                for g4 in range(4):
                    b = next_pb()
                    for j in range(4):
                        g = g4 * 4 + j
                        S.pe(lambda e, b=b, j=j, g=g: e.matmul(bank(b)[:, j * 128:(j + 1) * 128], lhsT=vn[:, g * 128:(g + 1) * 128], rhs=wmT[:, g, :],
                                                               start=True, stop=True), reads=[b_vnbb[i], b_wmT], writes=[PB[b]])
                    tf = R1f[:, 11264 + (g4 % 2) * 512:11264 + (g4 % 2 + 1) * 512]
                    b_tf = [b_tmpf, b_tmpf2][g4 % 2]
                    S.dve(lambda e, b=b, g4=g4, tf=tf: e.tensor_tensor(out=tf.rearrange("p (g t) -> p g t", g=4),
                                                                       in0=bank(b).rearrange("p (g t) -> p g t", g=4),
                                                                       in1=bsb[:, g4 * 4:(g4 + 1) * 4, :], op=ALU.add),
                          reads=[PB[b], b_bsb], writes=[b_tf])
                    yv = Bb[:, g4 * 4:(g4 + 1) * 4, n * 128:(n + 1) * 128]
                    eng = S.pool if g4 % 2 == 0 else S.dve
                    eng(lambda e, yv=yv, tf=tf: e.tensor_tensor(out=yv, in0=yv, in1=tf.rearrange("p (g t) -> p g t", g=4), op=ALU.mult),
                        reads=[b_tf] + Bbuf[g4 * 4:(g4 + 1) * 4], writes=Bbuf[g4 * 4:(g4 + 1) * 4])

            b_tmpf2 = Buf("tmpf2")
            region_switch(R1_all, [b_tmpf2])
            for n in range(-1, NT):
                if n + 1 < NT:
                    g1(n + 1)
                if n >= 0:
                    g2(n)
            linear_B(Bb, B_all, Wd["od_w_out"], KC, evac_to_ms)

        prefetch_w(Wd["ev_w_in"][:, 0:1024][:, 0:512], KC)
        prefetch_w(Wd["ev_w_in"][:, 0:1024][:, 512:1024], KC)
        upd_pass(x_in, None, None, None, "norm", GC_EV)
        even_mixer()
        if stop in ("qkv", "sb"):
            pass
        else:
            load_feature_major(Bb, Bbuf, mergedT, "mergedT")
            linear_B(Bb, B_all, Wd["ev_w_out"], KC, evac_to_ms)
            if stop == "mix0":
                upd_pass(x_in, ms, Wd["ev_norm_post"], out, None, None, final=True)
            else:
                prefetch_w(Wd["ffn_w1"][0][:, 0:2048][:, 0:512], KC)
                prefetch_w(Wd["ffn_w1"][0][:, 0:2048][:, 512:1024], KC)
                upd_pass(x_in, ms, Wd["ev_norm_post"], hs, "norm", GC_F0)
                ffn(0)
                if stop == "ffn0":
                    upd_pass(hs, ms, Wd["ffn_norm_post"][0:1, :], out, None, None, final=True)
                else:
                    prefetch_w(Wd["ple_w_proj"][0][:, 0:512], 2)
                    prefetch_w(Wd["ple_w_gate"][0][:, 0:512], KC)
                    upd_pass(hs, ms, Wd["ffn_norm_post"][0:1, :], hs, "raw", None)
                    ple(0)
                    if stop == "l0":
                        upd_pass(hs, ms, Wd["ple_norm"][0:1, :], out, None, None, final=True)
                    else:
                        prefetch_w(Wd["od_w_in"][:, 0:2048][:, 0:512], KC)
                        prefetch_w(Wd["od_w_in"][:, 0:2048][:, 512:1024], KC)
                        upd_pass(hs, ms, Wd["ple_norm"][0:1, :], hs, "norm", GC_OD)
                        odd_mixer()
                        if stop == "mix1":
                            upd_pass(hs, ms, Wd["od_norm_post"], out, None, None, final=True)
                        else:
                            prefetch_w(Wd["ffn_w1"][1][:, 0:2048][:, 0:512], KC)
                            prefetch_w(Wd["ffn_w1"][1][:, 0:2048][:, 512:1024], KC)
                            upd_pass(hs, ms, Wd["od_norm_post"], hs, "norm", GC_F1)
                            ffn(1)
                            prefetch_w(Wd["ple_w_proj"][1][:, 0:512], 2)
                            prefetch_w(Wd["ple_w_gate"][1][:, 0:512], KC)
                            upd_pass(hs, ms, Wd["ffn_norm_post"][1:2, :], hs, "raw", None)
                            ple(1)
                            upd_pass(hs, ms, Wd["ple_norm"][1:2, :], out, None, None, final=True)
        if stop in ("qkv", "sb"):
            o = S.dma("sp", lambda e: e.dma_start(out=out[0:128, :], in_=hst[:]), reads=[b_hst], writes=[b_out])
            out_ops.append(o)
        assert not pref, list(pref)
        S.emit(final_wait_ops=out_ops)
    return nc


def make_in_maps(inputs):
    f = lambda a: np.ascontiguousarray(np.asarray(a))
    shared = {}
    for name, shape in W_SPECS:
        shared[name] = f(inputs[name]).reshape(shape)
    maps = []
    for b in range(8):
        m = dict(shared)
        m["x"] = f(inputs["x"][b])
        m["p"] = f(inputs["p"][:, b])
        m["positions"] = f(inputs["positions"][b:b + 1]).astype(np.int32)
        maps.append(m)
    return maps


_NC_CACHE = {}


def kernel(**inputs):
    if "nc" not in _NC_CACHE:
        _NC_CACHE["nc"] = build()
    nc = _NC_CACHE["nc"]
    maps = make_in_maps(inputs)
    res = run_bass_kernel_spmd(nc, maps, core_ids=list(range(8)))
    return np.stack([np.asarray(r["out"]) for r in res.results], axis=0).astype(np.float32)
```

```python
import math
from contextlib import ExitStack

import numpy as np
import concourse.bass as bass
import concourse.mybir as mybir
from concourse.bass_utils import run_bass_kernel_spmd

F32 = mybir.dt.float32
BF16 = mybir.dt.bfloat16
I32 = mybir.dt.int32
AF = mybir.ActivationFunctionType
ALU = mybir.AluOpType
AX = mybir.AxisListType

ENGINES = ("pe", "act", "dve", "pool", "sp")
DMA_SEMS_PER_QUEUE = 8


class Buf:
    __slots__ = ("name", "W", "R")

    def __init__(self, name):
        self.name = name
        self.W = []
        self.R = []


class _Op:
    __slots__ = ("eng", "fn", "deps", "is_dma", "idx", "signal", "ticket", "dsem")

    def __init__(self, eng, fn, is_dma, idx):
        self.eng = eng
        self.fn = fn
        self.is_dma = is_dma
        self.idx = idx
        self.deps = set()
        self.signal = False
        self.ticket = None
        self.dsem = None


class Sched:
    def __init__(self, nc):
        self.nc = nc
        self.ops = []

    def op(self, eng, fn, reads=(), writes=(), dma=False):
        o = _Op(eng, fn, dma, len(self.ops))
        deps = set()
        for b in reads:
            deps.update(b.W)
        for b in writes:
            deps.update(b.W)
            deps.update(b.R)
        o.deps = deps
        for b in writes:
            b.W = [o.idx]
            b.R = []
        for b in reads:
            if b in writes:
                continue
            if not dma:
                b.R = [i for i in b.R if not (self.ops[i].eng == eng and not self.ops[i].is_dma)]
            b.R.append(o.idx)
        self.ops.append(o)
        return o

    def pe(self, fn, reads=(), writes=()):
        return self.op("pe", fn, reads, writes)

    def act(self, fn, reads=(), writes=()):
        return self.op("act", fn, reads, writes)

    def dve(self, fn, reads=(), writes=()):
        return self.op("dve", fn, reads, writes)

    def pool(self, fn, reads=(), writes=()):
        return self.op("pool", fn, reads, writes)

    def dma(self, eng, fn, reads=(), writes=()):
        return self.op(eng, fn, reads, writes, dma=True)

    def emit(self, final_wait_ops=()):
        nc = self.nc
        ops = self.ops
        for o in ops:
            for d in o.deps:
                p = ops[d]
                if p.is_dma:
                    continue
                if p.eng != o.eng or o.is_dma or o.eng != "pe":
                    p.signal = True
        with ExitStack() as es:
            esem = {e: es.enter_context(nc.semaphore("s_" + e)) for e in ENGINES}
            dsems = {}
            for e in ("sp", "act", "pool"):
                dsems[e] = [es.enter_context(nc.semaphore(f"d_{e}_{i}")) for i in range(DMA_SEMS_PER_QUEUE)]
            cnt = {e: 0 for e in ENGINES}
            dcnt = {e: 0 for e in dsems}
            dtot = {e: [0] * DMA_SEMS_PER_QUEUE for e in dsems}
            for o in ops:
                if o.is_dma:
                    j = dcnt[o.eng] % DMA_SEMS_PER_QUEUE
                    dcnt[o.eng] += 1
                    o.dsem = (o.eng, j, dtot[o.eng][j])
                    dtot[o.eng][j] += 16
                    o.ticket = (dsems[o.eng][j], dtot[o.eng][j], ("d", o.eng, j))
                elif o.signal:
                    cnt[o.eng] += 1
                    o.ticket = (esem[o.eng], cnt[o.eng], ("e", o.eng))
            self.max_ticket = dict(cnt)
            per_eng = {e: [o for o in ops if o.eng == e] for e in ENGINES}
            final = list(final_wait_ops)
            blk = es.enter_context(nc.Block())

            def run(e, eng):
                seen = {}
                for o in per_eng[e]:
                    waits = {}
                    for d in o.deps:
                        p = ops[d]
                        if p.ticket is None:
                            continue
                        if (not p.is_dma) and (not o.is_dma) and p.eng == e and e == "pe":
                            continue
                        sem, val, key = p.ticket
                        if seen.get(key, 0) >= val:
                            continue
                        if key not in waits or waits[key][1] < val:
                            waits[key] = (sem, val)
                    if o.is_dma:
                        qe, j, prev = o.dsem
                        key = ("d", qe, j)
                        if prev > 0 and seen.get(key, 0) < prev:
                            if key not in waits or waits[key][1] < prev:
                                waits[key] = (dsems[qe][j], prev)
                    for key, (sem, val) in waits.items():
                        eng.wait_ge(sem, val)
                        seen[key] = val
                    ins = o.fn(eng)
                    if o.ticket is not None:
                        sem, val, key = o.ticket
                        ins.then_inc(sem, 16 if o.is_dma else 1)
                if e == "sp":
                    for o in final:
                        sem, val, key = o.ticket
                        eng.wait_ge(sem, val)

            @blk.tensor
            def _(eng):
                run("pe", eng)

            @blk.scalar
            def _(eng):
                run("act", eng)

            @blk.vector
            def _(eng):
                run("dve", eng)

            @blk.gpsimd
            def _(eng):
                run("pool", eng)

            @blk.sync
            def _(eng):
                run("sp", eng)


D = 2048
T = 2048
NT = 16
KC = 16
HD = 128
DFF = 8192
PLE = 256
EPS = 1e-6
ROPE_THETA = 500000.0
ROT = 32
SCALE = HD ** -0.5
MAGIC = 12582912.0
TWO_PI = 2.0 * math.pi
C1 = 6.28125
C2 = 1015.0 / 524288.0
C3 = TWO_PI - C1 - C2

W_SPECS = [
    ("ev_norm_pre", [1, D]), ("ev_w_in", [D, 6144]),
    ("ev_lam_q1", [1, HD]), ("ev_lam_k1", [1, HD]), ("ev_lam_q2", [1, HD]), ("ev_lam_k2", [1, HD]),
    ("ev_subln", [1, 256]), ("ev_w_out", [D, D]), ("ev_norm_post", [1, D]),
    ("od_norm_pre", [1, D]), ("od_w_in", [D, 4096]), ("od_ln_g", [1, D]), ("od_ln_b", [1, D]),
    ("od_w_s", [16, 128, 128]), ("od_b_s", [1, 2048]), ("od_w_out", [D, D]), ("od_norm_post", [1, D]),
    ("ffn_norm_pre", [2, D]), ("ffn_w1", [2, D, DFF]), ("ffn_w2", [2, DFF, D]), ("ffn_norm_post", [2, D]),
    ("ple_w_proj", [2, PLE, D]), ("ple_w_gate", [2, D, D]), ("ple_norm", [2, D]),
]


def build(stop=None, debug=False):
    nc = bass.Bass("TRN2", target_bir_lowering=False)
    x_in = nc.dram_tensor("x", [T, D], F32, kind="ExternalInput").ap()
    p_in = nc.dram_tensor("p", [2, T, PLE], F32, kind="ExternalInput").ap()
    pos_in = nc.dram_tensor("positions", [1, T], I32, kind="ExternalInput").ap()
    Wd = {}
    for name, shape in W_SPECS:
        Wd[name] = nc.dram_tensor(name, shape, F32, kind="ExternalInput").ap()
    out = nc.dram_tensor("out", [T, D], F32, kind="ExternalOutput").ap()
    dk = "ExternalOutput" if debug else "Internal"
    hs = nc.dram_tensor("hs", [T, D], F32, kind=dk).ap()
    ms = nc.dram_tensor("ms", [T, D], F32, kind=dk).ap()
    sbqT = nc.dram_tensor("sbqT", [1024, T], BF16, kind=dk).ap()
    sbkT = nc.dram_tensor("sbkT", [1024, T], BF16, kind=dk).ap()
    sbv = nc.dram_tensor("sbv", [T, 1024], BF16, kind=dk).ap()
    dfqT = nc.dram_tensor("dfqT", [1024, T], BF16, kind=dk).ap()
    dfkT = nc.dram_tensor("dfkT", [1024, T], BF16, kind=dk).ap()
    dfv = nc.dram_tensor("dfv", [T, 1024], BF16, kind=dk).ap()
    mergedT = nc.dram_tensor("mergedT", [D, T], BF16, kind=dk).ap()
    vsc = nc.dram_tensor("vsc", [T, D], F32, kind=dk).ap()

    es = ExitStack()
    with es:
        def sb(name, shape, dt):
            return es.enter_context(nc.sbuf_tensor(name, shape, dt))

        R1 = sb("R1", [128, 32768], BF16)
        R2 = sb("R2", [128, 32768], BF16)
        WS = [sb(f"ws{i}", [128, 16, 512], BF16) for i in range(2)]
        hst = sb("hst", [128, D], F32)
        mst = sb("mst", [128, D], F32)
        gpb = sb("gpb", [128, D], F32)
        xnb = sb("xnb", [128, D], BF16)
        stg = [sb(f"stg{i}", [128, 512], F32) for i in range(4)]
        gcols = sb("gcols", [128, 16 * 5], F32)
        ident = sb("ident", [128, 128], BF16)
        identf = sb("identf", [128, 128], F32)
        sm = sb("sm", [128, 128], F32)
        pmat = sb("pmat", [128, 32], F32)
        g16 = sb("g16", [16, 5 * 128], F32)
        PS = es.enter_context(nc.psum_tensor("PS", [128, 4096], F32))

        S = Sched(nc)
        A = R1[:].rearrange("p (k t) -> p k t", k=16)
        Bb = R2[:].rearrange("p (k t) -> p k t", k=16)
        R1f = R1[:].bitcast(F32)
        R2f = R2[:].bitcast(F32)

        def bank(b):
            return PS[:, b * 512:(b + 1) * 512]

        def bank_bf(b):
            return PS[:, b * 512:(b + 1) * 512].bitcast(BF16)

        PB = [Buf(f"pb{b}") for b in range(8)]
        Abuf = [Buf(f"A{t}") for t in range(NT)]
        Bbuf = [Buf(f"B{k}") for k in range(KC)]
        WSb = [Buf("ws0"), Buf("ws1")]
        b_hst, b_mst, b_gpb, b_xnb = Buf("hst"), Buf("mst"), Buf("gpb"), Buf("xnb")
        b_stg = [Buf(f"stg{i}") for i in range(4)]
        b_gcols, b_ident, b_sm, b_pmat, b_g16 = Buf("gcols"), Buf("ident"), Buf("sm"), Buf("pmat"), Buf("g16")
        b_hs = [Buf(f"hs{t}") for t in range(NT)]
        b_ms = [Buf(f"ms{t}") for t in range(NT)]
        b_out = Buf("out")
        b_R1 = Buf("R1")
        b_R2 = Buf("R2")
        b_scr = {n: Buf(n) for n in ["sbqT", "sbkT", "sbv", "dfqT", "dfkT", "dfv", "mergedT", "vsc"]}
        out_ops = []
        st = {"stg": 0, "ws": 0, "pb": 0, "sm": 20}
        R1_all = set(Abuf)
        R2_all = set(Bbuf)

        def region_switch(region_all, new_bufs):
            Wu, Ru = set(), set()
            for b in region_all:
                Wu.update(b.W)
                Ru.update(b.R)
            for b in new_bufs:
                b.W = sorted(set(b.W) | Wu)
                b.R = sorted(set(b.R) | Ru)
            region_all.update(new_bufs)

        SM_SS, SM_VE, SM_RSTD, SM_NEGH, SM_EPS, SM_SS2, SM_RSTD2 = 0, 1, 2, 3, 4, 5, 6
        SM_MX, SM_L1, SM_L2, SM_LB, SM_R1, SM_R2, SM_NLAM, SM_TMP, SM_TMP2 = 8, 9, 10, 11, 12, 13, 14, 15, 16
        SM_NTOT = 17

        def smc(c, n=128):
            return sm[0:n, c:c + 1]

        S.pool(lambda e: e.memset(ident[:], 1.0), writes=[b_ident])
        S.pool(lambda e: e.affine_select(out=ident[:], in_=ident[:], pattern=[[-1, 128]], compare_op=ALU.is_equal,
                                         fill=0.0, base=0, channel_multiplier=1), reads=[b_ident], writes=[b_ident])
        S.pool(lambda e: e.memset(identf[:], 1.0), writes=[b_ident])
        S.pool(lambda e: e.affine_select(out=identf[:], in_=identf[:], pattern=[[-1, 128]], compare_op=ALU.is_equal,
                                         fill=0.0, base=0, channel_multiplier=1), reads=[b_ident], writes=[b_ident])
        S.pool(lambda e: e.memset(sm[:], 0.0), writes=[b_sm])
        S.pool(lambda e: e.memset(smc(SM_NEGH), -0.5), writes=[b_sm])
        S.pool(lambda e: e.memset(smc(SM_EPS), EPS), writes=[b_sm])
        gnames = [("ev_norm_pre", 0), ("ffn_norm_pre", 0), ("od_norm_pre", 0), ("ffn_norm_pre", 1)]
        for j, (nm, li) in enumerate(gnames):
            src = Wd[nm][li:li + 1, :].rearrange("o (k p) -> (o k) p", p=128)
            S.dma("sp", lambda e, j=j, src=src: e.dma_start(out=g16[:, j * 128:(j + 1) * 128], in_=src), writes=[b_g16])
        for j in range(len(gnames)):
            S.pe(lambda e, j=j: e.matmul(PS[:, 3584 + j * 16:3584 + (j + 1) * 16], lhsT=g16[:, j * 128:(j + 1) * 128],
                                          rhs=identf[0:16, 0:16], start=True, stop=True),
                 reads=[b_g16, b_ident], writes=[PB[7]])
        S.dve(lambda e: e.tensor_copy(out=gcols[:, 0:64], in_=PS[:, 3584:3584 + 64]), reads=[PB[7]], writes=[b_gcols])
        GC_EV, GC_F0, GC_OD, GC_F1 = 0, 1, 2, 3

        def rstd_from_ss(c_ss, c_out, n_feat, np_=128):
            S.dve(lambda e: e.tensor_scalar(out=smc(SM_VE, np_), in0=smc(c_ss, np_), scalar1=1.0 / n_feat, scalar2=EPS,
                                            op0=ALU.mult, op1=ALU.add), reads=[b_sm], writes=[b_sm])
            S.pool(lambda e: e.tensor_tensor(out=smc(c_out, np_), in0=smc(SM_VE, np_), in1=smc(SM_NEGH, np_), op=ALU.pow),
                   reads=[b_sm], writes=[b_sm])

        b_negh = Buf("negh")
        negh = sm[:, 120:121]
        S.pool(lambda e: e.memset(negh, -0.5), writes=[b_negh])
        st["sm"] = 20

        def smalloc():
            c = st["sm"]
            st["sm"] += 1
            assert c < 120
            return sm[:, c:c + 1], Buf(f"sm{c}")

        def rstd_op(ss, b_ss, ve, b_ve, rs, b_rs, n_feat):
            S.dve(lambda e: e.tensor_scalar(out=ve, in0=ss, scalar1=1.0 / n_feat, scalar2=EPS, op0=ALU.mult, op1=ALU.add),
                  reads=[b_ss], writes=[b_ve])
            S.pool(lambda e: e.tensor_tensor(out=rs, in0=ve, in1=negh, op=ALU.pow), reads=[b_ve, b_negh], writes=[b_rs])

        upd_sm = [[smalloc() for _ in range(6)] for _ in range(2)]

        def upd_pass(h_src, m_src, g_post, h_dst, a_mode, gcol_idx, final=False):
            NB3 = 3
            hstb = [R2f[:, i * 2048:(i + 1) * 2048] for i in range(NB3)]
            mstb = [R2f[:, 6144 + i * 2048:6144 + (i + 1) * 2048] for i in range(NB3)]
            xnbb = [R2[:, 24576 + i * 2048:24576 + (i + 1) * 2048] for i in range(2)]
            junkb = R2[:, 28672:30720]
            bh = [Buf(f"uh{i}") for i in range(NB3)]
            bm = [Buf(f"um{i}") for i in range(NB3)]
            bx = [Buf("ux0"), Buf("ux1")]
            bj = Buf("ujunk")
            region_switch(R2_all, bh + bm + bx + [bj])
            if m_src is not None:
                S.dma("sp", lambda e: e.dma_start(out=gpb[:], in_=g_post.partition_broadcast(128)), writes=[b_gpb])
            if a_mode is not None:
                region_switch(R1_all, Abuf)
            def s1(tt):
                i = tt % NB3
                hs_, ms_ = hstb[i], mstb[i]
                (ssm, b_ssm), (vem, b_vem), (rsm, b_rsm) = upd_sm[tt % 2][0:3]
                rows = slice(tt * 128, (tt + 1) * 128)
                rd = [b_hs[tt]] if h_src is hs else []
                S.dma("sp", lambda e: e.dma_start(out=hs_, in_=h_src[rows, :]), reads=rd, writes=[bh[i]])
                if m_src is not None:
                    S.dma("sp", lambda e: e.dma_start(out=ms_, in_=m_src[rows, :]), reads=[b_ms[tt]], writes=[bm[i]])
                    S.act(lambda e: e.activation(out=junkb, in_=ms_, func=AF.Square, accum_out=ssm),
                          reads=[bm[i]], writes=[bj, b_ssm])
                    rstd_op(ssm, b_ssm, vem, b_vem, rsm, b_rsm, D)
                    S.dve(lambda e: e.scalar_tensor_tensor(out=ms_, in0=ms_, scalar=rsm, in1=gpb[:], op0=ALU.mult, op1=ALU.mult),
                          reads=[bm[i], b_rsm, b_gpb], writes=[bm[i]])
                    S.dve(lambda e: e.tensor_tensor(out=hs_, in0=hs_, in1=ms_, op=ALU.add),
                          reads=[bh[i], bm[i]], writes=[bh[i]])
                if h_dst is not None:
                    o = S.dma("pool", lambda e: e.dma_start(out=h_dst[rows, :], in_=hs_), reads=[bh[i]],
                              writes=[b_hs[tt]] if h_dst is hs else [b_out])
                    if final:
                        out_ops.append(o)

            def s2(tt):
                if a_mode is None:
                    return
                i = tt % NB3
                ix = tt % 2
                hs_, xn_ = hstb[i], xnbb[ix]
                (ssh, b_ssh), (veh, b_veh), (rsh, b_rsh) = upd_sm[ix][3:6]
                if a_mode == "norm":
                    S.act(lambda e: e.activation(out=junkb, in_=hs_, func=AF.Square, accum_out=ssh),
                          reads=[bh[i]], writes=[bj, b_ssh])
                    rstd_op(ssh, b_ssh, veh, b_veh, rsh, b_rsh, D)
                    S.act(lambda e: e.activation(out=xn_, in_=hs_, func=AF.Copy, scale=rsh),
                          reads=[bh[i], b_rsh], writes=[bx[ix]])
                else:
                    S.act(lambda e: e.activation(out=xn_, in_=hs_, func=AF.Copy), reads=[bh[i]], writes=[bx[ix]])
                for half in range(2):
                    bk = 4 + half
                    for j in range(8):
                        k = half * 8 + j
                        S.pe(lambda e, bk=bk, j=j, k=k: e.transpose(out=bank_bf(bk)[:, j * 128:(j + 1) * 128],
                                                                    in_=xn_[:, k * 128:(k + 1) * 128], identity=ident[:]),
                             reads=[bx[ix], b_ident], writes=[PB[bk]])
                    dst = A[:, half * 8:(half + 1) * 8, tt * 128:(tt + 1) * 128]
                    src = bank_bf(bk).rearrange("p (k t) -> p k t", k=8)
                    if a_mode == "norm":
                        gc = gcols[:, gcol_idx * 16 + half * 8: gcol_idx * 16 + half * 8 + 8].unsqueeze(2).to_broadcast([128, 8, 128])
                        S.dve(lambda e, dst=dst, src=src, gc=gc: e.tensor_tensor(out=dst, in0=src, in1=gc, op=ALU.mult),
                              reads=[PB[bk], b_gcols], writes=[Abuf[tt]])
                    else:
                        S.dve(lambda e, dst=dst, src=src: e.tensor_copy(out=dst, in_=src), reads=[PB[bk]], writes=[Abuf[tt]])

            for n in range(-1, NT):
                if n + 1 < NT:
                    s1(n + 1)
                if n >= 0:
                    s2(n)

        pref = {}

        def prefetch_w(wap, kc):
            k = repr(wap)
            if k in pref:
                return
            pref[k] = load_w(wap, kc, _nopref=True)

        def load_w(wap, kc, eng="pool", _nopref=False):
            if not _nopref:
                k = repr(wap)
                if k in pref:
                    return pref.pop(k)
            i = st["ws"] % 2
            st["ws"] += 1
            ncols = wap.shape[1]
            src = wap.rearrange("(k p) n -> p k n", p=128)
            S.dma(eng, lambda e: e.dma_start(out=WS[i][:, 0:kc, 0:ncols], in_=src), writes=[WSb[i]])
            return i

        def next_pb(n=4):
            b = st["pb"] % n
            st["pb"] += 1
            return b

        def next_stg():
            i = st["stg"] % 4
            st["stg"] += 1
            return i

        def linear_A(X, Xbufs_for_tb, wap, kc, evac):
            N = wap.shape[1]
            for cb in range(N // 512):
                wi = load_w(wap[:, cb * 512:(cb + 1) * 512], kc)
                for nch in range(4):
                    for tb in range(4):
                        b = next_pb()
                        for k in range(kc):
                            S.pe(lambda e, b=b, wi=wi, k=k, nch=nch, tb=tb: e.matmul(
                                bank(b), lhsT=WS[wi][:, k, nch * 128:(nch + 1) * 128], rhs=X[:, k, tb * 512:(tb + 1) * 512],
                                start=(k == 0), stop=(k == kc - 1)),
                                 reads=[WSb[wi]] + Xbufs_for_tb(tb), writes=[PB[b]])
                        evac(bank(b), PB[b], cb * 4 + nch, tb)

        def linear_B(X, Xbufs_for_tt, wap, kc, evac, extra=None):
            N = wap.shape[1]
            for cb in range(N // 512):
                wi = load_w(wap[:, cb * 512:(cb + 1) * 512], kc)
                ex = extra(cb) if extra is not None else None
                for tt in range(NT):
                    b = next_pb()
                    for k in range(kc):
                        S.pe(lambda e, b=b, wi=wi, k=k, tt=tt: e.matmul(
                            bank(b), lhsT=X[:, k, tt * 128:(tt + 1) * 128], rhs=WS[wi][:, k, 0:512],
                            start=(k == 0), stop=(k == kc - 1)),
                             reads=[WSb[wi]] + Xbufs_for_tt(tt), writes=[PB[b]])
                    evac(bank(b), PB[b], tt, cb, ex)

        def A_tb(tb):
            return Abuf[tb * 4:(tb + 1) * 4]

        def A_tt(tt):
            return [Abuf[tt]]

        def B_all(_):
            return Bbuf

        def evac_to_ms(ps, pbuf, tt, cb, ex=None):
            i = next_stg()
            S.act(lambda e: e.activation(out=stg[i][:], in_=ps, func=AF.Copy), reads=[pbuf], writes=[b_stg[i]])
            S.dma("sp", lambda e: e.dma_start(out=ms[tt * 128:(tt + 1) * 128, cb * 512:(cb + 1) * 512], in_=stg[i][:]),
                  reads=[b_stg[i]], writes=[b_ms[tt]])

        def ffn(li):
            w1 = Wd["ffn_w1"][li]
            w2 = Wd["ffn_w2"][li]
            region_switch(R2_all, Bbuf)
            for fb in range(4):
                def evac1(ps, pbuf, nchg, tb):
                    i = next_stg()
                    S.act(lambda e: e.activation(out=stg[i][:], in_=ps, func=AF.Relu), reads=[pbuf], writes=[b_stg[i]])
                    S.dve(lambda e: e.scalar_tensor_tensor(out=Bb[:, nchg, tb * 512:(tb + 1) * 512], in0=ps, scalar=0.0,
                                                           in1=stg[i][:], op0=ALU.max, op1=ALU.mult),
                          reads=[pbuf, b_stg[i]], writes=[Bbuf[nchg]])
                linear_A(A, A_tb, w1[:, fb * 2048:(fb + 1) * 2048], KC, evac1)

                def evac2(ps, pbuf, tt, cb, ex=None, fb=fb):
                    i = next_stg()
                    dst = ms[tt * 128:(tt + 1) * 128, cb * 512:(cb + 1) * 512]
                    if fb == 0:
                        S.act(lambda e: e.activation(out=stg[i][:], in_=ps, func=AF.Copy), reads=[pbuf], writes=[b_stg[i]])
                    else:
                        S.dma("sp", lambda e: e.dma_start(out=stg[i][:], in_=dst), reads=[b_ms[tt]], writes=[b_stg[i]])
                        S.dve(lambda e: e.tensor_tensor(out=stg[i][:], in0=ps, in1=stg[i][:], op=ALU.add),
                              reads=[pbuf, b_stg[i]], writes=[b_stg[i]])
                    S.dma("sp", lambda e: e.dma_start(out=dst, in_=stg[i][:]), reads=[b_stg[i]], writes=[b_ms[tt]])
                linear_B(Bb, B_all, w2[fb * 2048:(fb + 1) * 2048, :], KC, evac2)

        def ple(li):
            pT = R2[:, 0:4096].rearrange("p (k t) -> p k t", k=2)
            b_pT = Buf("pT")
            region_switch(R2_all, [b_pT])
            for tt in range(NT):
                i = next_stg()
                S.dma("sp", lambda e, i=i, tt=tt: e.dma_start(out=stg[i][:, 0:256], in_=p_in[li, tt * 128:(tt + 1) * 128, :]),
                      writes=[b_stg[i]])
                S.act(lambda e, i=i: e.activation(out=xnb[:, 0:256], in_=stg[i][:, 0:256], func=AF.Copy),
                      reads=[b_stg[i]], writes=[b_xnb])
                for k in range(2):
                    S.pe(lambda e, k=k: e.transpose(out=bank_bf(4)[:, k * 128:(k + 1) * 128], in_=xnb[:, k * 128:(k + 1) * 128],
                                                    identity=ident[:]), reads=[b_xnb, b_ident], writes=[PB[4]])
                S.dve(lambda e, tt=tt: e.tensor_copy(out=pT[:, :, tt * 128:(tt + 1) * 128],
                                                     in_=bank_bf(4)[:, 0:256].rearrange("p (k t) -> p k t", k=2)),
                      reads=[PB[4]], writes=[b_pT])
            wg = Wd["ple_w_gate"][li]
            wp = Wd["ple_w_proj"][li]

            def extra(cb):
                return load_w(wp[:, cb * 512:(cb + 1) * 512], 2)

            def evac(ps, pbuf, tt, cb, wpi):
                S.pe(lambda e: e.matmul(bank(6), lhsT=pT[:, 0, tt * 128:(tt + 1) * 128], rhs=WS[wpi][:, 0, 0:512], start=True, stop=False),
                     reads=[b_pT, WSb[wpi]], writes=[PB[6]])
                S.pe(lambda e: e.matmul(bank(6), lhsT=pT[:, 1, tt * 128:(tt + 1) * 128], rhs=WS[wpi][:, 1, 0:512], start=False, stop=True),
                     reads=[b_pT, WSb[wpi]], writes=[PB[6]])
                i = next_stg()
                S.act(lambda e: e.activation(out=stg[i][:], in_=ps, func=AF.Sigmoid), reads=[pbuf], writes=[b_stg[i]])
                S.dve(lambda e: e.tensor_tensor(out=stg[i][:], in0=bank(6), in1=stg[i][:], op=ALU.mult),
                      reads=[PB[6], b_stg[i]], writes=[b_stg[i]])
                S.dma("sp", lambda e: e.dma_start(out=ms[tt * 128:(tt + 1) * 128, cb * 512:(cb + 1) * 512], in_=stg[i][:]),
                      reads=[b_stg[i]], writes=[b_ms[tt]])
            N = D
            for cb in range(N // 512):
                wpi = extra(cb)
                wi = load_w(wg[:, cb * 512:(cb + 1) * 512], KC)
                for tt in range(NT):
                    b = next_pb()
                    for k in range(KC):
                        S.pe(lambda e, b=b, wi=wi, k=k, tt=tt: e.matmul(
                            bank(b), lhsT=A[:, k, tt * 128:(tt + 1) * 128], rhs=WS[wi][:, k, 0:512],
                            start=(k == 0), stop=(k == KC - 1)), reads=[WSb[wi], Abuf[tt]], writes=[PB[b]])
                    evac(bank(b), PB[b], tt, cb, wpi)

        def even_mixer():
            cosT = R2f[0:32, 0:2048]
            sinT = R2f[0:32, 2048:4096]
            ang = R2f[0:32, 4096:6144]
            tmp = R2f[0:32, 6144:8192]
            tmp2 = R2f[0:32, 8192:10240]
            posi = R2f[0:32, 10240:12288].bitcast(I32)
            frow = R2f[0:1, 15360:15360 + 32]
            one2 = R2f[0:1, 15392:15394]
            b_rope = Buf("rope")
            region_switch(R2_all, [b_rope])
            S.dma("sp", lambda e: e.dma_start(out=posi, in_=pos_in.partition_broadcast(32)), writes=[b_rope])
            S.dve(lambda e: e.tensor_copy(out=ang, in_=posi), reads=[b_rope], writes=[b_rope])
            inv = (np.float32(ROPE_THETA) ** (-(np.arange(0, ROT, 2, dtype=np.float32)) / np.float32(ROT))).astype(np.float32)
            for i in range(16):
                for hh in range(2):
                    c = hh * 16 + i
                    S.pool(lambda e, c=c, v=float(inv[i]): e.memset(frow[:, c:c + 1], v), writes=[b_rope])
            S.pool(lambda e: e.memset(one2, 1.0), writes=[b_rope])
            S.pe(lambda e: e.matmul(PS[0:32, 3584:3586], lhsT=frow, rhs=one2, start=True, stop=True), reads=[b_rope], writes=[PB[7]])
            S.dve(lambda e: e.tensor_copy(out=sm[0:32, SM_TMP:SM_TMP + 1], in_=PS[0:32, 3584:3585]), reads=[PB[7]], writes=[b_sm])
            S.dve(lambda e: e.tensor_scalar(out=ang, in0=ang, scalar1=sm[0:32, SM_TMP:SM_TMP + 1], scalar2=None, op0=ALU.mult),
                  reads=[b_rope, b_sm], writes=[b_rope])
            S.dve(lambda e: e.tensor_scalar(out=tmp, in0=ang, scalar1=1.0 / TWO_PI, scalar2=MAGIC, op0=ALU.mult, op1=ALU.add),
                  reads=[b_rope], writes=[b_rope])
            S.dve(lambda e: e.tensor_scalar(out=tmp, in0=tmp, scalar1=-MAGIC, scalar2=None, op0=ALU.add), reads=[b_rope], writes=[b_rope])
            for cst in (C1, C2, C3):
                S.dve(lambda e, cst=cst: e.scalar_tensor_tensor(out=ang, in0=tmp, scalar=-cst, in1=ang, op0=ALU.mult, op1=ALU.add),
                      reads=[b_rope], writes=[b_rope])
            S.dve(lambda e: e.tensor_scalar(out=ang, in0=ang, scalar1=3.1415925, scalar2=-3.1415925, op0=ALU.min, op1=ALU.max),
                  reads=[b_rope], writes=[b_rope])
            S.act(lambda e: e.activation(out=sinT, in_=ang, func=AF.Sin), reads=[b_rope], writes=[b_rope])
            S.dve(lambda e: e.tensor_scalar(out=tmp2, in0=ang, scalar1=-1.0, scalar2=None, op0=ALU.mult), reads=[b_rope], writes=[b_rope])
            S.dve(lambda e: e.tensor_tensor(out=tmp2, in0=tmp2, in1=ang, op=ALU.max), reads=[b_rope], writes=[b_rope])
            S.dve(lambda e: e.tensor_scalar(out=tmp2, in0=tmp2, scalar1=-1.0, scalar2=math.pi / 2, op0=ALU.mult, op1=ALU.add),
                  reads=[b_rope], writes=[b_rope])
            S.act(lambda e: e.activation(out=cosT, in_=tmp2, func=AF.Sin), reads=[b_rope], writes=[b_rope])
            S.pool(lambda e: e.memset(pmat[:], 0.0), writes=[b_pmat])
            S.pool(lambda e: e.affine_select(out=pmat[:, 0:16], in_=pmat[:, 0:16], pattern=[[-1, 16]], compare_op=ALU.not_equal,
                                             fill=-1.0, base=-16, channel_multiplier=1), reads=[b_pmat], writes=[b_pmat])
            S.pool(lambda e: e.affine_select(out=pmat[:, 16:32], in_=pmat[:, 16:32], pattern=[[-1, 16]], compare_op=ALU.not_equal,
                                             fill=1.0, base=0, channel_multiplier=1), reads=[b_pmat], writes=[b_pmat])
            w_in = Wd["ev_w_in"]
            ostg = [R2[:, 24576 + i * 2048: 24576 + (i + 1) * 2048] for i in range(2)]
            b_ostg = [Buf("ostg0"), Buf("ostg1")]
            region_switch(R2_all, b_ostg)
            cnt = {"o": 0}

            def mk_evacA(dst_scr, dst_name, scale, rope):
                def evac(ps, pbuf, nchg, tb):
                    oi = (cnt["o"] // 4) % 2
                    cnt["o"] += 1
                    od = ostg[oi][:, tb * 512:(tb + 1) * 512]
                    if not rope:
                        S.act(lambda e: e.activation(out=od, in_=ps, func=AF.Copy, scale=scale), reads=[pbuf], writes=[b_ostg[oi]])
                    else:
                        i = next_stg()
                        S.act(lambda e: e.activation(out=stg[i][:], in_=ps, func=AF.Copy, scale=scale), reads=[pbuf], writes=[b_stg[i]])
                        S.pe(lambda e: e.matmul(PS[0:32, 3072:3584], lhsT=pmat[:], rhs=stg[i][:], start=True, stop=True),
                             reads=[b_pmat, b_stg[i]], writes=[PB[6]])
                        S.act(lambda e: e.activation(out=od, in_=stg[i][:], func=AF.Copy),
                              reads=[b_stg[i]], writes=[b_ostg[oi]])
                        S.dve(lambda e: e.tensor_tensor(out=tmp[:, 0:512], in0=PS[0:32, 3072:3584], in1=sinT[:, tb * 512:(tb + 1) * 512],
                                                        op=ALU.mult), reads=[PB[6], b_rope], writes=[b_rope])
                        S.dve(lambda e: e.tensor_tensor(out=tmp2[:, 0:512], in0=stg[i][0:32, :], in1=cosT[:, tb * 512:(tb + 1) * 512],
                                                        op=ALU.mult), reads=[b_stg[i], b_rope], writes=[b_rope])
                        S.dve(lambda e: e.tensor_tensor(out=od[0:32, :], in0=tmp[:, 0:512], in1=tmp2[:, 0:512], op=ALU.add),
                              reads=[b_rope], writes=[b_ostg[oi]])
                    if tb == 3:
                        S.dma("sp", lambda e: e.dma_start(out=dst_scr[nchg * 128:(nchg + 1) * 128, :], in_=ostg[oi]),
                              reads=[b_ostg[oi]], writes=[b_scr[dst_name]])
                return evac

            def mk_evacB(dst_scr, dst_name):
                def evac(ps, pbuf, tt, cb, ex=None):
                    i = next_stg()
                    sv = stg[i][:].bitcast(BF16)[:, 0:512]
                    S.act(lambda e: e.activation(out=sv, in_=ps, func=AF.Copy), reads=[pbuf], writes=[b_stg[i]])
                    S.dma("sp", lambda e: e.dma_start(out=dst_scr[tt * 128:(tt + 1) * 128, cb * 512:(cb + 1) * 512], in_=sv),
                          reads=[b_stg[i]], writes=[b_scr[dst_name]])
                return evac

            linear_A(A, A_tb, w_in[:, 0:1024], KC, mk_evacA(sbqT, "sbqT", SCALE, False))
            linear_A(A, A_tb, w_in[:, 1024:2048], KC, mk_evacA(sbkT, "sbkT", 1.0, False))
            linear_B(A, A_tt, w_in[:, 2048:3072], KC, mk_evacB(sbv, "sbv"))
            linear_A(A, A_tb, w_in[:, 3072:4096], KC, mk_evacA(dfqT, "dfqT", SCALE, True))
            linear_A(A, A_tb, w_in[:, 4096:5120], KC, mk_evacA(dfkT, "dfkT", 1.0, True))
            linear_B(A, A_tt, w_in[:, 5120:6144], KC, mk_evacB(dfv, "dfv"))
            if stop == "qkv":
                return
            prefetch_w(Wd["ev_w_out"][:, 0:512], KC)
            prefetch_w(Wd["ev_w_out"][:, 512:1024], KC)
            attention()

        def attention():
            ef = [R1f[:, i * 6144:i * 6144 + 2048] for i in range(2)]
            spf = [R1f[:, i * 6144 + 2048:i * 6144 + 4096] for i in range(2)]
            csf = [R1f[:, i * 6144 + 4096:i * 6144 + 6144] for i in range(2)]
            wb = [R1[:, 24576 + i * 2048:24576 + (i + 1) * 2048] for i in range(2)]
            wTb = [R1[:, 28672 + i * 2048:28672 + (i + 1) * 2048].rearrange("p (c t) -> p c t", c=16) for i in range(2)]
            qT = [R2[:, i * 2048:(i + 1) * 2048] for i in range(4)]
            vv = R2[:, 8192:12288]
            oT = R2[:, 12288:16384].rearrange("p (j t) -> p j t", j=2)
            onbb = [R2[:, 16384 + i * 256:16384 + (i + 1) * 256] for i in range(2)]
            gsub = R2f[:, 8448:8704]
            lamt = R2f[:, 8704:9216]
            junk = R2f[:, 9216:11264]
            b_e = [Buf("e0"), Buf("e1")]
            b_sp = [Buf("sp0"), Buf("sp1")]
            b_cs = [Buf("cs0"), Buf("cs1")]
            b_w = [Buf("w0"), Buf("w1")]
            b_wT = [Buf("wT0"), Buf("wT1")]
            b_onb = [Buf("onb0"), Buf("onb1")]
            b_v, b_oT, b_gsub, b_junk, b_lam = (Buf(n) for n in ["v", "oT", "gsub", "junk", "lam"])
            b_q = [Buf(f"q{i}") for i in range(4)]
            region_switch(R2_all, b_q + b_onb + [b_v, b_oT, b_gsub, b_junk])
            PB6s = [Buf(f"pb6_{i}") for i in range(4)]
            PB7s = [Buf(f"pb7_{i}") for i in range(2)]
            for bb in PB6s:
                bb.W = list(PB[6].W); bb.R = list(PB[6].R)
            for bb in PB7s:
                bb.W = list(PB[7].W); bb.R = list(PB[7].R)
            lambda_init = 0.8 - 0.6 * math.exp(-0.3 * 0)
            (t1, b_t1), (t2, b_t2), (nlam, b_nlam) = smalloc(), smalloc(), smalloc()
            for i, nm in enumerate(["ev_lam_q1", "ev_lam_k1", "ev_lam_q2", "ev_lam_k2"]):
                S.dma("sp", lambda e, i=i, nm=nm: e.dma_start(out=lamt[:, i * 128:(i + 1) * 128], in_=Wd[nm].partition_broadcast(128)),
                      writes=[b_gsub])
            S.dma("sp", lambda e, i=i: e.dma_start(out=gsub, in_=Wd["ev_subln"].partition_broadcast(128)), writes=[b_gsub])
            S.dve(lambda e, i=i: e.tensor_tensor(out=junk[:, 0:128], in0=lamt[:, 0:128], in1=lamt[:, 128:256], op=ALU.mult),
                  reads=[b_gsub], writes=[b_junk])
            S.dve(lambda e, i=i: e.tensor_reduce(out=t1, in_=junk[:, 0:128], axis=AX.X, op=ALU.add), reads=[b_junk], writes=[b_t1])
            S.dve(lambda e, i=i: e.tensor_tensor(out=junk[:, 0:128], in0=lamt[:, 256:384], in1=lamt[:, 384:512], op=ALU.mult),
                  reads=[b_gsub], writes=[b_junk])
            S.dve(lambda e, i=i: e.tensor_reduce(out=t2, in_=junk[:, 0:128], axis=AX.X, op=ALU.add), reads=[b_junk], writes=[b_t2])
            S.act(lambda e, i=i: e.activation(out=t1, in_=t1, func=AF.Exp), reads=[b_t1], writes=[b_t1])
            S.act(lambda e, i=i: e.activation(out=t2, in_=t2, func=AF.Exp), reads=[b_t2], writes=[b_t2])
            S.dve(lambda e, i=i: e.tensor_tensor(out=nlam, in0=t2, in1=t1, op=ALU.subtract), reads=[b_t1, b_t2], writes=[b_nlam])
            S.dve(lambda e, i=i: e.tensor_scalar(out=nlam, in0=nlam, scalar1=-lambda_init, scalar2=None, op0=ALU.add),
                  reads=[b_nlam], writes=[b_nlam])
            S.dve(lambda e, i=i: e.tensor_scalar(out=gsub, in0=gsub, scalar1=(1.0 - lambda_init), scalar2=None, op0=ALU.mult),
                  reads=[b_gsub], writes=[b_gsub])

            zps = PS[:, 0:2048]
            ZB = PB[0:4]
            it = {"n": 0}

            def transposes_w(nblk, w_ap, b_w_, wT_ap, b_wT_):
                for g0 in range(0, nblk, 8):
                    bk = 4 + (g0 // 8) % 2
                    n = min(8, nblk - g0)
                    for j in range(n):
                        c = g0 + j
                        S.pe(lambda e, bk=bk, j=j, c=c: e.transpose(out=bank_bf(bk)[:, j * 128:(j + 1) * 128],
                                                                    in_=w_ap[:, c * 128:(c + 1) * 128], identity=ident[:]),
                             reads=[b_w_, b_ident], writes=[PB[bk]])
                    src = bank_bf(bk)[:, 0:n * 128].rearrange("p (c t) -> p c t", c=n)
                    if (g0 // 8) % 2 == 0:
                        S.act(lambda e, src=src, g0=g0, n=n: e.activation(out=wT_ap[:, g0:g0 + n, :], in_=src, func=AF.Copy),
                              reads=[PB[bk]], writes=[b_wT_])
                    else:
                        S.dve(lambda e, src=src, g0=g0, n=n: e.tensor_copy(out=wT_ap[:, g0:g0 + n, :], in_=src),
                              reads=[PB[bk]], writes=[b_wT_])

            def run_pipeline(iters):
                N = len(iters)
                for n in range(-2, N):
                    if 0 <= n + 2 < N:
                        iters[n + 2][0]()
                    if 0 <= n + 1 < N:
                        iters[n + 1][1]()
                    if 0 <= n < N:
                        iters[n][2]()

            vvb = [R2[:, 8192:12288], R2[:, 22528:26624]]
            b_vb = [b_v, Buf("v1")]
            qk2 = [R2[:, 26624:28672], R2[:, 28672:30720]]
            b_qk2 = [Buf("q1b"), Buf("k1b")]
            region_switch(R2_all, [b_vb[1]] + b_qk2)
            e3 = [R1f[:, j * 2048:(j + 1) * 2048] for j in range(3)]
            sp3 = [R1f[:, 6144 + j * 2048:6144 + (j + 1) * 2048] for j in range(3)]
            cs2 = [R1f[:, 12288 + j * 2048:12288 + (j + 1) * 2048] for j in range(2)]
            wS = [R2[:, 4096 + j * 2048:4096 + (j + 1) * 2048] for j in range(2)]
            wTS = [R2[:, 14336:16384].rearrange("p (c t) -> p c t", c=16), R2[:, 30720:32768].rearrange("p (c t) -> p c t", c=16)]
            b_e3 = [Buf(f"e3_{j}") for j in range(3)]
            b_sp3 = [Buf(f"sp3_{j}") for j in range(3)]
            b_cs2 = [Buf(f"cs2_{j}") for j in range(2)]
            b_wS = [b_q[2], b_q[3]]
            b_wTS = [Buf("wTS0"), Buf("wTS1")]
            region_switch(R1_all, b_e3 + b_sp3 + b_cs2)
            region_switch(R2_all, b_wTS)
            sb_sm = [smalloc() for _ in range(2)]
            sit = []
            for hh in range(8):
                hp = hh % 2
                for qt in range(NT):
                    idx = len(sit)
                    sit.append(dict(hh=hh, qt=qt, j3=idx % 3, j2=idx % 2,
                                    qh=(qT[0] if hp == 0 else qk2[0]), kh=(qT[1] if hp == 0 else qk2[1]),
                                    b_qh=(b_q[0] if hp == 0 else b_qk2[0]), b_kh=(b_q[1] if hp == 0 else b_qk2[1]),
                                    v3=vvb[hp][:, 0:2048].rearrange("p (c d) -> p c d", c=16), b_vh=b_vb[hp]))

            def s_A(d):
                hh, qt, j3, qh, kh, b_qh, b_kh, v3, b_vh = (d[k] for k in ["hh", "qt", "j3", "qh", "kh", "b_qh", "b_kh", "v3", "b_vh"])
                if qt == 0:
                    S.dma("sp", lambda e: e.dma_start(out=qh, in_=sbqT[hh * 128:(hh + 1) * 128, :]), reads=[b_scr["sbqT"]], writes=[b_qh])
                    S.dma("sp", lambda e: e.dma_start(out=kh, in_=sbkT[hh * 128:(hh + 1) * 128, :]), reads=[b_scr["sbkT"]], writes=[b_kh])
                    S.dma("sp", lambda e: e.dma_start(out=v3, in_=sbv[:, hh * 128:(hh + 1) * 128].rearrange("(c p) d -> p c d", p=128)),
                          reads=[b_scr["sbv"]], writes=[b_vh])
                e_f, sp_f = e3[j3], sp3[j3]
                Wk = (qt + 1) * 128
                tq = slice(qt * 128, (qt + 1) * 128)
                nb = (Wk + 511) // 512
                for kb in range(nb):
                    k0, k1 = kb * 512, min((kb + 1) * 512, Wk)
                    S.pe(lambda e, k0=k0, k1=k1: e.matmul(zps[:, k0:k1], lhsT=qh[:, tq], rhs=kh[:, k0:k1], start=True, stop=True),
                         reads=[b_qh, b_kh], writes=[ZB[kb]])
                zb = ZB[0:nb]
                S.act(lambda e: e.activation(out=e_f[:, 0:Wk], in_=zps[:, 0:Wk], func=AF.Exp), reads=zb, writes=[b_e3[j3]])
                S.act(lambda e: e.activation(out=sp_f[:, 0:Wk], in_=e_f[:, 0:Wk], func=AF.Ln, bias=1.0), reads=[b_e3[j3]], writes=[b_sp3[j3]])
                S.pool(lambda e: e.affine_select(out=sp_f[:, Wk - 128:Wk], in_=sp_f[:, Wk - 128:Wk], pattern=[[-1, 128]],
                                                 compare_op=ALU.is_gt, fill=0.0, base=0, channel_multiplier=1),
                       reads=[b_sp3[j3]], writes=[b_sp3[j3]])

            def s_A2(d):
                qt, j3, j2 = d["qt"], d["j3"], d["j2"]
                ntot, b_nt = sb_sm[j2]
                sp_f, cs_f = sp3[j3], cs2[j2]
                Wk = (qt + 1) * 128
                S.dve(lambda e: e.tensor_tensor_scan(out=cs_f[:, 0:Wk], data0=sp_f[:, 0:Wk], data1=sp_f[:, 0:Wk],
                                                     initial=0.0, op0=ALU.add, op1=ALU.bypass), reads=[b_sp3[j3]], writes=[b_cs2[j2]])
                S.dve(lambda e: e.tensor_scalar(out=ntot, in0=cs_f[:, Wk - 1:Wk], scalar1=-1.0, scalar2=None, op0=ALU.mult),
                      reads=[b_cs2[j2]], writes=[b_nt])

            def s_B1(d):
                qt, j3, j2 = d["qt"], d["j3"], d["j2"]
                ntot, b_nt = sb_sm[j2]
                sp_f, cs_f = sp3[j3], cs2[j2]
                Wk = (qt + 1) * 128
                S.act(lambda e: e.activation(out=sp_f[:, 0:1], in_=ntot, func=AF.Exp), reads=[b_nt, b_sp3[j3], b_cs2[j2]], writes=[b_sp3[j3]])
                S.act(lambda e: e.activation(out=sp_f[:, 1:Wk], in_=cs_f[:, 0:Wk - 1], func=AF.Exp, bias=ntot),
                      reads=[b_cs2[j2], b_nt], writes=[b_sp3[j3]])

            def s_B2(d):
                qt, j3, j2 = d["qt"], d["j3"], d["j2"]
                Wk = (qt + 1) * 128
                e_f, sp_f, w_ = e3[j3], sp3[j3], wS[j2]
                S.dve(lambda e: e.tensor_tensor(out=w_[:, 0:Wk], in0=e_f[:, 0:Wk], in1=sp_f[:, 0:Wk], op=ALU.mult),
                      reads=[b_e3[j3], b_sp3[j3]], writes=[b_wS[j2]])
                S.pool(lambda e: e.affine_select(out=w_[:, Wk - 128:Wk], in_=w_[:, Wk - 128:Wk], pattern=[[-1, 128]],
                                                 compare_op=ALU.is_gt, fill=0.0, base=0, channel_multiplier=1),
                       reads=[b_wS[j2]], writes=[b_wS[j2]])

            def s_C1(d):
                j2 = d["j2"]
                transposes_w(d["qt"] + 1, wS[j2], b_wS[j2], wTS[j2], b_wTS[j2])

            def s_C2(d):
                hh, qt, j2, v3, b_vh = d["hh"], d["qt"], d["j2"], d["v3"], d["b_vh"]
                tq = slice(qt * 128, (qt + 1) * 128)
                sl = (hh * NT + qt) % 4
                ops_ = PS[:, 3072 + sl * 128: 3072 + (sl + 1) * 128]
                for c in range(qt + 1):
                    S.pe(lambda e, c=c: e.matmul(ops_, lhsT=v3[:, c, :], rhs=wTS[j2][:, c, :], start=(c == 0), stop=(c == qt)),
                         reads=[b_vh, b_wTS[j2]], writes=[PB6s[sl]])
                S.act(lambda e: e.activation(out=oT[:, 0, tq], in_=ops_, func=AF.Copy), reads=[PB6s[sl]], writes=[b_oT])
                if qt == NT - 1:
                    S.dma("sp", lambda e: e.dma_start(out=mergedT[hh * 128:(hh + 1) * 128, :], in_=oT[:, 0, :]), reads=[b_oT],
                          writes=[b_scr["mergedT"]])

            Ns = len(sit)
            for n in range(-3, Ns):
                if 0 <= n + 1 < Ns:
                    s_B1(sit[n + 1])
                if 0 <= n < Ns:
                    s_C1(sit[n])
                if 0 <= n + 1 < Ns:
                    s_B2(sit[n + 1])
                if 0 <= n + 3 < Ns:
                    s_A(sit[n + 3])
                if 0 <= n + 2 < Ns:
                    s_A2(sit[n + 2])
                if 0 <= n < Ns:
                    s_C2(sit[n])
            if stop == "sb":
                return
            region_switch(R1_all, b_e + b_sp + b_cs + b_w + b_wT)
            b_oT.W = sorted(set(b_oT.W) | set(b_wTS[0].W)); b_oT.R = sorted(set(b_oT.R) | set(b_wTS[0].R))
            df_sm = [[smalloc() for _ in range(11)] for _ in range(2)]
            dit = []
            for hh in range(4):
                hp = hh % 2
                v3 = vvb[hp].rearrange("p (c d) -> p c d", c=16)
                b_vh = b_vb[hp]
                for qt in range(NT):
                    i = it["n"] % 2
                    it["n"] += 1
                    dit.append(dict(hh=hh, qt=qt, i=i, v3=v3, b_vh=b_vh))

            def f_z(d, m):
                hh, qt, i, v3, b_vh = d["hh"], d["qt"], d["i"], d["v3"], d["b_vh"]
                if qt == 0 and m == 0:
                    for mm in range(2):
                        S.dma("sp", lambda e, mm=mm: e.dma_start(out=qT[mm], in_=dfqT[(hh * 2 + mm) * 128:(hh * 2 + mm + 1) * 128, :]),
                              reads=[b_scr["dfqT"]], writes=[b_q[mm]])
                        S.dma("sp", lambda e, mm=mm: e.dma_start(out=qT[2 + mm], in_=dfkT[(hh * 2 + mm) * 128:(hh * 2 + mm + 1) * 128, :]),
                              reads=[b_scr["dfkT"]], writes=[b_q[2 + mm]])
                    S.dma("sp", lambda e: e.dma_start(out=v3, in_=dfv[:, hh * 256:(hh + 1) * 256].rearrange("(c p) d -> p c d", p=128)),
                          reads=[b_scr["dfv"]], writes=[b_vh])
                sms = df_sm[i]
                Wk = (qt + 1) * 128
                tq = slice(qt * 128, (qt + 1) * 128)
                nb = (Wk + 511) // 512
                zb = ZB[0:nb]
                pm = [ef[i], csf[i]][m]
                b_pm = [b_e[i], b_cs[i]][m]
                (mx, b_mx), (ll, b_ll), (lb, b_lb) = sms[m * 3], sms[m * 3 + 1], sms[m * 3 + 2]
                for kb in range(nb):
                    k0, k1 = kb * 512, min((kb + 1) * 512, Wk)
                    S.pe(lambda e, k0=k0, k1=k1: e.matmul(zps[:, k0:k1], lhsT=qT[m][:, tq], rhs=qT[2 + m][:, k0:k1], start=True, stop=True),
                         reads=[b_q[m], b_q[2 + m]], writes=[ZB[kb]])
                S.dve(lambda e: e.tensor_reduce(out=mx, in_=zps[:, 0:Wk], axis=AX.X, op=ALU.max), reads=zb, writes=[b_mx])
                S.dve(lambda e: e.tensor_scalar(out=mx, in0=mx, scalar1=-1.0, scalar2=None, op0=ALU.mult), reads=[b_mx], writes=[b_mx])
                S.pool(lambda e: e.memset(lb, 0.0), writes=[b_lb])
                S.act(lambda e: e.activation(out=pm[:, 0:Wk - 64], in_=zps[:, 0:Wk - 64], func=AF.Exp, bias=mx, accum_out=ll),
                      reads=zb + [b_mx], writes=[b_pm, b_ll])
                S.act(lambda e: e.activation(out=pm[64:128, Wk - 64:Wk], in_=zps[64:128, Wk - 64:Wk], func=AF.Exp,
                                             bias=mx[64:128, :], accum_out=lb[64:128, :]),
                      reads=zb + [b_mx, b_lb], writes=[b_pm, b_lb])
                S.pool(lambda e: e.memset(pm[0:64, Wk - 64:Wk], 0.0), reads=[b_pm], writes=[b_pm])
                if m == 1:
                    for mm in range(2):
                        (ll2, b_ll2), (lb2, b_lb2) = sms[mm * 3 + 1], sms[mm * 3 + 2]
                        S.dve(lambda e, ll2=ll2, lb2=lb2: e.tensor_tensor(out=ll2, in0=ll2, in1=lb2, op=ALU.add), reads=[b_ll2, b_lb2], writes=[b_ll2])

            def f_a2a(d):
                i = d["i"]
                sms = df_sm[i]
                Wk = (d["qt"] + 1) * 128
                (r1, b_r1), (r2, b_r2) = sms[6], sms[7]
                l1, b_l1 = sms[1]
                l2, b_l2 = sms[4]
                p1 = ef[i]
                S.dve(lambda e: e.reciprocal(out=r1, in_=l1), reads=[b_l1], writes=[b_r1])
                S.dve(lambda e: e.reciprocal(out=r2, in_=l2), reads=[b_l2], writes=[b_r2])
                S.dve(lambda e: e.tensor_tensor(out=r2, in0=r2, in1=nlam, op=ALU.mult), reads=[b_r2, b_nlam], writes=[b_r2])
                S.act(lambda e: e.activation(out=p1[:, 0:Wk], in_=p1[:, 0:Wk], func=AF.Copy, scale=r1), reads=[b_e[i], b_r1], writes=[b_e[i]])

            def f_a2b(d):
                i = d["i"]
                sms = df_sm[i]
                Wk = (d["qt"] + 1) * 128
                r2, b_r2 = sms[7]
                p1, p2 = ef[i], csf[i]
                S.dve(lambda e: e.scalar_tensor_tensor(out=wb[i][:, 0:Wk], in0=p2[:, 0:Wk], scalar=r2, in1=p1[:, 0:Wk], op0=ALU.mult, op1=ALU.add),
                      reads=[b_e[i], b_cs[i], b_r2], writes=[b_w[i]])

            def f_T(d):
                i = d["i"]
                transposes_w(d["qt"] + 1, wb[i], b_w[i], wTb[i], b_wT[i])

            def f_AV(d):
                qt, i, v3, b_vh = d["qt"], d["i"], d["v3"], d["b_vh"]
                sms = df_sm[i]
                (ss, b_ss), (ve, b_ve), (rs, b_rs) = sms[8], sms[9], sms[10]
                ops_ = PS[:, 3072 + i * 256:3072 + (i + 1) * 256]
                for c in range(qt + 1):
                    S.pe(lambda e, c=c: e.matmul(ops_, lhsT=wTb[i][:, c, :], rhs=v3[:, c, :], start=(c == 0), stop=(c == qt)),
                         reads=[b_vh, b_wT[i]], writes=[PB6s[i]])
                S.act(lambda e: e.activation(out=junk[:, i * 256:(i + 1) * 256], in_=ops_, func=AF.Square, accum_out=ss),
                      reads=[PB6s[i]], writes=[b_junk, b_ss])
                rstd_op(ss, b_ss, ve, b_ve, rs, b_rs, 256)
                S.dve(lambda e: e.scalar_tensor_tensor(out=onbb[i], in0=ops_, scalar=rs, in1=gsub, op0=ALU.mult, op1=ALU.mult),
                      reads=[PB6s[i], b_rs, b_gsub], writes=[b_onb[i]])

            def f_O(d):
                hh, qt, i = d["hh"], d["qt"], d["i"]
                tq = slice(qt * 128, (qt + 1) * 128)
                tps = bank_bf(7)[:, i * 256:(i + 1) * 256]
                for j in range(2):
                    S.pe(lambda e, j=j: e.transpose(out=tps[:, j * 128:(j + 1) * 128], in_=onbb[i][:, j * 128:(j + 1) * 128], identity=ident[:]),
                         reads=[b_onb[i], b_ident], writes=[PB7s[i]])
                S.act(lambda e: e.activation(out=oT[:, :, tq], in_=tps.rearrange("p (j t) -> p j t", j=2), func=AF.Copy),
                      reads=[PB7s[i]], writes=[b_oT])
                if qt == NT - 1:
                    S.dma("sp", lambda e: e.dma_start(
                        out=mergedT[1024 + hh * 256:1024 + (hh + 1) * 256, :].rearrange("(j p) t -> p j t", p=128), in_=oT),
                        reads=[b_oT], writes=[b_scr["mergedT"]])

            Nn = len(dit)
            for n in range(-2, Nn):
                v0 = 0 <= n < Nn
                v1 = 0 <= n + 1 < Nn
                v2 = 0 <= n + 2 < Nn
                if v1:
                    f_a2a(dit[n + 1])
                if v0:
                    f_T(dit[n])
                if v2:
                    f_z(dit[n + 2], 0)
                if v1:
                    f_a2b(dit[n + 1])
                if v0:
                    f_AV(dit[n])
                if v2:
                    f_z(dit[n + 2], 1)
                if v0:
                    f_O(dit[n])
            for bb in PB6s:
                PB[6].W = sorted(set(PB[6].W) | set(bb.W)); PB[6].R = sorted(set(PB[6].R) | set(bb.R))
            for bb in PB7s:
                PB[7].W = sorted(set(PB[7].W) | set(bb.W)); PB[7].R = sorted(set(PB[7].R) | set(bb.R))

        def load_feature_major(dst3, dbufs, src_scr, src_name):
            region_switch(R2_all, Bbuf)
            for k in range(KC):
                S.dma("sp", lambda e, k=k: e.dma_start(out=dst3[:, k, :], in_=src_scr[k * 128:(k + 1) * 128, :]),
                      reads=[b_scr[src_name]], writes=[dbufs[k]])

        def odd_mixer():
            w_in = Wd["od_w_in"]
            region_switch(R2_all, Bbuf)
            def evac_u(ps, pbuf, nchg, tb):
                S.act(lambda e: e.activation(out=Bb[:, nchg, tb * 512:(tb + 1) * 512], in_=ps, func=AF.Gelu_apprx_tanh),
                      reads=[pbuf], writes=[Bbuf[nchg]])
            linear_A(A, A_tb, w_in[:, 0:2048], KC, evac_u)

            def evac_v(ps, pbuf, tt, cb, ex=None):
                i = next_stg()
                S.act(lambda e: e.activation(out=stg[i][:], in_=ps, func=AF.Gelu_apprx_tanh), reads=[pbuf], writes=[b_stg[i]])
                S.dma("sp", lambda e: e.dma_start(out=vsc[tt * 128:(tt + 1) * 128, cb * 512:(cb + 1) * 512], in_=stg[i][:]),
                      reads=[b_stg[i]], writes=[b_scr["vsc"]])
            linear_B(A, A_tt, w_in[:, 2048:4096], KC, evac_v)
            wsn = R1f[:, 0:2048].rearrange("p (g s) -> p g s", g=16)
            wsb = R1[:, 4096:6144].rearrange("p (g s) -> p g s", g=16)
            wmT = R1[:, 6144:8192].rearrange("p (g t) -> p g t", g=16)
            bsb = R1f[:, 4096:6144].rearrange("p (g t) -> p g t", g=16)
            lng = R1f[:, 6144:8192]
            lnb = R1f[:, 8192:10240]
            vnb = R1[:, 20480:22528]
            tmpf = R1f[:, 11264:11776]
            b_ws, b_wmT, b_bsb, b_ln, b_vnb, b_tmpf = (Buf(n) for n in ["wsn", "wmT", "bsb", "ln", "vnb", "tmpf"])
            region_switch(R1_all, [b_ws, b_wmT, b_bsb, b_ln, b_vnb, b_tmpf])
            S.dma("sp", lambda e: e.dma_start(out=wsn, in_=Wd["od_w_s"].rearrange("g t s -> t g s")), writes=[b_ws])
            S.dma("sp", lambda e: e.dma_start(out=bsb.rearrange("p g t -> p (g t)"), in_=Wd["od_b_s"].partition_broadcast(128)), writes=[b_bsb])
            S.dma("sp", lambda e: e.dma_start(out=lng, in_=Wd["od_ln_g"].partition_broadcast(128)), writes=[b_ln])
            S.dma("sp", lambda e: e.dma_start(out=lnb, in_=Wd["od_ln_b"].partition_broadcast(128)), writes=[b_ln])
            S.dve(lambda e: e.tensor_copy(out=wsb, in_=wsn), reads=[b_ws], writes=[b_ws])
            S.pool(lambda e: e.memset(wsb[0:64, :, 64:128], 0.0), reads=[b_ws], writes=[b_ws])
            for half in range(2):
                for j in range(8):
                    g = half * 8 + j
                    S.pe(lambda e, half=half, j=j, g=g: e.transpose(out=bank_bf(4 + half)[:, j * 128:(j + 1) * 128], in_=wsb[:, g, :],
                                                                    identity=ident[:]), reads=[b_ws, b_ident], writes=[PB[4 + half]])
                S.dve(lambda e, half=half: e.tensor_copy(out=wmT[:, half * 8:(half + 1) * 8, :],
                                                         in_=bank_bf(4 + half).rearrange("p (g t) -> p g t", g=8)),
                      reads=[PB[4 + half]], writes=[b_wmT])
            FMAX = 512
            stats = R1f[:, 12288:12288 + 4 * 6]
            mv = R1f[:, 12320:12322]
            prefetch_w(Wd["od_w_out"][:, 0:512], KC)
            prefetch_w(Wd["od_w_out"][:, 512:1024], KC)
            vbuf = [hst, mst]
            b_vbuf = [b_hst, b_mst]
            vnbb = [R1[:, 20480:22528], R1[:, 26624:28672]]
            b_vnbb = [b_vnb, Buf("vnb1")]
            region_switch(R1_all, [b_vnbb[1]])
            sp_sm = [[smalloc() for _ in range(3)] for _ in range(2)]
            statsb = [R1f[:, 12288 + i * 32:12288 + i * 32 + 24] for i in range(2)]
            mvb = [R1f[:, 12352 + i * 4:12352 + i * 4 + 2] for i in range(2)]
            b_stats = [Buf("stats0"), Buf("stats1")]
            region_switch(R1_all, b_stats)

            def g1(n):
                i = n % 2
                vt = vbuf[i]
                stats, mv = statsb[i], mvb[i]
                (ve, b_ve), (rs, b_rs), (nmr, b_nmr) = sp_sm[i]
                rows = slice(n * 128, (n + 1) * 128)
                S.dma("sp", lambda e: e.dma_start(out=vt[:], in_=vsc[rows, :]), reads=[b_scr["vsc"]], writes=[b_vbuf[i]])
                for c in range(4):
                    S.dve(lambda e, c=c: e.bn_stats(out=stats[:, c * 6:(c + 1) * 6], in_=vt[:, c * 512:(c + 1) * 512]), reads=[b_vbuf[i]], writes=[b_stats[i]])
                S.dve(lambda e: e.bn_aggr(out=mv, in_=stats), reads=[b_stats[i]], writes=[b_stats[i]])
                S.dve(lambda e: e.tensor_scalar(out=ve, in0=mv[:, 1:2], scalar1=EPS, scalar2=None, op0=ALU.add), reads=[b_stats[i]], writes=[b_ve])
                S.pool(lambda e: e.tensor_tensor(out=rs, in0=ve, in1=negh, op=ALU.pow), reads=[b_ve, b_negh], writes=[b_rs])
                S.dve(lambda e: e.scalar_tensor_tensor(out=nmr, in0=mv[:, 0:1], scalar=-1.0, in1=rs, op0=ALU.mult, op1=ALU.mult),
                      reads=[b_stats[i], b_rs], writes=[b_nmr])
                S.act(lambda e: e.activation(out=vt[:], in_=vt[:], func=AF.Identity, scale=rs, bias=nmr), reads=[b_vbuf[i], b_rs, b_nmr], writes=[b_vbuf[i]])
                S.pool(lambda e: e.tensor_tensor(out=vt[:], in0=vt[:], in1=lng, op=ALU.mult), reads=[b_vbuf[i], b_ln], writes=[b_vbuf[i]])
                S.dve(lambda e: e.tensor_tensor(out=vnbb[i], in0=vt[:], in1=lnb, op=ALU.add), reads=[b_vbuf[i], b_ln], writes=[b_vnbb[i]])

            def g2(n):
                i = n % 2
                vn = vnbb[i]
                for g4 in range(4):
                    b = next_pb()
                    for j in range(4):
                        g = g4 * 4 + j
                        S.pe(lambda e, b=b, j=j, g=g: e.matmul(bank(b)[:, j * 128:(j + 1) * 128], lhsT=vn[:, g * 128:(g + 1) * 128], rhs=wmT[:, g, :],
                                                               start=True, stop=True), reads=[b_vnbb[i], b_wmT], writes=[PB[b]])
                    tf = R1f[:, 11264 + (g4 % 2) * 512:11264 + (g4 % 2 + 1) * 512]
                    b_tf = [b_tmpf, b_tmpf2][g4 % 2]
                    S.dve(lambda e, b=b, g4=g4, tf=tf: e.tensor_tensor(out=tf.rearrange("p (g t) -> p g t", g=4),
                                                                       in0=bank(b).rearrange("p (g t) -> p g t", g=4),
                                                                       in1=bsb[:, g4 * 4:(g4 + 1) * 4, :], op=ALU.add),
                          reads=[PB[b], b_bsb], writes=[b_tf])
                    yv = Bb[:, g4 * 4:(g4 + 1) * 4, n * 128:(n + 1) * 128]
                    eng = S.pool if g4 % 2 == 0 else S.dve
                    eng(lambda e, yv=yv, tf=tf: e.tensor_tensor(out=yv, in0=yv, in1=tf.rearrange("p (g t) -> p g t", g=4), op=ALU.mult),
                        reads=[b_tf] + Bbuf[g4 * 4:(g4 + 1) * 4], writes=Bbuf[g4 * 4:(g4 + 1) * 4])

            b_tmpf2 = Buf("tmpf2")
            region_switch(R1_all, [b_tmpf2])
            for n in range(-1, NT):
                if n + 1 < NT:
                    g1(n + 1)
                if n >= 0:
                    g2(n)
            linear_B(Bb, B_all, Wd["od_w_out"], KC, evac_to_ms)

        prefetch_w(Wd["ev_w_in"][:, 0:1024][:, 0:512], KC)
        prefetch_w(Wd["ev_w_in"][:, 0:1024][:, 512:1024], KC)
        upd_pass(x_in, None, None, None, "norm", GC_EV)
        even_mixer()
        if stop in ("qkv", "sb"):
            pass
        else:
            load_feature_major(Bb, Bbuf, mergedT, "mergedT")
            linear_B(Bb, B_all, Wd["ev_w_out"], KC, evac_to_ms)
            if stop == "mix0":
                upd_pass(x_in, ms, Wd["ev_norm_post"], out, None, None, final=True)
            else:
                prefetch_w(Wd["ffn_w1"][0][:, 0:2048][:, 0:512], KC)
                prefetch_w(Wd["ffn_w1"][0][:, 0:2048][:, 512:1024], KC)
                upd_pass(x_in, ms, Wd["ev_norm_post"], hs, "norm", GC_F0)
                ffn(0)
                if stop == "ffn0":
                    upd_pass(hs, ms, Wd["ffn_norm_post"][0:1, :], out, None, None, final=True)
                else:
                    prefetch_w(Wd["ple_w_proj"][0][:, 0:512], 2)
                    prefetch_w(Wd["ple_w_gate"][0][:, 0:512], KC)
                    upd_pass(hs, ms, Wd["ffn_norm_post"][0:1, :], hs, "raw", None)
                    ple(0)
                    if stop == "l0":
                        upd_pass(hs, ms, Wd["ple_norm"][0:1, :], out, None, None, final=True)
                    else:
                        prefetch_w(Wd["od_w_in"][:, 0:2048][:, 0:512], KC)
                        prefetch_w(Wd["od_w_in"][:, 0:2048][:, 512:1024], KC)
                        upd_pass(hs, ms, Wd["ple_norm"][0:1, :], hs, "norm", GC_OD)
                        odd_mixer()
                        if stop == "mix1":
                            upd_pass(hs, ms, Wd["od_norm_post"], out, None, None, final=True)
                        else:
                            prefetch_w(Wd["ffn_w1"][1][:, 0:2048][:, 0:512], KC)
                            prefetch_w(Wd["ffn_w1"][1][:, 0:2048][:, 512:1024], KC)
                            upd_pass(hs, ms, Wd["od_norm_post"], hs, "norm", GC_F1)
                            ffn(1)
                            prefetch_w(Wd["ple_w_proj"][1][:, 0:512], 2)
                            prefetch_w(Wd["ple_w_gate"][1][:, 0:512], KC)
                            upd_pass(hs, ms, Wd["ffn_norm_post"][1:2, :], hs, "raw", None)
                            ple(1)
                            upd_pass(hs, ms, Wd["ple_norm"][1:2, :], out, None, None, final=True)
        if stop in ("qkv", "sb"):
            o = S.dma("sp", lambda e: e.dma_start(out=out[0:128, :], in_=hst[:]), reads=[b_hst], writes=[b_out])
            out_ops.append(o)
        assert not pref, list(pref)
        S.emit(final_wait_ops=out_ops)
    return nc


def make_in_maps(inputs):
    f = lambda a: np.ascontiguousarray(np.asarray(a))
    shared = {}
    for name, shape in W_SPECS:
        shared[name] = f(inputs[name]).reshape(shape)
    maps = []
    for b in range(8):
        m = dict(shared)
        m["x"] = f(inputs["x"][b])
        m["p"] = f(inputs["p"][:, b])
        m["positions"] = f(inputs["positions"][b:b + 1]).astype(np.int32)
        maps.append(m)
    return maps


_NC_CACHE = {}


def kernel(**inputs):
    if "nc" not in _NC_CACHE:
        _NC_CACHE["nc"] = build()
    nc = _NC_CACHE["nc"]
    maps = make_in_maps(inputs)
    res = run_bass_kernel_spmd(nc, maps, core_ids=list(range(8)))
    return np.stack([np.asarray(r["out"]) for r in res.results], axis=0).astype(np.float32)
```

```python
import math
from contextlib import ExitStack

import numpy as np
import concourse.bass as bass
import concourse.mybir as mybir
from concourse.bass_utils import run_bass_kernel_spmd

F32 = mybir.dt.float32
BF16 = mybir.dt.bfloat16
I32 = mybir.dt.int32
AF = mybir.ActivationFunctionType
ALU = mybir.AluOpType
AX = mybir.AxisListType

ENGINES = ("pe", "act", "dve", "pool", "sp")
DMA_SEMS_PER_QUEUE = 8


class Buf:
    __slots__ = ("name", "W", "R")

    def __init__(self, name):
        self.name = name
        self.W = []
        self.R = []


class _Op:
    __slots__ = ("eng", "fn", "deps", "is_dma", "idx", "signal", "ticket", "dsem")

    def __init__(self, eng, fn, is_dma, idx):
        self.eng = eng
        self.fn = fn
        self.is_dma = is_dma
        self.idx = idx
        self.deps = set()
        self.signal = False
        self.ticket = None
        self.dsem = None


class Sched:
    def __init__(self, nc):
        self.nc = nc
        self.ops = []

    def op(self, eng, fn, reads=(), writes=(), dma=False):
        o = _Op(eng, fn, dma, len(self.ops))
        deps = set()
        for b in reads:
            deps.update(b.W)
        for b in writes:
            deps.update(b.W)
            deps.update(b.R)
        o.deps = deps
        for b in writes:
            b.W = [o.idx]
            b.R = []
        for b in reads:
            if b in writes:
                continue
            if not dma:
                b.R = [i for i in b.R if not (self.ops[i].eng == eng and not self.ops[i].is_dma)]
            b.R.append(o.idx)
        self.ops.append(o)
        return o

    def pe(self, fn, reads=(), writes=()):
        return self.op("pe", fn, reads, writes)

    def act(self, fn, reads=(), writes=()):
        return self.op("act", fn, reads, writes)

    def dve(self, fn, reads=(), writes=()):
        return self.op("dve", fn, reads, writes)

    def pool(self, fn, reads=(), writes=()):
        return self.op("pool", fn, reads, writes)

    def dma(self, eng, fn, reads=(), writes=()):
        return self.op(eng, fn, reads, writes, dma=True)

    def emit(self, final_wait_ops=()):
        nc = self.nc
        ops = self.ops
        for o in ops:
            for d in o.deps:
                p = ops[d]
                if p.is_dma:
                    continue
                if p.eng != o.eng or o.is_dma or o.eng != "pe":
                    p.signal = True
        with ExitStack() as es:
            esem = {e: es.enter_context(nc.semaphore("s_" + e)) for e in ENGINES}
            dsems = {}
            for e in ("sp", "act", "pool"):
                dsems[e] = [es.enter_context(nc.semaphore(f"d_{e}_{i}")) for i in range(DMA_SEMS_PER_QUEUE)]
            cnt = {e: 0 for e in ENGINES}
            dcnt = {e: 0 for e in dsems}
            dtot = {e: [0] * DMA_SEMS_PER_QUEUE for e in dsems}
            for o in ops:
                if o.is_dma:
                    j = dcnt[o.eng] % DMA_SEMS_PER_QUEUE
                    dcnt[o.eng] += 1
                    o.dsem = (o.eng, j, dtot[o.eng][j])
                    dtot[o.eng][j] += 16
                    o.ticket = (dsems[o.eng][j], dtot[o.eng][j], ("d", o.eng, j))
                elif o.signal:
                    cnt[o.eng] += 1
                    o.ticket = (esem[o.eng], cnt[o.eng], ("e", o.eng))
            self.max_ticket = dict(cnt)
            per_eng = {e: [o for o in ops if o.eng == e] for e in ENGINES}
            final = list(final_wait_ops)
            blk = es.enter_context(nc.Block())

            def run(e, eng):
                seen = {}
                for o in per_eng[e]:
                    waits = {}
                    for d in o.deps:
                        p = ops[d]
                        if p.ticket is None:
                            continue
                        if (not p.is_dma) and (not o.is_dma) and p.eng == e and e == "pe":
                            continue
                        sem, val, key = p.ticket
                        if seen.get(key, 0) >= val:
                            continue
                        if key not in waits or waits[key][1] < val:
                            waits[key] = (sem, val)
                    if o.is_dma:
                        qe, j, prev = o.dsem
                        key = ("d", qe, j)
                        if prev > 0 and seen.get(key, 0) < prev:
                            if key not in waits or waits[key][1] < prev:
                                waits[key] = (dsems[qe][j], prev)
                    for key, (sem, val) in waits.items():
                        eng.wait_ge(sem, val)
                        seen[key] = val
                    ins = o.fn(eng)
                    if o.ticket is not None:
                        sem, val, key = o.ticket
                        ins.then_inc(sem, 16 if o.is_dma else 1)
                if e == "sp":
                    for o in final:
                        sem, val, key = o.ticket
                        eng.wait_ge(sem, val)

            @blk.tensor
            def _(eng):
                run("pe", eng)

            @blk.scalar
            def _(eng):
                run("act", eng)

            @blk.vector
            def _(eng):
                run("dve", eng)

            @blk.gpsimd
            def _(eng):
                run("pool", eng)

            @blk.sync
            def _(eng):
                run("sp", eng)


D = 2048
T = 2048
NT = 16
KC = 16
HD = 128
DFF = 8192
PLE = 256
EPS = 1e-6
ROPE_THETA = 500000.0
ROT = 32
SCALE = HD ** -0.5
MAGIC = 12582912.0
TWO_PI = 2.0 * math.pi
C1 = 6.28125
C2 = 1015.0 / 524288.0
C3 = TWO_PI - C1 - C2

W_SPECS = [
    ("ev_norm_pre", [1, D]), ("ev_w_in", [D, 6144]),
    ("ev_lam_q1", [1, HD]), ("ev_lam_k1", [1, HD]), ("ev_lam_q2", [1, HD]), ("ev_lam_k2", [1, HD]),
    ("ev_subln", [1, 256]), ("ev_w_out", [D, D]), ("ev_norm_post", [1, D]),
    ("od_norm_pre", [1, D]), ("od_w_in", [D, 4096]), ("od_ln_g", [1, D]), ("od_ln_b", [1, D]),
    ("od_w_s", [16, 128, 128]), ("od_b_s", [1, 2048]), ("od_w_out", [D, D]), ("od_norm_post", [1, D]),
    ("ffn_norm_pre", [2, D]), ("ffn_w1", [2, D, DFF]), ("ffn_w2", [2, DFF, D]), ("ffn_norm_post", [2, D]),
    ("ple_w_proj", [2, PLE, D]), ("ple_w_gate", [2, D, D]), ("ple_norm", [2, D]),
]


def build(stop=None, debug=False):
    nc = bass.Bass("TRN2", target_bir_lowering=False)
    x_in = nc.dram_tensor("x", [T, D], F32, kind="ExternalInput").ap()
    p_in = nc.dram_tensor("p", [2, T, PLE], F32, kind="ExternalInput").ap()
    pos_in = nc.dram_tensor("positions", [1, T], I32, kind="ExternalInput").ap()
    Wd = {}
    for name, shape in W_SPECS:
        Wd[name] = nc.dram_tensor(name, shape, F32, kind="ExternalInput").ap()
    out = nc.dram_tensor("out", [T, D], F32, kind="ExternalOutput").ap()
    dk = "ExternalOutput" if debug else "Internal"
    hs = nc.dram_tensor("hs", [T, D], F32, kind=dk).ap()
    ms = nc.dram_tensor("ms", [T, D], F32, kind=dk).ap()
    sbqT = nc.dram_tensor("sbqT", [1024, T], BF16, kind=dk).ap()
    sbkT = nc.dram_tensor("sbkT", [1024, T], BF16, kind=dk).ap()
    sbv = nc.dram_tensor("sbv", [T, 1024], BF16, kind=dk).ap()
    dfqT = nc.dram_tensor("dfqT", [1024, T], BF16, kind=dk).ap()
    dfkT = nc.dram_tensor("dfkT", [1024, T], BF16, kind=dk).ap()
    dfv = nc.dram_tensor("dfv", [T, 1024], BF16, kind=dk).ap()
    mergedT = nc.dram_tensor("mergedT", [D, T], BF16, kind=dk).ap()
    vsc = nc.dram_tensor("vsc", [T, D], F32, kind=dk).ap()

    es = ExitStack()
    with es:
        def sb(name, shape, dt):
            return es.enter_context(nc.sbuf_tensor(name, shape, dt))

        R1 = sb("R1", [128, 32768], BF16)
        R2 = sb("R2", [128, 32768], BF16)
        WS = [sb(f"ws{i}", [128, 16, 512], BF16) for i in range(2)]
        hst = sb("hst", [128, D], F32)
        mst = sb("mst", [128, D], F32)
        gpb = sb("gpb", [128, D], F32)
        xnb = sb("xnb", [128, D], BF16)
        stg = [sb(f"stg{i}", [128, 512], F32) for i in range(4)]
        gcols = sb("gcols", [128, 16 * 5], F32)
        ident = sb("ident", [128, 128], BF16)
        identf = sb("identf", [128, 128], F32)
        sm = sb("sm", [128, 128], F32)
        pmat = sb("pmat", [128, 32], F32)
        g16 = sb("g16", [16, 5 * 128], F32)
        PS = es.enter_context(nc.psum_tensor("PS", [128, 4096], F32))

        S = Sched(nc)
        A = R1[:].rearrange("p (k t) -> p k t", k=16)
        Bb = R2[:].rearrange("p (k t) -> p k t", k=16)
        R1f = R1[:].bitcast(F32)
        R2f = R2[:].bitcast(F32)

        def bank(b):
            return PS[:, b * 512:(b + 1) * 512]

        def bank_bf(b):
            return PS[:, b * 512:(b + 1) * 512].bitcast(BF16)

        PB = [Buf(f"pb{b}") for b in range(8)]
        Abuf = [Buf(f"A{t}") for t in range(NT)]
        Bbuf = [Buf(f"B{k}") for k in range(KC)]
        WSb = [Buf("ws0"), Buf("ws1")]
        b_hst, b_mst, b_gpb, b_xnb = Buf("hst"), Buf("mst"), Buf("gpb"), Buf("xnb")
        b_stg = [Buf(f"stg{i}") for i in range(4)]
        b_gcols, b_ident, b_sm, b_pmat, b_g16 = Buf("gcols"), Buf("ident"), Buf("sm"), Buf("pmat"), Buf("g16")
        b_hs = [Buf(f"hs{t}") for t in range(NT)]
        b_ms = [Buf(f"ms{t}") for t in range(NT)]
        b_out = Buf("out")
        b_R1 = Buf("R1")
        b_R2 = Buf("R2")
        b_scr = {n: Buf(n) for n in ["sbqT", "sbkT", "sbv", "dfqT", "dfkT", "dfv", "mergedT", "vsc"]}
        out_ops = []
        st = {"stg": 0, "ws": 0, "pb": 0, "sm": 20}
        R1_all = set(Abuf)
        R2_all = set(Bbuf)

        def region_switch(region_all, new_bufs):
            Wu, Ru = set(), set()
            for b in region_all:
                Wu.update(b.W)
                Ru.update(b.R)
            for b in new_bufs:
                b.W = sorted(set(b.W) | Wu)
                b.R = sorted(set(b.R) | Ru)
            region_all.update(new_bufs)

        SM_SS, SM_VE, SM_RSTD, SM_NEGH, SM_EPS, SM_SS2, SM_RSTD2 = 0, 1, 2, 3, 4, 5, 6
        SM_MX, SM_L1, SM_L2, SM_LB, SM_R1, SM_R2, SM_NLAM, SM_TMP, SM_TMP2 = 8, 9, 10, 11, 12, 13, 14, 15, 16
        SM_NTOT = 17

        def smc(c, n=128):
            return sm[0:n, c:c + 1]

        S.pool(lambda e: e.memset(ident[:], 1.0), writes=[b_ident])
        S.pool(lambda e: e.affine_select(out=ident[:], in_=ident[:], pattern=[[-1, 128]], compare_op=ALU.is_equal,
                                         fill=0.0, base=0, channel_multiplier=1), reads=[b_ident], writes=[b_ident])
        S.pool(lambda e: e.memset(identf[:], 1.0), writes=[b_ident])
        S.pool(lambda e: e.affine_select(out=identf[:], in_=identf[:], pattern=[[-1, 128]], compare_op=ALU.is_equal,
                                         fill=0.0, base=0, channel_multiplier=1), reads=[b_ident], writes=[b_ident])
        S.pool(lambda e: e.memset(sm[:], 0.0), writes=[b_sm])
        S.pool(lambda e: e.memset(smc(SM_NEGH), -0.5), writes=[b_sm])
        S.pool(lambda e: e.memset(smc(SM_EPS), EPS), writes=[b_sm])
        gnames = [("ev_norm_pre", 0), ("ffn_norm_pre", 0), ("od_norm_pre", 0), ("ffn_norm_pre", 1)]
        for j, (nm, li) in enumerate(gnames):
            src = Wd[nm][li:li + 1, :].rearrange("o (k p) -> (o k) p", p=128)
            S.dma("sp", lambda e, j=j, src=src: e.dma_start(out=g16[:, j * 128:(j + 1) * 128], in_=src), writes=[b_g16])
        for j in range(len(gnames)):
            S.pe(lambda e, j=j: e.matmul(PS[:, 3584 + j * 16:3584 + (j + 1) * 16], lhsT=g16[:, j * 128:(j + 1) * 128],
                                          rhs=identf[0:16, 0:16], start=True, stop=True),
                 reads=[b_g16, b_ident], writes=[PB[7]])
        S.dve(lambda e: e.tensor_copy(out=gcols[:, 0:64], in_=PS[:, 3584:3584 + 64]), reads=[PB[7]], writes=[b_gcols])
        GC_EV, GC_F0, GC_OD, GC_F1 = 0, 1, 2, 3

        def rstd_from_ss(c_ss, c_out, n_feat, np_=128):
            S.dve(lambda e: e.tensor_scalar(out=smc(SM_VE, np_), in0=smc(c_ss, np_), scalar1=1.0 / n_feat, scalar2=EPS,
                                            op0=ALU.mult, op1=ALU.add), reads=[b_sm], writes=[b_sm])
            S.pool(lambda e: e.tensor_tensor(out=smc(c_out, np_), in0=smc(SM_VE, np_), in1=smc(SM_NEGH, np_), op=ALU.pow),
                   reads=[b_sm], writes=[b_sm])

        b_negh = Buf("negh")
        negh = sm[:, 120:121]
        S.pool(lambda e: e.memset(negh, -0.5), writes=[b_negh])
        st["sm"] = 20

        def smalloc():
            c = st["sm"]
            st["sm"] += 1
            assert c < 120
            return sm[:, c:c + 1], Buf(f"sm{c}")

        def rstd_op(ss, b_ss, ve, b_ve, rs, b_rs, n_feat):
            S.dve(lambda e: e.tensor_scalar(out=ve, in0=ss, scalar1=1.0 / n_feat, scalar2=EPS, op0=ALU.mult, op1=ALU.add),
                  reads=[b_ss], writes=[b_ve])
            S.pool(lambda e: e.tensor_tensor(out=rs, in0=ve, in1=negh, op=ALU.pow), reads=[b_ve, b_negh], writes=[b_rs])

        upd_sm = [[smalloc() for _ in range(6)] for _ in range(2)]

        def upd_pass(h_src, m_src, g_post, h_dst, a_mode, gcol_idx, final=False):
            NB3 = 3
            hstb = [R2f[:, i * 2048:(i + 1) * 2048] for i in range(NB3)]
            mstb = [R2f[:, 6144 + i * 2048:6144 + (i + 1) * 2048] for i in range(NB3)]
            xnbb = [R2[:, 24576 + i * 2048:24576 + (i + 1) * 2048] for i in range(2)]
            junkb = R2[:, 28672:30720]
            bh = [Buf(f"uh{i}") for i in range(NB3)]
            bm = [Buf(f"um{i}") for i in range(NB3)]
            bx = [Buf("ux0"), Buf("ux1")]
            bj = Buf("ujunk")
            region_switch(R2_all, bh + bm + bx + [bj])
            if m_src is not None:
                S.dma("sp", lambda e: e.dma_start(out=gpb[:], in_=g_post.partition_broadcast(128)), writes=[b_gpb])
            if a_mode is not None:
                region_switch(R1_all, Abuf)
            def s1(tt):
                i = tt % NB3
                hs_, ms_ = hstb[i], mstb[i]
                (ssm, b_ssm), (vem, b_vem), (rsm, b_rsm) = upd_sm[tt % 2][0:3]
                rows = slice(tt * 128, (tt + 1) * 128)
                rd = [b_hs[tt]] if h_src is hs else []
                S.dma("sp", lambda e: e.dma_start(out=hs_, in_=h_src[rows, :]), reads=rd, writes=[bh[i]])
                if m_src is not None:
                    S.dma("sp", lambda e: e.dma_start(out=ms_, in_=m_src[rows, :]), reads=[b_ms[tt]], writes=[bm[i]])
                    S.act(lambda e: e.activation(out=junkb, in_=ms_, func=AF.Square, accum_out=ssm),
                          reads=[bm[i]], writes=[bj, b_ssm])
                    rstd_op(ssm, b_ssm, vem, b_vem, rsm, b_rsm, D)
                    S.dve(lambda e: e.scalar_tensor_tensor(out=ms_, in0=ms_, scalar=rsm, in1=gpb[:], op0=ALU.mult, op1=ALU.mult),
                          reads=[bm[i], b_rsm, b_gpb], writes=[bm[i]])
                    S.dve(lambda e: e.tensor_tensor(out=hs_, in0=hs_, in1=ms_, op=ALU.add),
                          reads=[bh[i], bm[i]], writes=[bh[i]])
                if h_dst is not None:
                    o = S.dma("pool", lambda e: e.dma_start(out=h_dst[rows, :], in_=hs_), reads=[bh[i]],
                              writes=[b_hs[tt]] if h_dst is hs else [b_out])
                    if final:
                        out_ops.append(o)

            def s2(tt):
                if a_mode is None:
                    return
                i = tt % NB3
                ix = tt % 2
                hs_, xn_ = hstb[i], xnbb[ix]
                (ssh, b_ssh), (veh, b_veh), (rsh, b_rsh) = upd_sm[ix][3:6]
                if a_mode == "norm":
                    S.act(lambda e: e.activation(out=junkb, in_=hs_, func=AF.Square, accum_out=ssh),
                          reads=[bh[i]], writes=[bj, b_ssh])
                    rstd_op(ssh, b_ssh, veh, b_veh, rsh, b_rsh, D)
                    S.act(lambda e: e.activation(out=xn_, in_=hs_, func=AF.Copy, scale=rsh),
                          reads=[bh[i], b_rsh], writes=[bx[ix]])
                else:
                    S.act(lambda e: e.activation(out=xn_, in_=hs_, func=AF.Copy), reads=[bh[i]], writes=[bx[ix]])
                for half in range(2):
                    bk = 4 + half
                    for j in range(8):
                        k = half * 8 + j
                        S.pe(lambda e, bk=bk, j=j, k=k: e.transpose(out=bank_bf(bk)[:, j * 128:(j + 1) * 128],
                                                                    in_=xn_[:, k * 128:(k + 1) * 128], identity=ident[:]),
                             reads=[bx[ix], b_ident], writes=[PB[bk]])
                    dst = A[:, half * 8:(half + 1) * 8, tt * 128:(tt + 1) * 128]
                    src = bank_bf(bk).rearrange("p (k t) -> p k t", k=8)
                    if a_mode == "norm":
                        gc = gcols[:, gcol_idx * 16 + half * 8: gcol_idx * 16 + half * 8 + 8].unsqueeze(2).to_broadcast([128, 8, 128])
                        S.dve(lambda e, dst=dst, src=src, gc=gc: e.tensor_tensor(out=dst, in0=src, in1=gc, op=ALU.mult),
                              reads=[PB[bk], b_gcols], writes=[Abuf[tt]])
                    else:
                        S.dve(lambda e, dst=dst, src=src: e.tensor_copy(out=dst, in_=src), reads=[PB[bk]], writes=[Abuf[tt]])

            for n in range(-1, NT):
                if n + 1 < NT:
                    s1(n + 1)
                if n >= 0:
                    s2(n)

        pref = {}

        def prefetch_w(wap, kc):
            k = repr(wap)
            if k in pref:
                return
            pref[k] = load_w(wap, kc, _nopref=True)

        def load_w(wap, kc, eng="pool", _nopref=False):
            if not _nopref:
                k = repr(wap)
                if k in pref:
                    return pref.pop(k)
            i = st["ws"] % 2
            st["ws"] += 1
            ncols = wap.shape[1]
            src = wap.rearrange("(k p) n -> p k n", p=128)
            S.dma(eng, lambda e: e.dma_start(out=WS[i][:, 0:kc, 0:ncols], in_=src), writes=[WSb[i]])
            return i

        def next_pb(n=4):
            b = st["pb"] % n
            st["pb"] += 1
            return b

        def next_stg():
            i = st["stg"] % 4
            st["stg"] += 1
            return i

        def linear_A(X, Xbufs_for_tb, wap, kc, evac):
            N = wap.shape[1]
            for cb in range(N // 512):
                wi = load_w(wap[:, cb * 512:(cb + 1) * 512], kc)
                for nch in range(4):
                    for tb in range(4):
                        b = next_pb()
                        for k in range(kc):
                            S.pe(lambda e, b=b, wi=wi, k=k, nch=nch, tb=tb: e.matmul(
                                bank(b), lhsT=WS[wi][:, k, nch * 128:(nch + 1) * 128], rhs=X[:, k, tb * 512:(tb + 1) * 512],
                                start=(k == 0), stop=(k == kc - 1)),
                                 reads=[WSb[wi]] + Xbufs_for_tb(tb), writes=[PB[b]])
                        evac(bank(b), PB[b], cb * 4 + nch, tb)

        def linear_B(X, Xbufs_for_tt, wap, kc, evac, extra=None, pre=None):
            N = wap.shape[1]
            for cb in range(N // 512):
                wi = load_w(wap[:, cb * 512:(cb + 1) * 512], kc)
                ex = extra(cb) if extra is not None else None
                for tt in range(NT):
                    if pre is not None:
                        pre(tt, cb)
                    b = next_pb()
                    for k in range(kc):
                        S.pe(lambda e, b=b, wi=wi, k=k, tt=tt: e.matmul(
                            bank(b), lhsT=X[:, k, tt * 128:(tt + 1) * 128], rhs=WS[wi][:, k, 0:512],
                            start=(k == 0), stop=(k == kc - 1)),
                             reads=[WSb[wi]] + Xbufs_for_tt(tt), writes=[PB[b]])
                    evac(bank(b), PB[b], tt, cb, ex)

        def A_tb(tb):
            return Abuf[tb * 4:(tb + 1) * 4]

        def A_tt(tt):
            return [Abuf[tt]]

        def B_all(_):
            return Bbuf

        def evac_to_ms(ps, pbuf, tt, cb, ex=None):
            i = next_stg()
            S.act(lambda e: e.activation(out=stg[i][:], in_=ps, func=AF.Copy), reads=[pbuf], writes=[b_stg[i]])
            S.dma("sp", lambda e: e.dma_start(out=ms[tt * 128:(tt + 1) * 128, cb * 512:(cb + 1) * 512], in_=stg[i][:]),
                  reads=[b_stg[i]], writes=[b_ms[tt]])

        def ffn(li):
            w1 = Wd["ffn_w1"][li]
            w2 = Wd["ffn_w2"][li]
            region_switch(R2_all, Bbuf)
            for fb in range(4):
                def evac1(ps, pbuf, nchg, tb):
                    i = next_stg()
                    S.act(lambda e: e.activation(out=stg[i][:], in_=ps, func=AF.Relu), reads=[pbuf], writes=[b_stg[i]])
                    S.dve(lambda e: e.scalar_tensor_tensor(out=Bb[:, nchg, tb * 512:(tb + 1) * 512], in0=ps, scalar=0.0,
                                                           in1=stg[i][:], op0=ALU.max, op1=ALU.mult),
                          reads=[pbuf, b_stg[i]], writes=[Bbuf[nchg]])
                linear_A(A, A_tb, w1[:, fb * 2048:(fb + 1) * 2048], KC, evac1)

                def ld_prev(q):
                    cb_, tt_ = q // NT, q % NT
                    i_ = q % 4
                    dst_ = ms[tt_ * 128:(tt_ + 1) * 128, cb_ * 512:(cb_ + 1) * 512]
                    S.dma("sp", lambda e: e.dma_start(out=stg[i_][:], in_=dst_), reads=[b_ms[tt_]], writes=[b_stg[i_]])

                def pre2(tt, cb, fb=fb):
                    if fb == 0:
                        return
                    q = cb * NT + tt
                    if q == 0:
                        ld_prev(0)
                    if q + 1 < 4 * NT:
                        ld_prev(q + 1)

                def evac2(ps, pbuf, tt, cb, ex=None, fb=fb):
                    dst = ms[tt * 128:(tt + 1) * 128, cb * 512:(cb + 1) * 512]
                    if fb == 0:
                        i = next_stg()
                        S.act(lambda e: e.activation(out=stg[i][:], in_=ps, func=AF.Copy), reads=[pbuf], writes=[b_stg[i]])
                    else:
                        i = (cb * NT + tt) % 4
                        S.dve(lambda e: e.tensor_tensor(out=stg[i][:], in0=ps, in1=stg[i][:], op=ALU.add),
                              reads=[pbuf, b_stg[i]], writes=[b_stg[i]])
                    S.dma("sp", lambda e: e.dma_start(out=dst, in_=stg[i][:]), reads=[b_stg[i]], writes=[b_ms[tt]])
                linear_B(Bb, B_all, w2[fb * 2048:(fb + 1) * 2048, :], KC, evac2, pre=pre2)

        def ple(li):
            pT = R2[:, 0:4096].rearrange("p (k t) -> p k t", k=2)
            b_pT = Buf("pT")
            region_switch(R2_all, [b_pT])
            for tt in range(NT):
                i = next_stg()
                S.dma("sp", lambda e, i=i, tt=tt: e.dma_start(out=stg[i][:, 0:256], in_=p_in[li, tt * 128:(tt + 1) * 128, :]),
                      writes=[b_stg[i]])
                S.act(lambda e, i=i: e.activation(out=xnb[:, 0:256], in_=stg[i][:, 0:256], func=AF.Copy),
                      reads=[b_stg[i]], writes=[b_xnb])
                for k in range(2):
                    S.pe(lambda e, k=k: e.transpose(out=bank_bf(4)[:, k * 128:(k + 1) * 128], in_=xnb[:, k * 128:(k + 1) * 128],
                                                    identity=ident[:]), reads=[b_xnb, b_ident], writes=[PB[4]])
                S.dve(lambda e, tt=tt: e.tensor_copy(out=pT[:, :, tt * 128:(tt + 1) * 128],
                                                     in_=bank_bf(4)[:, 0:256].rearrange("p (k t) -> p k t", k=2)),
                      reads=[PB[4]], writes=[b_pT])
            wg = Wd["ple_w_gate"][li]
            wp = Wd["ple_w_proj"][li]

            def extra(cb):
                return load_w(wp[:, cb * 512:(cb + 1) * 512], 2)

            def evac(ps, pbuf, tt, cb, wpi):
                S.pe(lambda e: e.matmul(bank(6), lhsT=pT[:, 0, tt * 128:(tt + 1) * 128], rhs=WS[wpi][:, 0, 0:512], start=True, stop=False),
                     reads=[b_pT, WSb[wpi]], writes=[PB[6]])
                S.pe(lambda e: e.matmul(bank(6), lhsT=pT[:, 1, tt * 128:(tt + 1) * 128], rhs=WS[wpi][:, 1, 0:512], start=False, stop=True),
                     reads=[b_pT, WSb[wpi]], writes=[PB[6]])
                i = next_stg()
                S.act(lambda e: e.activation(out=stg[i][:], in_=ps, func=AF.Sigmoid), reads=[pbuf], writes=[b_stg[i]])
                S.dve(lambda e: e.tensor_tensor(out=stg[i][:], in0=bank(6), in1=stg[i][:], op=ALU.mult),
                      reads=[PB[6], b_stg[i]], writes=[b_stg[i]])
                S.dma("sp", lambda e: e.dma_start(out=ms[tt * 128:(tt + 1) * 128, cb * 512:(cb + 1) * 512], in_=stg[i][:]),
                      reads=[b_stg[i]], writes=[b_ms[tt]])
            N = D
            for cb in range(N // 512):
                wpi = extra(cb)
                wi = load_w(wg[:, cb * 512:(cb + 1) * 512], KC)
                for tt in range(NT):
                    b = next_pb()
                    for k in range(KC):
                        S.pe(lambda e, b=b, wi=wi, k=k, tt=tt: e.matmul(
                            bank(b), lhsT=A[:, k, tt * 128:(tt + 1) * 128], rhs=WS[wi][:, k, 0:512],
                            start=(k == 0), stop=(k == KC - 1)), reads=[WSb[wi], Abuf[tt]], writes=[PB[b]])
                    evac(bank(b), PB[b], tt, cb, wpi)

        def even_mixer():
            cosT = R2f[0:32, 0:2048]
            sinT = R2f[0:32, 2048:4096]
            ang = R2f[0:32, 4096:6144]
            tmp = R2f[0:32, 6144:8192]
            tmp2 = R2f[0:32, 8192:10240]
            posi = R2f[0:32, 10240:12288].bitcast(I32)
            frow = R2f[0:1, 15360:15360 + 32]
            one2 = R2f[0:1, 15392:15394]
            b_rope = Buf("rope")
            region_switch(R2_all, [b_rope])
            S.dma("sp", lambda e: e.dma_start(out=posi, in_=pos_in.partition_broadcast(32)), writes=[b_rope])
            S.dve(lambda e: e.tensor_copy(out=ang, in_=posi), reads=[b_rope], writes=[b_rope])
            inv = (np.float32(ROPE_THETA) ** (-(np.arange(0, ROT, 2, dtype=np.float32)) / np.float32(ROT))).astype(np.float32)
            for i in range(16):
                for hh in range(2):
                    c = hh * 16 + i
                    S.pool(lambda e, c=c, v=float(inv[i]): e.memset(frow[:, c:c + 1], v), writes=[b_rope])
            S.pool(lambda e: e.memset(one2, 1.0), writes=[b_rope])
            S.pe(lambda e: e.matmul(PS[0:32, 3584:3586], lhsT=frow, rhs=one2, start=True, stop=True), reads=[b_rope], writes=[PB[7]])
            S.dve(lambda e: e.tensor_copy(out=sm[0:32, SM_TMP:SM_TMP + 1], in_=PS[0:32, 3584:3585]), reads=[PB[7]], writes=[b_sm])
            S.dve(lambda e: e.tensor_scalar(out=ang, in0=ang, scalar1=sm[0:32, SM_TMP:SM_TMP + 1], scalar2=None, op0=ALU.mult),
                  reads=[b_rope, b_sm], writes=[b_rope])
            S.dve(lambda e: e.tensor_scalar(out=tmp, in0=ang, scalar1=1.0 / TWO_PI, scalar2=MAGIC, op0=ALU.mult, op1=ALU.add),
                  reads=[b_rope], writes=[b_rope])
            S.dve(lambda e: e.tensor_scalar(out=tmp, in0=tmp, scalar1=-MAGIC, scalar2=None, op0=ALU.add), reads=[b_rope], writes=[b_rope])
            for cst in (C1, C2, C3):
                S.dve(lambda e, cst=cst: e.scalar_tensor_tensor(out=ang, in0=tmp, scalar=-cst, in1=ang, op0=ALU.mult, op1=ALU.add),
                      reads=[b_rope], writes=[b_rope])
            S.dve(lambda e: e.tensor_scalar(out=ang, in0=ang, scalar1=3.1415925, scalar2=-3.1415925, op0=ALU.min, op1=ALU.max),
                  reads=[b_rope], writes=[b_rope])
            S.act(lambda e: e.activation(out=sinT, in_=ang, func=AF.Sin), reads=[b_rope], writes=[b_rope])
            S.dve(lambda e: e.tensor_scalar(out=tmp2, in0=ang, scalar1=-1.0, scalar2=None, op0=ALU.mult), reads=[b_rope], writes=[b_rope])
            S.dve(lambda e: e.tensor_tensor(out=tmp2, in0=tmp2, in1=ang, op=ALU.max), reads=[b_rope], writes=[b_rope])
            S.dve(lambda e: e.tensor_scalar(out=tmp2, in0=tmp2, scalar1=-1.0, scalar2=math.pi / 2, op0=ALU.mult, op1=ALU.add),
                  reads=[b_rope], writes=[b_rope])
            S.act(lambda e: e.activation(out=cosT, in_=tmp2, func=AF.Sin), reads=[b_rope], writes=[b_rope])
            S.pool(lambda e: e.memset(pmat[:], 0.0), writes=[b_pmat])
            S.pool(lambda e: e.affine_select(out=pmat[:, 0:16], in_=pmat[:, 0:16], pattern=[[-1, 16]], compare_op=ALU.not_equal,
                                             fill=-1.0, base=-16, channel_multiplier=1), reads=[b_pmat], writes=[b_pmat])
            S.pool(lambda e: e.affine_select(out=pmat[:, 16:32], in_=pmat[:, 16:32], pattern=[[-1, 16]], compare_op=ALU.not_equal,
                                             fill=1.0, base=0, channel_multiplier=1), reads=[b_pmat], writes=[b_pmat])
            w_in = Wd["ev_w_in"]
            ostg = [R2[:, 24576 + i * 2048: 24576 + (i + 1) * 2048] for i in range(2)]
            b_ostg = [Buf("ostg0"), Buf("ostg1")]
            region_switch(R2_all, b_ostg)
            cnt = {"o": 0}

            def mk_evacA(dst_scr, dst_name, scale, rope):
                def evac(ps, pbuf, nchg, tb):
                    oi = (cnt["o"] // 4) % 2
                    cnt["o"] += 1
                    od = ostg[oi][:, tb * 512:(tb + 1) * 512]
                    if not rope:
                        S.act(lambda e: e.activation(out=od, in_=ps, func=AF.Copy, scale=scale), reads=[pbuf], writes=[b_ostg[oi]])
                    else:
                        i = next_stg()
                        S.act(lambda e: e.activation(out=stg[i][:], in_=ps, func=AF.Copy, scale=scale), reads=[pbuf], writes=[b_stg[i]])
                        S.pe(lambda e: e.matmul(PS[0:32, 3072:3584], lhsT=pmat[:], rhs=stg[i][:], start=True, stop=True),
                             reads=[b_pmat, b_stg[i]], writes=[PB[6]])
                        S.act(lambda e: e.activation(out=od, in_=stg[i][:], func=AF.Copy),
                              reads=[b_stg[i]], writes=[b_ostg[oi]])
                        S.dve(lambda e: e.tensor_tensor(out=tmp[:, 0:512], in0=PS[0:32, 3072:3584], in1=sinT[:, tb * 512:(tb + 1) * 512],
                                                        op=ALU.mult), reads=[PB[6], b_rope], writes=[b_rope])
                        S.dve(lambda e: e.tensor_tensor(out=tmp2[:, 0:512], in0=stg[i][0:32, :], in1=cosT[:, tb * 512:(tb + 1) * 512],
                                                        op=ALU.mult), reads=[b_stg[i], b_rope], writes=[b_rope])
                        S.dve(lambda e: e.tensor_tensor(out=od[0:32, :], in0=tmp[:, 0:512], in1=tmp2[:, 0:512], op=ALU.add),
                              reads=[b_rope], writes=[b_ostg[oi]])
                    if tb == 3:
                        S.dma("sp", lambda e: e.dma_start(out=dst_scr[nchg * 128:(nchg + 1) * 128, :], in_=ostg[oi]),
                              reads=[b_ostg[oi]], writes=[b_scr[dst_name]])
                return evac

            def mk_evacB(dst_scr, dst_name):
                def evac(ps, pbuf, tt, cb, ex=None):
                    i = next_stg()
                    sv = stg[i][:].bitcast(BF16)[:, 0:512]
                    S.act(lambda e: e.activation(out=sv, in_=ps, func=AF.Copy), reads=[pbuf], writes=[b_stg[i]])
                    S.dma("sp", lambda e: e.dma_start(out=dst_scr[tt * 128:(tt + 1) * 128, cb * 512:(cb + 1) * 512], in_=sv),
                          reads=[b_stg[i]], writes=[b_scr[dst_name]])
                return evac

            linear_A(A, A_tb, w_in[:, 0:1024], KC, mk_evacA(sbqT, "sbqT", SCALE, False))
            linear_A(A, A_tb, w_in[:, 1024:2048], KC, mk_evacA(sbkT, "sbkT", 1.0, False))
            linear_B(A, A_tt, w_in[:, 2048:3072], KC, mk_evacB(sbv, "sbv"))
            linear_A(A, A_tb, w_in[:, 3072:4096], KC, mk_evacA(dfqT, "dfqT", SCALE, True))
            linear_A(A, A_tb, w_in[:, 4096:5120], KC, mk_evacA(dfkT, "dfkT", 1.0, True))
            linear_B(A, A_tt, w_in[:, 5120:6144], KC, mk_evacB(dfv, "dfv"))
            if stop == "qkv":
                return
            prefetch_w(Wd["ev_w_out"][:, 0:512], KC)
            prefetch_w(Wd["ev_w_out"][:, 512:1024], KC)
            attention()

        def attention():
            ef = [R1f[:, i * 6144:i * 6144 + 2048] for i in range(2)]
            spf = [R1f[:, i * 6144 + 2048:i * 6144 + 4096] for i in range(2)]
            csf = [R1f[:, i * 6144 + 4096:i * 6144 + 6144] for i in range(2)]
            wb = [R1[:, 24576 + i * 2048:24576 + (i + 1) * 2048] for i in range(2)]
            wTb = [R1[:, 28672 + i * 2048:28672 + (i + 1) * 2048].rearrange("p (c t) -> p c t", c=16) for i in range(2)]
            qT = [R2[:, i * 2048:(i + 1) * 2048] for i in range(4)]
            vv = R2[:, 8192:12288]
            oT = R2[:, 12288:16384].rearrange("p (j t) -> p j t", j=2)
            onbb = [R2[:, 16384 + i * 256:16384 + (i + 1) * 256] for i in range(2)]
            gsub = R2f[:, 8448:8704]
            lamt = R2f[:, 8704:9216]
            junk = R2f[:, 9216:11264]
            b_e = [Buf("e0"), Buf("e1")]
            b_sp = [Buf("sp0"), Buf("sp1")]
            b_cs = [Buf("cs0"), Buf("cs1")]
            b_w = [Buf("w0"), Buf("w1")]
            b_wT = [Buf("wT0"), Buf("wT1")]
            b_onb = [Buf("onb0"), Buf("onb1")]
            b_v, b_oT, b_gsub, b_junk, b_lam = (Buf(n) for n in ["v", "oT", "gsub", "junk", "lam"])
            b_q = [Buf(f"q{i}") for i in range(4)]
            region_switch(R1_all, b_e + b_sp + b_cs + b_w + b_wT)
            region_switch(R2_all, b_q + b_onb + [b_v, b_oT, b_gsub, b_junk])
            PB6s = [Buf(f"pb6_{i}") for i in range(4)]
            PB7s = [Buf(f"pb7_{i}") for i in range(2)]
            for bb in PB6s:
                bb.W = list(PB[6].W); bb.R = list(PB[6].R)
            for bb in PB7s:
                bb.W = list(PB[7].W); bb.R = list(PB[7].R)
            lambda_init = 0.8 - 0.6 * math.exp(-0.3 * 0)
            (t1, b_t1), (t2, b_t2), (nlam, b_nlam) = smalloc(), smalloc(), smalloc()
            for i, nm in enumerate(["ev_lam_q1", "ev_lam_k1", "ev_lam_q2", "ev_lam_k2"]):
                S.dma("sp", lambda e, i=i, nm=nm: e.dma_start(out=lamt[:, i * 128:(i + 1) * 128], in_=Wd[nm].partition_broadcast(128)),
                      writes=[b_gsub])
            S.dma("sp", lambda e, i=i: e.dma_start(out=gsub, in_=Wd["ev_subln"].partition_broadcast(128)), writes=[b_gsub])
            S.dve(lambda e, i=i: e.tensor_tensor(out=junk[:, 0:128], in0=lamt[:, 0:128], in1=lamt[:, 128:256], op=ALU.mult),
                  reads=[b_gsub], writes=[b_junk])
            S.dve(lambda e, i=i: e.tensor_reduce(out=t1, in_=junk[:, 0:128], axis=AX.X, op=ALU.add), reads=[b_junk], writes=[b_t1])
            S.dve(lambda e, i=i: e.tensor_tensor(out=junk[:, 0:128], in0=lamt[:, 256:384], in1=lamt[:, 384:512], op=ALU.mult),
                  reads=[b_gsub], writes=[b_junk])
            S.dve(lambda e, i=i: e.tensor_reduce(out=t2, in_=junk[:, 0:128], axis=AX.X, op=ALU.add), reads=[b_junk], writes=[b_t2])
            S.act(lambda e, i=i: e.activation(out=t1, in_=t1, func=AF.Exp), reads=[b_t1], writes=[b_t1])
            S.act(lambda e, i=i: e.activation(out=t2, in_=t2, func=AF.Exp), reads=[b_t2], writes=[b_t2])
            S.dve(lambda e, i=i: e.tensor_tensor(out=nlam, in0=t2, in1=t1, op=ALU.subtract), reads=[b_t1, b_t2], writes=[b_nlam])
            S.dve(lambda e, i=i: e.tensor_scalar(out=nlam, in0=nlam, scalar1=-lambda_init, scalar2=None, op0=ALU.add),
                  reads=[b_nlam], writes=[b_nlam])
            S.dve(lambda e, i=i: e.tensor_scalar(out=gsub, in0=gsub, scalar1=(1.0 - lambda_init), scalar2=None, op0=ALU.mult),
                  reads=[b_gsub], writes=[b_gsub])

            zps = PS[:, 0:2048]
            ZB = PB[0:4]
            it = {"n": 0}

            def transposes_w(nblk, i):
                for g0 in range(0, nblk, 8):
                    bk = 4 + (g0 // 8) % 2
                    n = min(8, nblk - g0)
                    for j in range(n):
                        c = g0 + j
                        S.pe(lambda e, i=i, bk=bk, j=j, c=c: e.transpose(out=bank_bf(bk)[:, j * 128:(j + 1) * 128],
                                                                    in_=wb[i][:, c * 128:(c + 1) * 128], identity=ident[:]),
                             reads=[b_w[i], b_ident], writes=[PB[bk]])
                    src = bank_bf(bk)[:, 0:n * 128].rearrange("p (c t) -> p c t", c=n)
                    if (g0 // 8) % 2 == 0:
                        S.act(lambda e, i=i, src=src, g0=g0, n=n: e.activation(out=wTb[i][:, g0:g0 + n, :], in_=src, func=AF.Copy),
                              reads=[PB[bk]], writes=[b_wT[i]])
                    else:
                        S.dve(lambda e, i=i, src=src, g0=g0, n=n: e.tensor_copy(out=wTb[i][:, g0:g0 + n, :], in_=src),
                              reads=[PB[bk]], writes=[b_wT[i]])

            def run_pipeline(iters):
                N = len(iters)
                for n in range(-2, N):
                    if 0 <= n + 2 < N:
                        iters[n + 2][0]()
                    if 0 <= n + 1 < N:
                        iters[n + 1][1]()
                    if 0 <= n < N:
                        iters[n][2]()

            vvb = [R2[:, 8192:12288], R2[:, 22528:26624]]
            b_vb = [b_v, Buf("v1")]
            qk2 = [R2[:, 26624:28672], R2[:, 28672:30720]]
            b_qk2 = [Buf("q1b"), Buf("k1b")]
            region_switch(R2_all, [b_vb[1]] + b_qk2)
            sb_sm = [smalloc() for _ in range(2)]
            iters = []
            for hh in range(8):
                hp = hh % 2
                qh = qT[0] if hp == 0 else qk2[0]
                kh = qT[1] if hp == 0 else qk2[1]
                b_qh = b_q[0] if hp == 0 else b_qk2[0]
                b_kh = b_q[1] if hp == 0 else b_qk2[1]
                v3 = vvb[hp][:, 0:2048].rearrange("p (c d) -> p c d", c=16)
                b_vh = b_vb[hp]
                for qt in range(NT):
                    i = it["n"] % 2
                    it["n"] += 1

                    def a1(hh=hh, qt=qt, i=i, qh=qh, kh=kh, b_qh=b_qh, b_kh=b_kh, v3=v3, b_vh=b_vh):
                        if qt == 0:
                            S.dma("sp", lambda e: e.dma_start(out=qh, in_=sbqT[hh * 128:(hh + 1) * 128, :]), reads=[b_scr["sbqT"]], writes=[b_qh])
                            S.dma("sp", lambda e: e.dma_start(out=kh, in_=sbkT[hh * 128:(hh + 1) * 128, :]), reads=[b_scr["sbkT"]], writes=[b_kh])
                            S.dma("sp", lambda e: e.dma_start(out=v3, in_=sbv[:, hh * 128:(hh + 1) * 128].rearrange("(c p) d -> p c d", p=128)),
                                  reads=[b_scr["sbv"]], writes=[b_vh])
                        e_f, sp_f = ef[i], spf[i]
                        Wk = (qt + 1) * 128
                        tq = slice(qt * 128, (qt + 1) * 128)
                        nb = (Wk + 511) // 512
                        for kb in range(nb):
                            k0, k1 = kb * 512, min((kb + 1) * 512, Wk)
                            S.pe(lambda e, k0=k0, k1=k1: e.matmul(zps[:, k0:k1], lhsT=qh[:, tq], rhs=kh[:, k0:k1], start=True, stop=True),
                                 reads=[b_qh, b_kh], writes=[ZB[kb]])
                        zb = ZB[0:nb]
                        S.act(lambda e: e.activation(out=e_f[:, 0:Wk], in_=zps[:, 0:Wk], func=AF.Exp), reads=zb, writes=[b_e[i]])
                        S.act(lambda e: e.activation(out=sp_f[:, 0:Wk], in_=e_f[:, 0:Wk], func=AF.Ln, bias=1.0),
                              reads=[b_e[i]], writes=[b_sp[i]])
                        S.pool(lambda e: e.affine_select(out=sp_f[:, Wk - 128:Wk], in_=sp_f[:, Wk - 128:Wk], pattern=[[-1, 128]],
                                                         compare_op=ALU.is_gt, fill=0.0, base=0, channel_multiplier=1),
                               reads=[b_sp[i]], writes=[b_sp[i]])

                    def a2(hh=hh, qt=qt, i=i):
                        ntot, b_nt = sb_sm[i]
                        e_f, sp_f, cs_f = ef[i], spf[i], csf[i]
                        Wk = (qt + 1) * 128
                        S.dve(lambda e: e.tensor_tensor_scan(out=cs_f[:, 0:Wk], data0=sp_f[:, 0:Wk], data1=sp_f[:, 0:Wk],
                                                             initial=0.0, op0=ALU.add, op1=ALU.bypass),
                              reads=[b_sp[i]], writes=[b_cs[i]])
                        S.dve(lambda e: e.tensor_scalar(out=ntot, in0=cs_f[:, Wk - 1:Wk], scalar1=-1.0, scalar2=None, op0=ALU.mult),
                              reads=[b_cs[i]], writes=[b_nt])
                        S.act(lambda e: e.activation(out=sp_f[:, 0:1], in_=ntot, func=AF.Exp), reads=[b_nt, b_sp[i], b_cs[i]], writes=[b_sp[i]])
                        S.act(lambda e: e.activation(out=sp_f[:, 1:Wk], in_=cs_f[:, 0:Wk - 1], func=AF.Exp, bias=ntot),
                              reads=[b_cs[i], b_nt], writes=[b_sp[i]])
                        S.dve(lambda e: e.tensor_tensor(out=wb[i][:, 0:Wk], in0=e_f[:, 0:Wk], in1=sp_f[:, 0:Wk], op=ALU.mult),
                              reads=[b_e[i], b_sp[i]], writes=[b_w[i]])
                        S.pool(lambda e: e.affine_select(out=wb[i][:, Wk - 128:Wk], in_=wb[i][:, Wk - 128:Wk], pattern=[[-1, 128]],
                                                         compare_op=ALU.is_gt, fill=0.0, base=0, channel_multiplier=1),
                               reads=[b_w[i]], writes=[b_w[i]])

                    def bst(hh=hh, qt=qt, i=i, v3=v3, b_vh=b_vh):
                        tq = slice(qt * 128, (qt + 1) * 128)
                        transposes_w(qt + 1, i)
                        sl = (hh * NT + qt) % 4
                        ops_ = PS[:, 3072 + sl * 128: 3072 + (sl + 1) * 128]
                        for c in range(qt + 1):
                            S.pe(lambda e, c=c: e.matmul(ops_, lhsT=v3[:, c, :], rhs=wTb[i][:, c, :], start=(c == 0), stop=(c == qt)),
                                 reads=[b_vh, b_wT[i]], writes=[PB6s[sl]])
                        S.act(lambda e: e.activation(out=oT[:, 0, tq], in_=ops_, func=AF.Copy), reads=[PB6s[sl]], writes=[b_oT])
                        if qt == NT - 1:
                            S.dma("sp", lambda e: e.dma_start(out=mergedT[hh * 128:(hh + 1) * 128, :], in_=oT[:, 0, :]), reads=[b_oT],
                                  writes=[b_scr["mergedT"]])
                    iters.append((a1, a2, bst))
            run_pipeline(iters)
            if stop == "sb":
                return
            df_sm = [[smalloc() for _ in range(11)] for _ in range(2)]
            dit = []
            for hh in range(4):
                hp = hh % 2
                v3 = vvb[hp].rearrange("p (c d) -> p c d", c=16)
                b_vh = b_vb[hp]
                for qt in range(NT):
                    i = it["n"] % 2
                    it["n"] += 1
                    dit.append(dict(hh=hh, qt=qt, i=i, v3=v3, b_vh=b_vh))

            def f_z(d, m):
                hh, qt, i, v3, b_vh = d["hh"], d["qt"], d["i"], d["v3"], d["b_vh"]
                if qt == 0 and m == 0:
                    for mm in range(2):
                        S.dma("sp", lambda e, mm=mm: e.dma_start(out=qT[mm], in_=dfqT[(hh * 2 + mm) * 128:(hh * 2 + mm + 1) * 128, :]),
                              reads=[b_scr["dfqT"]], writes=[b_q[mm]])
                        S.dma("sp", lambda e, mm=mm: e.dma_start(out=qT[2 + mm], in_=dfkT[(hh * 2 + mm) * 128:(hh * 2 + mm + 1) * 128, :]),
                              reads=[b_scr["dfkT"]], writes=[b_q[2 + mm]])
                    S.dma("sp", lambda e: e.dma_start(out=v3, in_=dfv[:, hh * 256:(hh + 1) * 256].rearrange("(c p) d -> p c d", p=128)),
                          reads=[b_scr["dfv"]], writes=[b_vh])
                sms = df_sm[i]
                Wk = (qt + 1) * 128
                tq = slice(qt * 128, (qt + 1) * 128)
                nb = (Wk + 511) // 512
                zb = ZB[0:nb]
                pm = [ef[i], csf[i]][m]
                b_pm = [b_e[i], b_cs[i]][m]
                (mx, b_mx), (ll, b_ll), (lb, b_lb) = sms[m * 3], sms[m * 3 + 1], sms[m * 3 + 2]
                for kb in range(nb):
                    k0, k1 = kb * 512, min((kb + 1) * 512, Wk)
                    S.pe(lambda e, k0=k0, k1=k1: e.matmul(zps[:, k0:k1], lhsT=qT[m][:, tq], rhs=qT[2 + m][:, k0:k1], start=True, stop=True),
                         reads=[b_q[m], b_q[2 + m]], writes=[ZB[kb]])
                S.dve(lambda e: e.tensor_reduce(out=mx, in_=zps[:, 0:Wk], axis=AX.X, op=ALU.max), reads=zb, writes=[b_mx])
                S.dve(lambda e: e.tensor_scalar(out=mx, in0=mx, scalar1=-1.0, scalar2=None, op0=ALU.mult), reads=[b_mx], writes=[b_mx])
                S.pool(lambda e: e.memset(lb, 0.0), writes=[b_lb])
                S.act(lambda e: e.activation(out=pm[:, 0:Wk - 64], in_=zps[:, 0:Wk - 64], func=AF.Exp, bias=mx, accum_out=ll),
                      reads=zb + [b_mx], writes=[b_pm, b_ll])
                S.act(lambda e: e.activation(out=pm[64:128, Wk - 64:Wk], in_=zps[64:128, Wk - 64:Wk], func=AF.Exp,
                                             bias=mx[64:128, :], accum_out=lb[64:128, :]),
                      reads=zb + [b_mx, b_lb], writes=[b_pm, b_lb])
                S.pool(lambda e: e.memset(pm[0:64, Wk - 64:Wk], 0.0), reads=[b_pm], writes=[b_pm])
                if m == 1:
                    for mm in range(2):
                        (ll2, b_ll2), (lb2, b_lb2) = sms[mm * 3 + 1], sms[mm * 3 + 2]
                        S.dve(lambda e, ll2=ll2, lb2=lb2: e.tensor_tensor(out=ll2, in0=ll2, in1=lb2, op=ALU.add), reads=[b_ll2, b_lb2], writes=[b_ll2])

            def f_a2a(d):
                i = d["i"]
                sms = df_sm[i]
                Wk = (d["qt"] + 1) * 128
                (r1, b_r1), (r2, b_r2) = sms[6], sms[7]
                l1, b_l1 = sms[1]
                l2, b_l2 = sms[4]
                p1 = ef[i]
                S.dve(lambda e: e.reciprocal(out=r1, in_=l1), reads=[b_l1], writes=[b_r1])
                S.dve(lambda e: e.reciprocal(out=r2, in_=l2), reads=[b_l2], writes=[b_r2])
                S.dve(lambda e: e.tensor_tensor(out=r2, in0=r2, in1=nlam, op=ALU.mult), reads=[b_r2, b_nlam], writes=[b_r2])
                S.act(lambda e: e.activation(out=p1[:, 0:Wk], in_=p1[:, 0:Wk], func=AF.Copy, scale=r1), reads=[b_e[i], b_r1], writes=[b_e[i]])

            def f_a2b(d):
                i = d["i"]
                sms = df_sm[i]
                Wk = (d["qt"] + 1) * 128
                r2, b_r2 = sms[7]
                p1, p2 = ef[i], csf[i]
                S.dve(lambda e: e.scalar_tensor_tensor(out=wb[i][:, 0:Wk], in0=p2[:, 0:Wk], scalar=r2, in1=p1[:, 0:Wk], op0=ALU.mult, op1=ALU.add),
                      reads=[b_e[i], b_cs[i], b_r2], writes=[b_w[i]])

            def f_T(d):
                transposes_w(d["qt"] + 1, d["i"])

            def f_AV(d):
                qt, i, v3, b_vh = d["qt"], d["i"], d["v3"], d["b_vh"]
                sms = df_sm[i]
                (ss, b_ss), (ve, b_ve), (rs, b_rs) = sms[8], sms[9], sms[10]
                ops_ = PS[:, 3072 + i * 256:3072 + (i + 1) * 256]
                for c in range(qt + 1):
                    S.pe(lambda e, c=c: e.matmul(ops_, lhsT=wTb[i][:, c, :], rhs=v3[:, c, :], start=(c == 0), stop=(c == qt)),
                         reads=[b_vh, b_wT[i]], writes=[PB6s[i]])
                S.act(lambda e: e.activation(out=junk[:, i * 256:(i + 1) * 256], in_=ops_, func=AF.Square, accum_out=ss),
                      reads=[PB6s[i]], writes=[b_junk, b_ss])
                rstd_op(ss, b_ss, ve, b_ve, rs, b_rs, 256)
                S.dve(lambda e: e.scalar_tensor_tensor(out=onbb[i], in0=ops_, scalar=rs, in1=gsub, op0=ALU.mult, op1=ALU.mult),
                      reads=[PB6s[i], b_rs, b_gsub], writes=[b_onb[i]])

            def f_O(d):
                hh, qt, i = d["hh"], d["qt"], d["i"]
                tq = slice(qt * 128, (qt + 1) * 128)
                tps = bank_bf(7)[:, i * 256:(i + 1) * 256]
                for j in range(2):
                    S.pe(lambda e, j=j: e.transpose(out=tps[:, j * 128:(j + 1) * 128], in_=onbb[i][:, j * 128:(j + 1) * 128], identity=ident[:]),
                         reads=[b_onb[i], b_ident], writes=[PB7s[i]])
                S.act(lambda e: e.activation(out=oT[:, :, tq], in_=tps.rearrange("p (j t) -> p j t", j=2), func=AF.Copy),
                      reads=[PB7s[i]], writes=[b_oT])
                if qt == NT - 1:
                    S.dma("sp", lambda e: e.dma_start(
                        out=mergedT[1024 + hh * 256:1024 + (hh + 1) * 256, :].rearrange("(j p) t -> p j t", p=128), in_=oT),
                        reads=[b_oT], writes=[b_scr["mergedT"]])

            Nn = len(dit)
            for n in range(-2, Nn):
                v0 = 0 <= n < Nn
                v1 = 0 <= n + 1 < Nn
                v2 = 0 <= n + 2 < Nn
                if v1:
                    f_a2a(dit[n + 1])
                if v0:
                    f_T(dit[n])
                if v2:
                    f_z(dit[n + 2], 0)
                if v1:
                    f_a2b(dit[n + 1])
                if v0:
                    f_AV(dit[n])
                if v2:
                    f_z(dit[n + 2], 1)
                if v0:
                    f_O(dit[n])
            for bb in PB6s:
                PB[6].W = sorted(set(PB[6].W) | set(bb.W)); PB[6].R = sorted(set(PB[6].R) | set(bb.R))
            for bb in PB7s:
                PB[7].W = sorted(set(PB[7].W) | set(bb.W)); PB[7].R = sorted(set(PB[7].R) | set(bb.R))

        def load_feature_major(dst3, dbufs, src_scr, src_name):
            region_switch(R2_all, Bbuf)
            for k in range(KC):
                S.dma("sp", lambda e, k=k: e.dma_start(out=dst3[:, k, :], in_=src_scr[k * 128:(k + 1) * 128, :]),
                      reads=[b_scr[src_name]], writes=[dbufs[k]])

        def odd_mixer():
            w_in = Wd["od_w_in"]
            region_switch(R2_all, Bbuf)
            def evac_u(ps, pbuf, nchg, tb):
                S.act(lambda e: e.activation(out=Bb[:, nchg, tb * 512:(tb + 1) * 512], in_=ps, func=AF.Gelu_apprx_tanh),
                      reads=[pbuf], writes=[Bbuf[nchg]])
            linear_A(A, A_tb, w_in[:, 0:2048], KC, evac_u)

            def evac_v(ps, pbuf, tt, cb, ex=None):
                i = next_stg()
                S.act(lambda e: e.activation(out=stg[i][:], in_=ps, func=AF.Gelu_apprx_tanh), reads=[pbuf], writes=[b_stg[i]])
                S.dma("sp", lambda e: e.dma_start(out=vsc[tt * 128:(tt + 1) * 128, cb * 512:(cb + 1) * 512], in_=stg[i][:]),
                      reads=[b_stg[i]], writes=[b_scr["vsc"]])
            linear_B(A, A_tt, w_in[:, 2048:4096], KC, evac_v)
            wsn = R1f[:, 0:2048].rearrange("p (g s) -> p g s", g=16)
            wsb = R1[:, 4096:6144].rearrange("p (g s) -> p g s", g=16)
            wmT = R1[:, 6144:8192].rearrange("p (g t) -> p g t", g=16)
            bsb = R1f[:, 4096:6144].rearrange("p (g t) -> p g t", g=16)
            lng = R1f[:, 6144:8192]
            lnb = R1f[:, 8192:10240]
            vnb = R1[:, 20480:22528]
            tmpf = R1f[:, 11264:11776]
            b_ws, b_wmT, b_bsb, b_ln, b_vnb, b_tmpf = (Buf(n) for n in ["wsn", "wmT", "bsb", "ln", "vnb", "tmpf"])
            region_switch(R1_all, [b_ws, b_wmT, b_bsb, b_ln, b_vnb, b_tmpf])
            S.dma("sp", lambda e: e.dma_start(out=wsn, in_=Wd["od_w_s"].rearrange("g t s -> t g s")), writes=[b_ws])
            S.dma("sp", lambda e: e.dma_start(out=bsb.rearrange("p g t -> p (g t)"), in_=Wd["od_b_s"].partition_broadcast(128)), writes=[b_bsb])
            S.dma("sp", lambda e: e.dma_start(out=lng, in_=Wd["od_ln_g"].partition_broadcast(128)), writes=[b_ln])
            S.dma("sp", lambda e: e.dma_start(out=lnb, in_=Wd["od_ln_b"].partition_broadcast(128)), writes=[b_ln])
            S.dve(lambda e: e.tensor_copy(out=wsb, in_=wsn), reads=[b_ws], writes=[b_ws])
            S.pool(lambda e: e.memset(wsb[0:64, :, 64:128], 0.0), reads=[b_ws], writes=[b_ws])
            for half in range(2):
                for j in range(8):
                    g = half * 8 + j
                    S.pe(lambda e, half=half, j=j, g=g: e.transpose(out=bank_bf(4 + half)[:, j * 128:(j + 1) * 128], in_=wsb[:, g, :],
                                                                    identity=ident[:]), reads=[b_ws, b_ident], writes=[PB[4 + half]])
                S.dve(lambda e, half=half: e.tensor_copy(out=wmT[:, half * 8:(half + 1) * 8, :],
                                                         in_=bank_bf(4 + half).rearrange("p (g t) -> p g t", g=8)),
                      reads=[PB[4 + half]], writes=[b_wmT])
            FMAX = 512
            stats = R1f[:, 12288:12288 + 4 * 6]
            mv = R1f[:, 12320:12322]
            prefetch_w(Wd["od_w_out"][:, 0:512], KC)
            prefetch_w(Wd["od_w_out"][:, 512:1024], KC)
            vbuf = [hst, mst]
            b_vbuf = [b_hst, b_mst]
            vnbb = [R1[:, 20480:22528], R1[:, 26624:28672]]
            b_vnbb = [b_vnb, Buf("vnb1")]
            region_switch(R1_all, [b_vnbb[1]])
            sp_sm = [[smalloc() for _ in range(3)] for _ in range(2)]
            statsb = [R1f[:, 12288 + i * 32:12288 + i * 32 + 24] for i in range(2)]
            mvb = [R1f[:, 12352 + i * 4:12352 + i * 4 + 2] for i in range(2)]
            b_stats = [Buf("stats0"), Buf("stats1")]
            region_switch(R1_all, b_stats)

            def g1(n):
                i = n % 2
                vt = vbuf[i]
                stats, mv = statsb[i], mvb[i]
                (ve, b_ve), (rs, b_rs), (nmr, b_nmr) = sp_sm[i]
                rows = slice(n * 128, (n + 1) * 128)
                S.dma("sp", lambda e: e.dma_start(out=vt[:], in_=vsc[rows, :]), reads=[b_scr["vsc"]], writes=[b_vbuf[i]])
                for c in range(4):
                    S.dve(lambda e, c=c: e.bn_stats(out=stats[:, c * 6:(c + 1) * 6], in_=vt[:, c * 512:(c + 1) * 512]), reads=[b_vbuf[i]], writes=[b_stats[i]])
                S.dve(lambda e: e.bn_aggr(out=mv, in_=stats), reads=[b_stats[i]], writes=[b_stats[i]])
                S.dve(lambda e: e.tensor_scalar(out=ve, in0=mv[:, 1:2], scalar1=EPS, scalar2=None, op0=ALU.add), reads=[b_stats[i]], writes=[b_ve])
                S.pool(lambda e: e.tensor_tensor(out=rs, in0=ve, in1=negh, op=ALU.pow), reads=[b_ve, b_negh], writes=[b_rs])
                S.dve(lambda e: e.scalar_tensor_tensor(out=nmr, in0=mv[:, 0:1], scalar=-1.0, in1=rs, op0=ALU.mult, op1=ALU.mult),
                      reads=[b_stats[i], b_rs], writes=[b_nmr])
                S.act(lambda e: e.activation(out=vt[:], in_=vt[:], func=AF.Identity, scale=rs, bias=nmr), reads=[b_vbuf[i], b_rs, b_nmr], writes=[b_vbuf[i]])
                S.pool(lambda e: e.tensor_tensor(out=vt[:], in0=vt[:], in1=lng, op=ALU.mult), reads=[b_vbuf[i], b_ln], writes=[b_vbuf[i]])
                S.dve(lambda e: e.tensor_tensor(out=vnbb[i], in0=vt[:], in1=lnb, op=ALU.add), reads=[b_vbuf[i], b_ln], writes=[b_vnbb[i]])

            def g2(n):
                i = n % 2
                vn = vnbb[i]
                for g4 in range(4):
                    b = next_pb()
                    for j in range(4):
                        g = g4 * 4 + j
                        S.pe(lambda e, b=b, j=j, g=g: e.matmul(bank(b)[:, j * 128:(j + 1) * 128], lhsT=vn[:, g * 128:(g + 1) * 128], rhs=wmT[:, g, :],
                                                               start=True, stop=True), reads=[b_vnbb[i], b_wmT], writes=[PB[b]])
                    tf = R1f[:, 11264 + (g4 % 2) * 512:11264 + (g4 % 2 + 1) * 512]
                    b_tf = [b_tmpf, b_tmpf2][g4 % 2]
                    S.dve(lambda e, b=b, g4=g4, tf=tf: e.tensor_tensor(out=tf.rearrange("p (g t) -> p g t", g=4),
                                                                       in0=bank(b).rearrange("p (g t) -> p g t", g=4),
                                                                       in1=bsb[:, g4 * 4:(g4 + 1) * 4, :], op=ALU.add),
                          reads=[PB[b], b_bsb], writes=[b_tf])
                    yv = Bb[:, g4 * 4:(g4 + 1) * 4, n * 128:(n + 1) * 128]
                    eng = S.pool if g4 % 2 == 0 else S.dve
                    eng(lambda e, yv=yv, tf=tf: e.tensor_tensor(out=yv, in0=yv, in1=tf.rearrange("p (g t) -> p g t", g=4), op=ALU.mult),
                        reads=[b_tf] + Bbuf[g4 * 4:(g4 + 1) * 4], writes=Bbuf[g4 * 4:(g4 + 1) * 4])

            b_tmpf2 = Buf("tmpf2")
            region_switch(R1_all, [b_tmpf2])
            for n in range(-1, NT):
                if n + 1 < NT:
                    g1(n + 1)
                if n >= 0:
                    g2(n)
            linear_B(Bb, B_all, Wd["od_w_out"], KC, evac_to_ms)

        prefetch_w(Wd["ev_w_in"][:, 0:1024][:, 0:512], KC)
        prefetch_w(Wd["ev_w_in"][:, 0:1024][:, 512:1024], KC)
        upd_pass(x_in, None, None, None, "norm", GC_EV)
        even_mixer()
        if stop in ("qkv", "sb"):
            pass
        else:
            load_feature_major(Bb, Bbuf, mergedT, "mergedT")
            linear_B(Bb, B_all, Wd["ev_w_out"], KC, evac_to_ms)
            if stop == "mix0":
                upd_pass(x_in, ms, Wd["ev_norm_post"], out, None, None, final=True)
            else:
                prefetch_w(Wd["ffn_w1"][0][:, 0:2048][:, 0:512], KC)
                prefetch_w(Wd["ffn_w1"][0][:, 0:2048][:, 512:1024], KC)
                upd_pass(x_in, ms, Wd["ev_norm_post"], hs, "norm", GC_F0)
                ffn(0)
                if stop == "ffn0":
                    upd_pass(hs, ms, Wd["ffn_norm_post"][0:1, :], out, None, None, final=True)
                else:
                    prefetch_w(Wd["ple_w_proj"][0][:, 0:512], 2)
                    prefetch_w(Wd["ple_w_gate"][0][:, 0:512], KC)
                    upd_pass(hs, ms, Wd["ffn_norm_post"][0:1, :], hs, "raw", None)
                    ple(0)
                    if stop == "l0":
                        upd_pass(hs, ms, Wd["ple_norm"][0:1, :], out, None, None, final=True)
                    else:
                        prefetch_w(Wd["od_w_in"][:, 0:2048][:, 0:512], KC)
                        prefetch_w(Wd["od_w_in"][:, 0:2048][:, 512:1024], KC)
                        upd_pass(hs, ms, Wd["ple_norm"][0:1, :], hs, "norm", GC_OD)
                        odd_mixer()
                        if stop == "mix1":
                            upd_pass(hs, ms, Wd["od_norm_post"], out, None, None, final=True)
                        else:
                            prefetch_w(Wd["ffn_w1"][1][:, 0:2048][:, 0:512], KC)
                            prefetch_w(Wd["ffn_w1"][1][:, 0:2048][:, 512:1024], KC)
                            upd_pass(hs, ms, Wd["od_norm_post"], hs, "norm", GC_F1)
                            ffn(1)
                            prefetch_w(Wd["ple_w_proj"][1][:, 0:512], 2)
                            prefetch_w(Wd["ple_w_gate"][1][:, 0:512], KC)
                            upd_pass(hs, ms, Wd["ffn_norm_post"][1:2, :], hs, "raw", None)
                            ple(1)
                            upd_pass(hs, ms, Wd["ple_norm"][1:2, :], out, None, None, final=True)
        if stop in ("qkv", "sb"):
            o = S.dma("sp", lambda e: e.dma_start(out=out[0:128, :], in_=hst[:]), reads=[b_hst], writes=[b_out])
            out_ops.append(o)
        assert not pref, list(pref)
        S.emit(final_wait_ops=out_ops)
    return nc


def make_in_maps(inputs):
    f = lambda a: np.ascontiguousarray(np.asarray(a))
    shared = {}
    for name, shape in W_SPECS:
        shared[name] = f(inputs[name]).reshape(shape)
    maps = []
    for b in range(8):
        m = dict(shared)
        m["x"] = f(inputs["x"][b])
        m["p"] = f(inputs["p"][:, b])
        m["positions"] = f(inputs["positions"][b:b + 1]).astype(np.int32)
        maps.append(m)
    return maps


_NC_CACHE = {}


def kernel(**inputs):
    if "nc" not in _NC_CACHE:
        _NC_CACHE["nc"] = build()
    nc = _NC_CACHE["nc"]
    maps = make_in_maps(inputs)
    res = run_bass_kernel_spmd(nc, maps, core_ids=list(range(8)))
    return np.stack([np.asarray(r["out"]) for r in res.results], axis=0).astype(np.float32)
```

```python
import math
from contextlib import ExitStack

import numpy as np
import concourse.bass as bass
import concourse.mybir as mybir
from concourse.bass_utils import run_bass_kernel_spmd

F32 = mybir.dt.float32
BF16 = mybir.dt.bfloat16
I32 = mybir.dt.int32
AF = mybir.ActivationFunctionType
ALU = mybir.AluOpType
AX = mybir.AxisListType

ENGINES = ("pe", "act", "dve", "pool", "sp")
DMA_SEMS_PER_QUEUE = 8


class Buf:
    __slots__ = ("name", "W", "R")

    def __init__(self, name):
        self.name = name
        self.W = []
        self.R = []


class _Op:
    __slots__ = ("eng", "fn", "deps", "is_dma", "idx", "signal", "ticket", "dsem")

    def __init__(self, eng, fn, is_dma, idx):
        self.eng = eng
        self.fn = fn
        self.is_dma = is_dma
        self.idx = idx
        self.deps = set()
        self.signal = False
        self.ticket = None
        self.dsem = None


class Sched:
    def __init__(self, nc):
        self.nc = nc
        self.ops = []

    def op(self, eng, fn, reads=(), writes=(), dma=False):
        o = _Op(eng, fn, dma, len(self.ops))
        deps = set()
        for b in reads:
            deps.update(b.W)
        for b in writes:
            deps.update(b.W)
            deps.update(b.R)
        o.deps = deps
        for b in writes:
            b.W = [o.idx]
            b.R = []
        for b in reads:
            if b in writes:
                continue
            if not dma:
                b.R = [i for i in b.R if not (self.ops[i].eng == eng and not self.ops[i].is_dma)]
            b.R.append(o.idx)
        self.ops.append(o)
        return o

    def pe(self, fn, reads=(), writes=()):
        return self.op("pe", fn, reads, writes)

    def act(self, fn, reads=(), writes=()):
        return self.op("act", fn, reads, writes)

    def dve(self, fn, reads=(), writes=()):
        return self.op("dve", fn, reads, writes)

    def pool(self, fn, reads=(), writes=()):
        return self.op("pool", fn, reads, writes)

    def dma(self, eng, fn, reads=(), writes=()):
        return self.op(eng, fn, reads, writes, dma=True)

    def emit(self, final_wait_ops=()):
        nc = self.nc
        ops = self.ops
        for o in ops:
            for d in o.deps:
                p = ops[d]
                if p.is_dma:
                    continue
                if p.eng != o.eng or o.is_dma or o.eng != "pe":
                    p.signal = True
        with ExitStack() as es:
            esem = {e: es.enter_context(nc.semaphore("s_" + e)) for e in ENGINES}
            dsems = {}
            for e in ("sp", "act", "pool"):
                dsems[e] = [es.enter_context(nc.semaphore(f"d_{e}_{i}")) for i in range(DMA_SEMS_PER_QUEUE)]
            cnt = {e: 0 for e in ENGINES}
            dcnt = {e: 0 for e in dsems}
            dtot = {e: [0] * DMA_SEMS_PER_QUEUE for e in dsems}
            for o in ops:
                if o.is_dma:
                    j = dcnt[o.eng] % DMA_SEMS_PER_QUEUE
                    dcnt[o.eng] += 1
                    o.dsem = (o.eng, j, dtot[o.eng][j])
                    dtot[o.eng][j] += 16
                    o.ticket = (dsems[o.eng][j], dtot[o.eng][j], ("d", o.eng, j))
                elif o.signal:
                    cnt[o.eng] += 1
                    o.ticket = (esem[o.eng], cnt[o.eng], ("e", o.eng))
            self.max_ticket = dict(cnt)
            per_eng = {e: [o for o in ops if o.eng == e] for e in ENGINES}
            final = list(final_wait_ops)
            blk = es.enter_context(nc.Block())

            def run(e, eng):
                seen = {}
                for o in per_eng[e]:
                    waits = {}
                    for d in o.deps:
                        p = ops[d]
                        if p.ticket is None:
                            continue
                        if (not p.is_dma) and (not o.is_dma) and p.eng == e and e == "pe":
                            continue
                        sem, val, key = p.ticket
                        if seen.get(key, 0) >= val:
                            continue
                        if key not in waits or waits[key][1] < val:
                            waits[key] = (sem, val)
                    if o.is_dma:
                        qe, j, prev = o.dsem
                        key = ("d", qe, j)
                        if prev > 0 and seen.get(key, 0) < prev:
                            if key not in waits or waits[key][1] < prev:
                                waits[key] = (dsems[qe][j], prev)
                    for key, (sem, val) in waits.items():
                        eng.wait_ge(sem, val)
                        seen[key] = val
                    ins = o.fn(eng)
                    if o.ticket is not None:
                        sem, val, key = o.ticket
                        ins.then_inc(sem, 16 if o.is_dma else 1)
                if e == "sp":
                    for o in final:
                        sem, val, key = o.ticket
                        eng.wait_ge(sem, val)

            @blk.tensor
            def _(eng):
                run("pe", eng)

            @blk.scalar
            def _(eng):
                run("act", eng)

            @blk.vector
            def _(eng):
                run("dve", eng)

            @blk.gpsimd
            def _(eng):
                run("pool", eng)

            @blk.sync
            def _(eng):
                run("sp", eng)


D = 2048
T = 2048
NT = 16
KC = 16
HD = 128
DFF = 8192
PLE = 256
EPS = 1e-6
ROPE_THETA = 500000.0
ROT = 32
SCALE = HD ** -0.5
MAGIC = 12582912.0
TWO_PI = 2.0 * math.pi
C1 = 6.28125
C2 = 1015.0 / 524288.0
C3 = TWO_PI - C1 - C2

W_SPECS = [
    ("ev_norm_pre", [1, D]), ("ev_w_in", [D, 6144]),
    ("ev_lam_q1", [1, HD]), ("ev_lam_k1", [1, HD]), ("ev_lam_q2", [1, HD]), ("ev_lam_k2", [1, HD]),
    ("ev_subln", [1, 256]), ("ev_w_out", [D, D]), ("ev_norm_post", [1, D]),
    ("od_norm_pre", [1, D]), ("od_w_in", [D, 4096]), ("od_ln_g", [1, D]), ("od_ln_b", [1, D]),
    ("od_w_s", [16, 128, 128]), ("od_b_s", [1, 2048]), ("od_w_out", [D, D]), ("od_norm_post", [1, D]),
    ("ffn_norm_pre", [2, D]), ("ffn_w1", [2, D, DFF]), ("ffn_w2", [2, DFF, D]), ("ffn_norm_post", [2, D]),
    ("ple_w_proj", [2, PLE, D]), ("ple_w_gate", [2, D, D]), ("ple_norm", [2, D]),
]


def build(stop=None, debug=False):
    nc = bass.Bass("TRN2", target_bir_lowering=False)
    x_in = nc.dram_tensor("x", [T, D], F32, kind="ExternalInput").ap()
    p_in = nc.dram_tensor("p", [2, T, PLE], F32, kind="ExternalInput").ap()
    pos_in = nc.dram_tensor("positions", [1, T], I32, kind="ExternalInput").ap()
    Wd = {}
    for name, shape in W_SPECS:
        Wd[name] = nc.dram_tensor(name, shape, F32, kind="ExternalInput").ap()
    out = nc.dram_tensor("out", [T, D], F32, kind="ExternalOutput").ap()
    dk = "ExternalOutput" if debug else "Internal"
    hs = nc.dram_tensor("hs", [T, D], F32, kind=dk).ap()
    ms = nc.dram_tensor("ms", [T, D], F32, kind=dk).ap()
    sbqT = nc.dram_tensor("sbqT", [1024, T], BF16, kind=dk).ap()
    sbkT = nc.dram_tensor("sbkT", [1024, T], BF16, kind=dk).ap()
    sbv = nc.dram_tensor("sbv", [T, 1024], BF16, kind=dk).ap()
    dfqT = nc.dram_tensor("dfqT", [1024, T], BF16, kind=dk).ap()
    dfkT = nc.dram_tensor("dfkT", [1024, T], BF16, kind=dk).ap()
    dfv = nc.dram_tensor("dfv", [T, 1024], BF16, kind=dk).ap()
    mergedT = nc.dram_tensor("mergedT", [D, T], BF16, kind=dk).ap()
    vsc = nc.dram_tensor("vsc", [T, D], F32, kind=dk).ap()

    es = ExitStack()
    with es:
        def sb(name, shape, dt):
            return es.enter_context(nc.sbuf_tensor(name, shape, dt))

        R1 = sb("R1", [128, 32768], BF16)
        R2 = sb("R2", [128, 32768], BF16)
        WS = [sb(f"ws{i}", [128, 16, 512], BF16) for i in range(2)]
        hst = sb("hst", [128, D], F32)
        mst = sb("mst", [128, D], F32)
        gpb = sb("gpb", [128, D], F32)
        xnb = sb("xnb", [128, D], BF16)
        stg = [sb(f"stg{i}", [128, 512], F32) for i in range(4)]
        gcols = sb("gcols", [128, 16 * 5], F32)
        ident = sb("ident", [128, 128], BF16)
        identf = sb("identf", [128, 128], F32)
        sm = sb("sm", [128, 128], F32)
        pmat = sb("pmat", [128, 32], F32)
        g16 = sb("g16", [16, 5 * 128], F32)
        WP = [sb(f"wp{i}", [128, 2, 512], BF16) for i in range(2)]
        WPb = [Buf("wp0"), Buf("wp1")]
        PS = es.enter_context(nc.psum_tensor("PS", [128, 4096], F32))

        S = Sched(nc)
        A = R1[:].rearrange("p (k t) -> p k t", k=16)
        Bb = R2[:].rearrange("p (k t) -> p k t", k=16)
        R1f = R1[:].bitcast(F32)
        R2f = R2[:].bitcast(F32)

        def bank(b):
            return PS[:, b * 512:(b + 1) * 512]

        def bank_bf(b):
            return PS[:, b * 512:(b + 1) * 512].bitcast(BF16)

        PB = [Buf(f"pb{b}") for b in range(8)]
        Abuf = [Buf(f"A{t}") for t in range(NT)]
        Bbuf = [Buf(f"B{k}") for k in range(KC)]
        WSb = [Buf("ws0"), Buf("ws1")]
        b_hst, b_mst, b_gpb, b_xnb = Buf("hst"), Buf("mst"), Buf("gpb"), Buf("xnb")
        b_stg = [Buf(f"stg{i}") for i in range(4)]
        b_gcols, b_ident, b_sm, b_pmat, b_g16 = Buf("gcols"), Buf("ident"), Buf("sm"), Buf("pmat"), Buf("g16")
        b_hs = [Buf(f"hs{t}") for t in range(NT)]
        b_ms = [Buf(f"ms{t}") for t in range(NT)]
        b_out = Buf("out")
        b_R1 = Buf("R1")
        b_R2 = Buf("R2")
        b_scr = {n: Buf(n) for n in ["sbqT", "sbkT", "sbv", "dfqT", "dfkT", "dfv", "mergedT", "vsc"]}
        out_ops = []
        st = {"stg": 0, "ws": 0, "pb": 0, "sm": 20}
        R1_all = set(Abuf)
        R2_all = set(Bbuf)

        def region_switch(region_all, new_bufs):
            Wu, Ru = set(), set()
            for b in region_all:
                Wu.update(b.W)
                Ru.update(b.R)
            for b in new_bufs:
                b.W = sorted(set(b.W) | Wu)
                b.R = sorted(set(b.R) | Ru)
            region_all.update(new_bufs)

        SM_SS, SM_VE, SM_RSTD, SM_NEGH, SM_EPS, SM_SS2, SM_RSTD2 = 0, 1, 2, 3, 4, 5, 6
        SM_MX, SM_L1, SM_L2, SM_LB, SM_R1, SM_R2, SM_NLAM, SM_TMP, SM_TMP2 = 8, 9, 10, 11, 12, 13, 14, 15, 16
        SM_NTOT = 17

        def smc(c, n=128):
            return sm[0:n, c:c + 1]

        S.pool(lambda e: e.memset(ident[:], 1.0), writes=[b_ident])
        S.pool(lambda e: e.affine_select(out=ident[:], in_=ident[:], pattern=[[-1, 128]], compare_op=ALU.is_equal,
                                         fill=0.0, base=0, channel_multiplier=1), reads=[b_ident], writes=[b_ident])
        S.pool(lambda e: e.memset(identf[:], 1.0), writes=[b_ident])
        S.pool(lambda e: e.affine_select(out=identf[:], in_=identf[:], pattern=[[-1, 128]], compare_op=ALU.is_equal,
                                         fill=0.0, base=0, channel_multiplier=1), reads=[b_ident], writes=[b_ident])
        S.pool(lambda e: e.memset(sm[:], 0.0), writes=[b_sm])
        S.pool(lambda e: e.memset(smc(SM_NEGH), -0.5), writes=[b_sm])
        S.pool(lambda e: e.memset(smc(SM_EPS), EPS), writes=[b_sm])
        gnames = [("ev_norm_pre", 0), ("ffn_norm_pre", 0), ("od_norm_pre", 0), ("ffn_norm_pre", 1)]
        for j, (nm, li) in enumerate(gnames):
            src = Wd[nm][li:li + 1, :].rearrange("o (k p) -> (o k) p", p=128)
            S.dma("sp", lambda e, j=j, src=src: e.dma_start(out=g16[:, j * 128:(j + 1) * 128], in_=src), writes=[b_g16])
        for j in range(len(gnames)):
            S.pe(lambda e, j=j: e.matmul(PS[:, 3584 + j * 16:3584 + (j + 1) * 16], lhsT=g16[:, j * 128:(j + 1) * 128],
                                          rhs=identf[0:16, 0:16], start=True, stop=True),
                 reads=[b_g16, b_ident], writes=[PB[7]])
        S.dve(lambda e: e.tensor_copy(out=gcols[:, 0:64], in_=PS[:, 3584:3584 + 64]), reads=[PB[7]], writes=[b_gcols])
        GC_EV, GC_F0, GC_OD, GC_F1 = 0, 1, 2, 3

        def rstd_from_ss(c_ss, c_out, n_feat, np_=128):
            S.dve(lambda e: e.tensor_scalar(out=smc(SM_VE, np_), in0=smc(c_ss, np_), scalar1=1.0 / n_feat, scalar2=EPS,
                                            op0=ALU.mult, op1=ALU.add), reads=[b_sm], writes=[b_sm])
            S.pool(lambda e: e.tensor_tensor(out=smc(c_out, np_), in0=smc(SM_VE, np_), in1=smc(SM_NEGH, np_), op=ALU.pow),
                   reads=[b_sm], writes=[b_sm])

        b_negh = Buf("negh")
        negh = sm[:, 120:121]
        S.pool(lambda e: e.memset(negh, -0.5), writes=[b_negh])
        st["sm"] = 20

        def smalloc():
            c = st["sm"]
            st["sm"] += 1
            assert c < 120
            return sm[:, c:c + 1], Buf(f"sm{c}")

        def rstd_op(ss, b_ss, ve, b_ve, rs, b_rs, n_feat):
            S.dve(lambda e: e.tensor_scalar(out=ve, in0=ss, scalar1=1.0 / n_feat, scalar2=EPS, op0=ALU.mult, op1=ALU.add),
                  reads=[b_ss], writes=[b_ve])
            S.pool(lambda e: e.tensor_tensor(out=rs, in0=ve, in1=negh, op=ALU.pow), reads=[b_ve, b_negh], writes=[b_rs])

        upd_sm = [[smalloc() for _ in range(6)] for _ in range(2)]

        def upd_pass(h_src, m_src, g_post, h_dst, a_mode, gcol_idx, final=False):
            NB3 = 3
            hstb = [R2f[:, i * 2048:(i + 1) * 2048] for i in range(NB3)]
            mstb = [R2f[:, 6144 + i * 2048:6144 + (i + 1) * 2048] for i in range(NB3)]
            xnbb = [R2[:, 24576 + i * 2048:24576 + (i + 1) * 2048] for i in range(2)]
            junkb = R2[:, 28672:30720]
            bh = [Buf(f"uh{i}") for i in range(NB3)]
            bm = [Buf(f"um{i}") for i in range(NB3)]
            bx = [Buf("ux0"), Buf("ux1")]
            bj = Buf("ujunk")
            region_switch(R2_all, bh + bm + bx + [bj])
            if m_src is not None:
                S.dma("sp", lambda e: e.dma_start(out=gpb[:], in_=g_post.partition_broadcast(128)), writes=[b_gpb])
            if a_mode is not None:
                region_switch(R1_all, Abuf)
            def s1(tt):
                i = tt % NB3
                hs_, ms_ = hstb[i], mstb[i]
                (ssm, b_ssm), (vem, b_vem), (rsm, b_rsm) = upd_sm[tt % 2][0:3]
                rows = slice(tt * 128, (tt + 1) * 128)
                rd = [b_hs[tt]] if h_src is hs else []
                S.dma("sp", lambda e: e.dma_start(out=hs_, in_=h_src[rows, :]), reads=rd, writes=[bh[i]])
                if m_src is not None:
                    S.dma("sp", lambda e: e.dma_start(out=ms_, in_=m_src[rows, :]), reads=[b_ms[tt]], writes=[bm[i]])
                    S.act(lambda e: e.activation(out=junkb, in_=ms_, func=AF.Square, accum_out=ssm),
                          reads=[bm[i]], writes=[bj, b_ssm])
                    rstd_op(ssm, b_ssm, vem, b_vem, rsm, b_rsm, D)
                    S.dve(lambda e: e.scalar_tensor_tensor(out=ms_, in0=ms_, scalar=rsm, in1=gpb[:], op0=ALU.mult, op1=ALU.mult),
                          reads=[bm[i], b_rsm, b_gpb], writes=[bm[i]])
                    S.dve(lambda e: e.tensor_tensor(out=hs_, in0=hs_, in1=ms_, op=ALU.add),
                          reads=[bh[i], bm[i]], writes=[bh[i]])
                if h_dst is not None:
                    o = S.dma("pool", lambda e: e.dma_start(out=h_dst[rows, :], in_=hs_), reads=[bh[i]],
                              writes=[b_hs[tt]] if h_dst is hs else [b_out])
                    if final:
                        out_ops.append(o)

            def s2(tt):
                if a_mode is None:
                    return
                i = tt % NB3
                ix = tt % 2
                hs_, xn_ = hstb[i], xnbb[ix]
                (ssh, b_ssh), (veh, b_veh), (rsh, b_rsh) = upd_sm[ix][3:6]
                if a_mode == "norm":
                    S.act(lambda e: e.activation(out=junkb, in_=hs_, func=AF.Square, accum_out=ssh),
                          reads=[bh[i]], writes=[bj, b_ssh])
                    rstd_op(ssh, b_ssh, veh, b_veh, rsh, b_rsh, D)
                    S.act(lambda e: e.activation(out=xn_, in_=hs_, func=AF.Copy, scale=rsh),
                          reads=[bh[i], b_rsh], writes=[bx[ix]])
                else:
                    S.act(lambda e: e.activation(out=xn_, in_=hs_, func=AF.Copy), reads=[bh[i]], writes=[bx[ix]])
                for half in range(2):
                    bk = 4 + half
                    for j in range(8):
                        k = half * 8 + j
                        S.pe(lambda e, bk=bk, j=j, k=k: e.transpose(out=bank_bf(bk)[:, j * 128:(j + 1) * 128],
                                                                    in_=xn_[:, k * 128:(k + 1) * 128], identity=ident[:]),
                             reads=[bx[ix], b_ident], writes=[PB[bk]])
                    dst = A[:, half * 8:(half + 1) * 8, tt * 128:(tt + 1) * 128]
                    src = bank_bf(bk).rearrange("p (k t) -> p k t", k=8)
                    if a_mode == "norm":
                        gc = gcols[:, gcol_idx * 16 + half * 8: gcol_idx * 16 + half * 8 + 8].unsqueeze(2).to_broadcast([128, 8, 128])
                        S.dve(lambda e, dst=dst, src=src, gc=gc: e.tensor_tensor(out=dst, in0=src, in1=gc, op=ALU.mult),
                              reads=[PB[bk], b_gcols], writes=[Abuf[tt]])
                    else:
                        S.dve(lambda e, dst=dst, src=src: e.tensor_copy(out=dst, in_=src), reads=[PB[bk]], writes=[Abuf[tt]])

            for n in range(-1, NT):
                if n + 1 < NT:
                    s1(n + 1)
                if n >= 0:
                    s2(n)

        pref = {}

        def prefetch_w(wap, kc):
            k = repr(wap)
            if k in pref:
                return
            pref[k] = load_w(wap, kc, _nopref=True)

        def load_w(wap, kc, eng="pool", _nopref=False):
            if not _nopref:
                k = repr(wap)
                if k in pref:
                    return pref.pop(k)
            i = st["ws"] % 2
            st["ws"] += 1
            ncols = wap.shape[1]
            src = wap.rearrange("(k p) n -> p k n", p=128)
            S.dma(eng, lambda e: e.dma_start(out=WS[i][:, 0:kc, 0:ncols], in_=src), writes=[WSb[i]])
            return i

        def next_pb(n=4):
            b = st["pb"] % n
            st["pb"] += 1
            return b

        def next_stg():
            i = st["stg"] % 4
            st["stg"] += 1
            return i

        def linear_A(X, Xbufs_for_tb, wap, kc, evac):
            N = wap.shape[1]
            for cb in range(N // 512):
                wi = load_w(wap[:, cb * 512:(cb + 1) * 512], kc)
                for nch in range(4):
                    for tb in range(4):
                        b = next_pb()
                        for k in range(kc):
                            S.pe(lambda e, b=b, wi=wi, k=k, nch=nch, tb=tb: e.matmul(
                                bank(b), lhsT=WS[wi][:, k, nch * 128:(nch + 1) * 128], rhs=X[:, k, tb * 512:(tb + 1) * 512],
                                start=(k == 0), stop=(k == kc - 1)),
                                 reads=[WSb[wi]] + Xbufs_for_tb(tb), writes=[PB[b]])
                        evac(bank(b), PB[b], cb * 4 + nch, tb)

        def linear_B(X, Xbufs_for_tt, wap, kc, evac, extra=None, pre=None):
            N = wap.shape[1]
            for cb in range(N // 512):
                wi = load_w(wap[:, cb * 512:(cb + 1) * 512], kc)
                ex = extra(cb) if extra is not None else None
                for tt in range(NT):
                    if pre is not None:
                        pre(tt, cb)
                    b = next_pb()
                    for k in range(kc):
                        S.pe(lambda e, b=b, wi=wi, k=k, tt=tt: e.matmul(
                            bank(b), lhsT=X[:, k, tt * 128:(tt + 1) * 128], rhs=WS[wi][:, k, 0:512],
                            start=(k == 0), stop=(k == kc - 1)),
                             reads=[WSb[wi]] + Xbufs_for_tt(tt), writes=[PB[b]])
                    evac(bank(b), PB[b], tt, cb, ex)

        def A_tb(tb):
            return Abuf[tb * 4:(tb + 1) * 4]

        def A_tt(tt):
            return [Abuf[tt]]

        def B_all(_):
            return Bbuf

        def evac_to_ms(ps, pbuf, tt, cb, ex=None):
            i = next_stg()
            S.act(lambda e: e.activation(out=stg[i][:], in_=ps, func=AF.Copy), reads=[pbuf], writes=[b_stg[i]])
            S.dma("sp", lambda e: e.dma_start(out=ms[tt * 128:(tt + 1) * 128, cb * 512:(cb + 1) * 512], in_=stg[i][:]),
                  reads=[b_stg[i]], writes=[b_ms[tt]])

        def ffn(li):
            w1 = Wd["ffn_w1"][li]
            w2 = Wd["ffn_w2"][li]
            region_switch(R2_all, Bbuf)
            for fb in range(4):
                def evac1(ps, pbuf, nchg, tb):
                    i = next_stg()
                    S.act(lambda e: e.activation(out=stg[i][:], in_=ps, func=AF.Relu), reads=[pbuf], writes=[b_stg[i]])
                    S.dve(lambda e: e.scalar_tensor_tensor(out=Bb[:, nchg, tb * 512:(tb + 1) * 512], in0=ps, scalar=0.0,
                                                           in1=stg[i][:], op0=ALU.max, op1=ALU.mult),
                          reads=[pbuf, b_stg[i]], writes=[Bbuf[nchg]])
                linear_A(A, A_tb, w1[:, fb * 2048:(fb + 1) * 2048], KC, evac1)

                def ld_prev(q):
                    cb_, tt_ = q // NT, q % NT
                    i_ = q % 4
                    dst_ = ms[tt_ * 128:(tt_ + 1) * 128, cb_ * 512:(cb_ + 1) * 512]
                    S.dma("sp", lambda e: e.dma_start(out=stg[i_][:], in_=dst_), reads=[b_ms[tt_]], writes=[b_stg[i_]])

                def pre2(tt, cb, fb=fb):
                    if fb == 0:
                        return
                    q = cb * NT + tt
                    if q == 0:
                        ld_prev(0)
                    if q + 1 < 4 * NT:
                        ld_prev(q + 1)

                def evac2(ps, pbuf, tt, cb, ex=None, fb=fb):
                    dst = ms[tt * 128:(tt + 1) * 128, cb * 512:(cb + 1) * 512]
                    if fb == 0:
                        i = next_stg()
                        S.act(lambda e: e.activation(out=stg[i][:], in_=ps, func=AF.Copy), reads=[pbuf], writes=[b_stg[i]])
                    else:
                        i = (cb * NT + tt) % 4
                        S.dve(lambda e: e.tensor_tensor(out=stg[i][:], in0=ps, in1=stg[i][:], op=ALU.add),
                              reads=[pbuf, b_stg[i]], writes=[b_stg[i]])
                    S.dma("sp", lambda e: e.dma_start(out=dst, in_=stg[i][:]), reads=[b_stg[i]], writes=[b_ms[tt]])
                linear_B(Bb, B_all, w2[fb * 2048:(fb + 1) * 2048, :], KC, evac2, pre=pre2)

        def ple(li):
            pT = R2[:, 0:4096].rearrange("p (k t) -> p k t", k=2)
            b_pT = Buf("pT")
            region_switch(R2_all, [b_pT])
            for tt in range(NT):
                i = next_stg()
                S.dma("sp", lambda e, i=i, tt=tt: e.dma_start(out=stg[i][:, 0:256], in_=p_in[li, tt * 128:(tt + 1) * 128, :]),
                      writes=[b_stg[i]])
                S.act(lambda e, i=i: e.activation(out=xnb[:, 0:256], in_=stg[i][:, 0:256], func=AF.Copy),
                      reads=[b_stg[i]], writes=[b_xnb])
                for k in range(2):
                    S.pe(lambda e, k=k: e.transpose(out=bank_bf(4)[:, k * 128:(k + 1) * 128], in_=xnb[:, k * 128:(k + 1) * 128],
                                                    identity=ident[:]), reads=[b_xnb, b_ident], writes=[PB[4]])
                S.dve(lambda e, tt=tt: e.tensor_copy(out=pT[:, :, tt * 128:(tt + 1) * 128],
                                                     in_=bank_bf(4)[:, 0:256].rearrange("p (k t) -> p k t", k=2)),
                      reads=[PB[4]], writes=[b_pT])
            wg = Wd["ple_w_gate"][li]
            wp = Wd["ple_w_proj"][li]

            def extra(cb):
                i = cb % 2
                src = wp[:, cb * 512:(cb + 1) * 512].rearrange("(k p) n -> p k n", p=128)
                S.dma("pool", lambda e: e.dma_start(out=WP[i][:], in_=src), writes=[WPb[i]])
                return i

            def evac(ps, pbuf, tt, cb, wpi):
                S.pe(lambda e: e.matmul(bank(6), lhsT=pT[:, 0, tt * 128:(tt + 1) * 128], rhs=WP[wpi][:, 0, 0:512], start=True, stop=False),
                     reads=[b_pT, WPb[wpi]], writes=[PB[6]])
                S.pe(lambda e: e.matmul(bank(6), lhsT=pT[:, 1, tt * 128:(tt + 1) * 128], rhs=WP[wpi][:, 1, 0:512], start=False, stop=True),
                     reads=[b_pT, WPb[wpi]], writes=[PB[6]])
                i = next_stg()
                S.act(lambda e: e.activation(out=stg[i][:], in_=ps, func=AF.Sigmoid), reads=[pbuf], writes=[b_stg[i]])
                S.dve(lambda e: e.tensor_tensor(out=stg[i][:], in0=bank(6), in1=stg[i][:], op=ALU.mult),
                      reads=[PB[6], b_stg[i]], writes=[b_stg[i]])
                S.dma("sp", lambda e: e.dma_start(out=ms[tt * 128:(tt + 1) * 128, cb * 512:(cb + 1) * 512], in_=stg[i][:]),
                      reads=[b_stg[i]], writes=[b_ms[tt]])
            N = D
            for cb in range(N // 512):
                wpi = extra(cb)
                wi = load_w(wg[:, cb * 512:(cb + 1) * 512], KC)
                for tt in range(NT):
                    b = next_pb()
                    for k in range(KC):
                        S.pe(lambda e, b=b, wi=wi, k=k, tt=tt: e.matmul(
                            bank(b), lhsT=A[:, k, tt * 128:(tt + 1) * 128], rhs=WS[wi][:, k, 0:512],
                            start=(k == 0), stop=(k == KC - 1)), reads=[WSb[wi], Abuf[tt]], writes=[PB[b]])
                    evac(bank(b), PB[b], tt, cb, wpi)

        def even_mixer():
            cosT = R2f[0:32, 0:2048]
            sinT = R2f[0:32, 2048:4096]
            ang = R2f[0:32, 4096:6144]
            tmp = R2f[0:32, 6144:8192]
            tmp2 = R2f[0:32, 8192:10240]
            posi = R2f[0:32, 10240:12288].bitcast(I32)
            frow = R2f[0:1, 15360:15360 + 32]
            one2 = R2f[0:1, 15392:15394]
            b_rope = Buf("rope")
            region_switch(R2_all, [b_rope])
            S.dma("sp", lambda e: e.dma_start(out=posi, in_=pos_in.partition_broadcast(32)), writes=[b_rope])
            S.dve(lambda e: e.tensor_copy(out=ang, in_=posi), reads=[b_rope], writes=[b_rope])
            inv = (np.float32(ROPE_THETA) ** (-(np.arange(0, ROT, 2, dtype=np.float32)) / np.float32(ROT))).astype(np.float32)
            for i in range(16):
                for hh in range(2):
                    c = hh * 16 + i
                    S.pool(lambda e, c=c, v=float(inv[i]): e.memset(frow[:, c:c + 1], v), writes=[b_rope])
            S.pool(lambda e: e.memset(one2, 1.0), writes=[b_rope])
            S.pe(lambda e: e.matmul(PS[0:32, 3584:3586], lhsT=frow, rhs=one2, start=True, stop=True), reads=[b_rope], writes=[PB[7]])
            S.dve(lambda e: e.tensor_copy(out=sm[0:32, SM_TMP:SM_TMP + 1], in_=PS[0:32, 3584:3585]), reads=[PB[7]], writes=[b_sm])
            S.dve(lambda e: e.tensor_scalar(out=ang, in0=ang, scalar1=sm[0:32, SM_TMP:SM_TMP + 1], scalar2=None, op0=ALU.mult),
                  reads=[b_rope, b_sm], writes=[b_rope])
            S.dve(lambda e: e.tensor_scalar(out=tmp, in0=ang, scalar1=1.0 / TWO_PI, scalar2=MAGIC, op0=ALU.mult, op1=ALU.add),
                  reads=[b_rope], writes=[b_rope])
            S.dve(lambda e: e.tensor_scalar(out=tmp, in0=tmp, scalar1=-MAGIC, scalar2=None, op0=ALU.add), reads=[b_rope], writes=[b_rope])
            for cst in (C1, C2, C3):
                S.dve(lambda e, cst=cst: e.scalar_tensor_tensor(out=ang, in0=tmp, scalar=-cst, in1=ang, op0=ALU.mult, op1=ALU.add),
                      reads=[b_rope], writes=[b_rope])
            S.dve(lambda e: e.tensor_scalar(out=ang, in0=ang, scalar1=3.1415925, scalar2=-3.1415925, op0=ALU.min, op1=ALU.max),
                  reads=[b_rope], writes=[b_rope])
            S.act(lambda e: e.activation(out=sinT, in_=ang, func=AF.Sin), reads=[b_rope], writes=[b_rope])
            S.dve(lambda e: e.tensor_scalar(out=tmp2, in0=ang, scalar1=-1.0, scalar2=None, op0=ALU.mult), reads=[b_rope], writes=[b_rope])
            S.dve(lambda e: e.tensor_tensor(out=tmp2, in0=tmp2, in1=ang, op=ALU.max), reads=[b_rope], writes=[b_rope])
            S.dve(lambda e: e.tensor_scalar(out=tmp2, in0=tmp2, scalar1=-1.0, scalar2=math.pi / 2, op0=ALU.mult, op1=ALU.add),
                  reads=[b_rope], writes=[b_rope])
            S.act(lambda e: e.activation(out=cosT, in_=tmp2, func=AF.Sin), reads=[b_rope], writes=[b_rope])
            S.pool(lambda e: e.memset(pmat[:], 0.0), writes=[b_pmat])
            S.pool(lambda e: e.affine_select(out=pmat[:, 0:16], in_=pmat[:, 0:16], pattern=[[-1, 16]], compare_op=ALU.not_equal,
                                             fill=-1.0, base=-16, channel_multiplier=1), reads=[b_pmat], writes=[b_pmat])
            S.pool(lambda e: e.affine_select(out=pmat[:, 16:32], in_=pmat[:, 16:32], pattern=[[-1, 16]], compare_op=ALU.not_equal,
                                             fill=1.0, base=0, channel_multiplier=1), reads=[b_pmat], writes=[b_pmat])
            w_in = Wd["ev_w_in"]
            ostg = [R2[:, 24576 + i * 2048: 24576 + (i + 1) * 2048] for i in range(2)]
            b_ostg = [Buf("ostg0"), Buf("ostg1")]
            region_switch(R2_all, b_ostg)
            cnt = {"o": 0}

            def mk_evacA(dst_scr, dst_name, scale, rope):
                def evac(ps, pbuf, nchg, tb):
                    oi = (cnt["o"] // 4) % 2
                    cnt["o"] += 1
                    od = ostg[oi][:, tb * 512:(tb + 1) * 512]
                    if not rope:
                        S.act(lambda e: e.activation(out=od, in_=ps, func=AF.Copy, scale=scale), reads=[pbuf], writes=[b_ostg[oi]])
                    else:
                        i = next_stg()
                        S.act(lambda e: e.activation(out=stg[i][:], in_=ps, func=AF.Copy, scale=scale), reads=[pbuf], writes=[b_stg[i]])
                        S.pe(lambda e: e.matmul(PS[0:32, 3072:3584], lhsT=pmat[:], rhs=stg[i][:], start=True, stop=True),
                             reads=[b_pmat, b_stg[i]], writes=[PB[6]])
                        S.act(lambda e: e.activation(out=od, in_=stg[i][:], func=AF.Copy),
                              reads=[b_stg[i]], writes=[b_ostg[oi]])
                        S.dve(lambda e: e.tensor_tensor(out=tmp[:, 0:512], in0=PS[0:32, 3072:3584], in1=sinT[:, tb * 512:(tb + 1) * 512],
                                                        op=ALU.mult), reads=[PB[6], b_rope], writes=[b_rope])
                        S.dve(lambda e: e.tensor_tensor(out=tmp2[:, 0:512], in0=stg[i][0:32, :], in1=cosT[:, tb * 512:(tb + 1) * 512],
                                                        op=ALU.mult), reads=[b_stg[i], b_rope], writes=[b_rope])
                        S.dve(lambda e: e.tensor_tensor(out=od[0:32, :], in0=tmp[:, 0:512], in1=tmp2[:, 0:512], op=ALU.add),
                              reads=[b_rope], writes=[b_ostg[oi]])
                    if tb == 3:
                        S.dma("sp", lambda e: e.dma_start(out=dst_scr[nchg * 128:(nchg + 1) * 128, :], in_=ostg[oi]),
                              reads=[b_ostg[oi]], writes=[b_scr[dst_name]])
                return evac

            def mk_evacB(dst_scr, dst_name):
                def evac(ps, pbuf, tt, cb, ex=None):
                    i = next_stg()
                    sv = stg[i][:].bitcast(BF16)[:, 0:512]
                    S.act(lambda e: e.activation(out=sv, in_=ps, func=AF.Copy), reads=[pbuf], writes=[b_stg[i]])
                    S.dma("sp", lambda e: e.dma_start(out=dst_scr[tt * 128:(tt + 1) * 128, cb * 512:(cb + 1) * 512], in_=sv),
                          reads=[b_stg[i]], writes=[b_scr[dst_name]])
                return evac

            linear_A(A, A_tb, w_in[:, 0:1024], KC, mk_evacA(sbqT, "sbqT", SCALE, False))
            linear_A(A, A_tb, w_in[:, 1024:2048], KC, mk_evacA(sbkT, "sbkT", 1.0, False))
            linear_B(A, A_tt, w_in[:, 2048:3072], KC, mk_evacB(sbv, "sbv"))
            linear_A(A, A_tb, w_in[:, 3072:4096], KC, mk_evacA(dfqT, "dfqT", SCALE, True))
            linear_A(A, A_tb, w_in[:, 4096:5120], KC, mk_evacA(dfkT, "dfkT", 1.0, True))
            linear_B(A, A_tt, w_in[:, 5120:6144], KC, mk_evacB(dfv, "dfv"))
            if stop == "qkv":
                return
            prefetch_w(Wd["ev_w_out"][:, 0:512], KC)
            prefetch_w(Wd["ev_w_out"][:, 512:1024], KC)
            attention()

        def attention():
            ef = [R1f[:, i * 6144:i * 6144 + 2048] for i in range(2)]
            spf = [R1f[:, i * 6144 + 2048:i * 6144 + 4096] for i in range(2)]
            csf = [R1f[:, i * 6144 + 4096:i * 6144 + 6144] for i in range(2)]
            wb = [R1[:, 24576 + i * 2048:24576 + (i + 1) * 2048] for i in range(2)]
            wTb = [R1[:, 28672 + i * 2048:28672 + (i + 1) * 2048].rearrange("p (c t) -> p c t", c=16) for i in range(2)]
            qT = [R2[:, i * 2048:(i + 1) * 2048] for i in range(4)]
            vv = R2[:, 8192:12288]
            oT = R2[:, 12288:16384].rearrange("p (j t) -> p j t", j=2)
            onbb = [R2[:, 16384 + i * 256:16384 + (i + 1) * 256] for i in range(2)]
            gsub = R2f[:, 8448:8704]
            lamt = R2f[:, 8704:9216]
            junk = R2f[:, 9216:11264]
            b_e = [Buf("e0"), Buf("e1")]
            b_sp = [Buf("sp0"), Buf("sp1")]
            b_cs = [Buf("cs0"), Buf("cs1")]
            b_w = [Buf("w0"), Buf("w1")]
            b_wT = [Buf("wT0"), Buf("wT1")]
            b_onb = [Buf("onb0"), Buf("onb1")]
            b_v, b_oT, b_gsub, b_junk, b_lam = (Buf(n) for n in ["v", "oT", "gsub", "junk", "lam"])
            b_q = [Buf(f"q{i}") for i in range(4)]
            region_switch(R1_all, b_e + b_sp + b_cs + b_w + b_wT)
            region_switch(R2_all, b_q + b_onb + [b_v, b_oT, b_gsub, b_junk])
            PB6s = [Buf(f"pb6_{i}") for i in range(4)]
            PB7s = [Buf(f"pb7_{i}") for i in range(2)]
            for bb in PB6s:
                bb.W = list(PB[6].W); bb.R = list(PB[6].R)
            for bb in PB7s:
                bb.W = list(PB[7].W); bb.R = list(PB[7].R)
            lambda_init = 0.8 - 0.6 * math.exp(-0.3 * 0)
            (t1, b_t1), (t2, b_t2), (nlam, b_nlam) = smalloc(), smalloc(), smalloc()
            for i, nm in enumerate(["ev_lam_q1", "ev_lam_k1", "ev_lam_q2", "ev_lam_k2"]):
                S.dma("sp", lambda e, i=i, nm=nm: e.dma_start(out=lamt[:, i * 128:(i + 1) * 128], in_=Wd[nm].partition_broadcast(128)),
                      writes=[b_gsub])
            S.dma("sp", lambda e, i=i: e.dma_start(out=gsub, in_=Wd["ev_subln"].partition_broadcast(128)), writes=[b_gsub])
            S.dve(lambda e, i=i: e.tensor_tensor(out=junk[:, 0:128], in0=lamt[:, 0:128], in1=lamt[:, 128:256], op=ALU.mult),
                  reads=[b_gsub], writes=[b_junk])
            S.dve(lambda e, i=i: e.tensor_reduce(out=t1, in_=junk[:, 0:128], axis=AX.X, op=ALU.add), reads=[b_junk], writes=[b_t1])
            S.dve(lambda e, i=i: e.tensor_tensor(out=junk[:, 0:128], in0=lamt[:, 256:384], in1=lamt[:, 384:512], op=ALU.mult),
                  reads=[b_gsub], writes=[b_junk])
            S.dve(lambda e, i=i: e.tensor_reduce(out=t2, in_=junk[:, 0:128], axis=AX.X, op=ALU.add), reads=[b_junk], writes=[b_t2])
            S.act(lambda e, i=i: e.activation(out=t1, in_=t1, func=AF.Exp), reads=[b_t1], writes=[b_t1])
            S.act(lambda e, i=i: e.activation(out=t2, in_=t2, func=AF.Exp), reads=[b_t2], writes=[b_t2])
            S.dve(lambda e, i=i: e.tensor_tensor(out=nlam, in0=t2, in1=t1, op=ALU.subtract), reads=[b_t1, b_t2], writes=[b_nlam])
            S.dve(lambda e, i=i: e.tensor_scalar(out=nlam, in0=nlam, scalar1=-lambda_init, scalar2=None, op0=ALU.add),
                  reads=[b_nlam], writes=[b_nlam])
            S.dve(lambda e, i=i: e.tensor_scalar(out=gsub, in0=gsub, scalar1=(1.0 - lambda_init), scalar2=None, op0=ALU.mult),
                  reads=[b_gsub], writes=[b_gsub])

            zps = PS[:, 0:2048]
            ZB = PB[0:4]
            it = {"n": 0}

            def transposes_w(nblk, i):
                for g0 in range(0, nblk, 8):
                    bk = 4 + (g0 // 8) % 2
                    n = min(8, nblk - g0)
                    for j in range(n):
                        c = g0 + j
                        S.pe(lambda e, i=i, bk=bk, j=j, c=c: e.transpose(out=bank_bf(bk)[:, j * 128:(j + 1) * 128],
                                                                    in_=wb[i][:, c * 128:(c + 1) * 128], identity=ident[:]),
                             reads=[b_w[i], b_ident], writes=[PB[bk]])
                    src = bank_bf(bk)[:, 0:n * 128].rearrange("p (c t) -> p c t", c=n)
                    if (g0 // 8) % 2 == 0:
                        S.act(lambda e, i=i, src=src, g0=g0, n=n: e.activation(out=wTb[i][:, g0:g0 + n, :], in_=src, func=AF.Copy),
                              reads=[PB[bk]], writes=[b_wT[i]])
                    else:
                        S.dve(lambda e, i=i, src=src, g0=g0, n=n: e.tensor_copy(out=wTb[i][:, g0:g0 + n, :], in_=src),
                              reads=[PB[bk]], writes=[b_wT[i]])

            def run_pipeline(iters):
                N = len(iters)
                for n in range(-2, N):
                    if 0 <= n + 2 < N:
                        iters[n + 2][0]()
                    if 0 <= n + 1 < N:
                        iters[n + 1][1]()
                    if 0 <= n < N:
                        iters[n][2]()

            vvb = [R2[:, 8192:12288], R2[:, 22528:26624]]
            b_vb = [b_v, Buf("v1")]
            qk2 = [R2[:, 26624:28672], R2[:, 28672:30720]]
            b_qk2 = [Buf("q1b"), Buf("k1b")]
            region_switch(R2_all, [b_vb[1]] + b_qk2)
            sb_sm = [smalloc() for _ in range(2)]
            iters = []
            for hh in range(8):
                hp = hh % 2
                qh = qT[0] if hp == 0 else qk2[0]
                kh = qT[1] if hp == 0 else qk2[1]
                b_qh = b_q[0] if hp == 0 else b_qk2[0]
                b_kh = b_q[1] if hp == 0 else b_qk2[1]
                v3 = vvb[hp][:, 0:2048].rearrange("p (c d) -> p c d", c=16)
                b_vh = b_vb[hp]
                for qt in range(NT):
                    i = it["n"] % 2
                    it["n"] += 1

                    def a1(hh=hh, qt=qt, i=i, qh=qh, kh=kh, b_qh=b_qh, b_kh=b_kh, v3=v3, b_vh=b_vh):
                        if qt == 0:
                            S.dma("sp", lambda e: e.dma_start(out=qh, in_=sbqT[hh * 128:(hh + 1) * 128, :]), reads=[b_scr["sbqT"]], writes=[b_qh])
                            S.dma("sp", lambda e: e.dma_start(out=kh, in_=sbkT[hh * 128:(hh + 1) * 128, :]), reads=[b_scr["sbkT"]], writes=[b_kh])
                            S.dma("sp", lambda e: e.dma_start(out=v3, in_=sbv[:, hh * 128:(hh + 1) * 128].rearrange("(c p) d -> p c d", p=128)),
                                  reads=[b_scr["sbv"]], writes=[b_vh])
                        e_f, sp_f = ef[i], spf[i]
                        Wk = (qt + 1) * 128
                        tq = slice(qt * 128, (qt + 1) * 128)
                        nb = (Wk + 511) // 512
                        for kb in range(nb):
                            k0, k1 = kb * 512, min((kb + 1) * 512, Wk)
                            S.pe(lambda e, k0=k0, k1=k1: e.matmul(zps[:, k0:k1], lhsT=qh[:, tq], rhs=kh[:, k0:k1], start=True, stop=True),
                                 reads=[b_qh, b_kh], writes=[ZB[kb]])
                        zb = ZB[0:nb]
                        S.act(lambda e: e.activation(out=e_f[:, 0:Wk], in_=zps[:, 0:Wk], func=AF.Exp), reads=zb, writes=[b_e[i]])
                        S.act(lambda e: e.activation(out=sp_f[:, 0:Wk], in_=e_f[:, 0:Wk], func=AF.Ln, bias=1.0),
                              reads=[b_e[i]], writes=[b_sp[i]])
                        S.pool(lambda e: e.affine_select(out=sp_f[:, Wk - 128:Wk], in_=sp_f[:, Wk - 128:Wk], pattern=[[-1, 128]],
                                                         compare_op=ALU.is_gt, fill=0.0, base=0, channel_multiplier=1),
                               reads=[b_sp[i]], writes=[b_sp[i]])

                    def a2(hh=hh, qt=qt, i=i):
                        ntot, b_nt = sb_sm[i]
                        e_f, sp_f, cs_f = ef[i], spf[i], csf[i]
                        Wk = (qt + 1) * 128
                        S.dve(lambda e: e.tensor_tensor_scan(out=cs_f[:, 0:Wk], data0=sp_f[:, 0:Wk], data1=sp_f[:, 0:Wk],
                                                             initial=0.0, op0=ALU.add, op1=ALU.bypass),
                              reads=[b_sp[i]], writes=[b_cs[i]])
                        S.dve(lambda e: e.tensor_scalar(out=ntot, in0=cs_f[:, Wk - 1:Wk], scalar1=-1.0, scalar2=None, op0=ALU.mult),
                              reads=[b_cs[i]], writes=[b_nt])
                        S.act(lambda e: e.activation(out=sp_f[:, 0:1], in_=ntot, func=AF.Exp), reads=[b_nt, b_sp[i], b_cs[i]], writes=[b_sp[i]])
                        S.act(lambda e: e.activation(out=sp_f[:, 1:Wk], in_=cs_f[:, 0:Wk - 1], func=AF.Exp, bias=ntot),
                              reads=[b_cs[i], b_nt], writes=[b_sp[i]])
                        S.dve(lambda e: e.tensor_tensor(out=wb[i][:, 0:Wk], in0=e_f[:, 0:Wk], in1=sp_f[:, 0:Wk], op=ALU.mult),
                              reads=[b_e[i], b_sp[i]], writes=[b_w[i]])
                        S.pool(lambda e: e.affine_select(out=wb[i][:, Wk - 128:Wk], in_=wb[i][:, Wk - 128:Wk], pattern=[[-1, 128]],
                                                         compare_op=ALU.is_gt, fill=0.0, base=0, channel_multiplier=1),
                               reads=[b_w[i]], writes=[b_w[i]])

                    def bst(hh=hh, qt=qt, i=i, v3=v3, b_vh=b_vh):
                        tq = slice(qt * 128, (qt + 1) * 128)
                        transposes_w(qt + 1, i)
                        sl = (hh * NT + qt) % 4
                        ops_ = PS[:, 3072 + sl * 128: 3072 + (sl + 1) * 128]
                        for c in range(qt + 1):
                            S.pe(lambda e, c=c: e.matmul(ops_, lhsT=v3[:, c, :], rhs=wTb[i][:, c, :], start=(c == 0), stop=(c == qt)),
                                 reads=[b_vh, b_wT[i]], writes=[PB6s[sl]])
                        S.act(lambda e: e.activation(out=oT[:, 0, tq], in_=ops_, func=AF.Copy), reads=[PB6s[sl]], writes=[b_oT])
                        if qt == NT - 1:
                            S.dma("sp", lambda e: e.dma_start(out=mergedT[hh * 128:(hh + 1) * 128, :], in_=oT[:, 0, :]), reads=[b_oT],
                                  writes=[b_scr["mergedT"]])
                    iters.append((a1, a2, bst))
            run_pipeline(iters)
            if stop == "sb":
                return
            df_sm = [[smalloc() for _ in range(11)] for _ in range(2)]
            dit = []
            for hh in range(4):
                hp = hh % 2
                v3 = vvb[hp].rearrange("p (c d) -> p c d", c=16)
                b_vh = b_vb[hp]
                for qt in range(NT):
                    i = it["n"] % 2
                    it["n"] += 1
                    dit.append(dict(hh=hh, qt=qt, i=i, v3=v3, b_vh=b_vh))

            def f_z(d, m):
                hh, qt, i, v3, b_vh = d["hh"], d["qt"], d["i"], d["v3"], d["b_vh"]
                if qt == 0 and m == 0:
                    for mm in range(2):
                        S.dma("sp", lambda e, mm=mm: e.dma_start(out=qT[mm], in_=dfqT[(hh * 2 + mm) * 128:(hh * 2 + mm + 1) * 128, :]),
                              reads=[b_scr["dfqT"]], writes=[b_q[mm]])
                        S.dma("sp", lambda e, mm=mm: e.dma_start(out=qT[2 + mm], in_=dfkT[(hh * 2 + mm) * 128:(hh * 2 + mm + 1) * 128, :]),
                              reads=[b_scr["dfkT"]], writes=[b_q[2 + mm]])
                    S.dma("sp", lambda e: e.dma_start(out=v3, in_=dfv[:, hh * 256:(hh + 1) * 256].rearrange("(c p) d -> p c d", p=128)),
                          reads=[b_scr["dfv"]], writes=[b_vh])
                sms = df_sm[i]
                Wk = (qt + 1) * 128
                tq = slice(qt * 128, (qt + 1) * 128)
                nb = (Wk + 511) // 512
                zb = ZB[0:nb]
                pm = [ef[i], csf[i]][m]
                b_pm = [b_e[i], b_cs[i]][m]
                (mx, b_mx), (ll, b_ll), (lb, b_lb) = sms[m * 3], sms[m * 3 + 1], sms[m * 3 + 2]
                for kb in range(nb):
                    k0, k1 = kb * 512, min((kb + 1) * 512, Wk)
                    S.pe(lambda e, k0=k0, k1=k1: e.matmul(zps[:, k0:k1], lhsT=qT[m][:, tq], rhs=qT[2 + m][:, k0:k1], start=True, stop=True),
                         reads=[b_q[m], b_q[2 + m]], writes=[ZB[kb]])
                S.dve(lambda e: e.tensor_reduce(out=mx, in_=zps[:, 0:Wk], axis=AX.X, op=ALU.max), reads=zb, writes=[b_mx])
                S.dve(lambda e: e.tensor_scalar(out=mx, in0=mx, scalar1=-1.0, scalar2=None, op0=ALU.mult), reads=[b_mx], writes=[b_mx])
                S.pool(lambda e: e.memset(lb, 0.0), writes=[b_lb])
                S.act(lambda e: e.activation(out=pm[:, 0:Wk - 64], in_=zps[:, 0:Wk - 64], func=AF.Exp, bias=mx, accum_out=ll),
                      reads=zb + [b_mx], writes=[b_pm, b_ll])
                S.act(lambda e: e.activation(out=pm[64:128, Wk - 64:Wk], in_=zps[64:128, Wk - 64:Wk], func=AF.Exp,
                                             bias=mx[64:128, :], accum_out=lb[64:128, :]),
                      reads=zb + [b_mx, b_lb], writes=[b_pm, b_lb])
                S.pool(lambda e: e.memset(pm[0:64, Wk - 64:Wk], 0.0), reads=[b_pm], writes=[b_pm])
                if m == 1:
                    for mm in range(2):
                        (ll2, b_ll2), (lb2, b_lb2) = sms[mm * 3 + 1], sms[mm * 3 + 2]
                        S.dve(lambda e, ll2=ll2, lb2=lb2: e.tensor_tensor(out=ll2, in0=ll2, in1=lb2, op=ALU.add), reads=[b_ll2, b_lb2], writes=[b_ll2])

            def f_a2a(d):
                i = d["i"]
                sms = df_sm[i]
                Wk = (d["qt"] + 1) * 128
                (r1, b_r1), (r2, b_r2) = sms[6], sms[7]
                l1, b_l1 = sms[1]
                l2, b_l2 = sms[4]
                p1 = ef[i]
                S.dve(lambda e: e.reciprocal(out=r1, in_=l1), reads=[b_l1], writes=[b_r1])
                S.dve(lambda e: e.reciprocal(out=r2, in_=l2), reads=[b_l2], writes=[b_r2])
                S.dve(lambda e: e.tensor_tensor(out=r2, in0=r2, in1=nlam, op=ALU.mult), reads=[b_r2, b_nlam], writes=[b_r2])
                S.act(lambda e: e.activation(out=p1[:, 0:Wk], in_=p1[:, 0:Wk], func=AF.Copy, scale=r1), reads=[b_e[i], b_r1], writes=[b_e[i]])

            def f_a2b(d):
                i = d["i"]
                sms = df_sm[i]
                Wk = (d["qt"] + 1) * 128
                r2, b_r2 = sms[7]
                p1, p2 = ef[i], csf[i]
                S.dve(lambda e: e.scalar_tensor_tensor(out=wb[i][:, 0:Wk], in0=p2[:, 0:Wk], scalar=r2, in1=p1[:, 0:Wk], op0=ALU.mult, op1=ALU.add),
                      reads=[b_e[i], b_cs[i], b_r2], writes=[b_w[i]])

            def f_T(d):
                transposes_w(d["qt"] + 1, d["i"])

            def f_AV(d):
                qt, i, v3, b_vh = d["qt"], d["i"], d["v3"], d["b_vh"]
                sms = df_sm[i]
                (ss, b_ss), (ve, b_ve), (rs, b_rs) = sms[8], sms[9], sms[10]
                ops_ = PS[:, 3072 + i * 256:3072 + (i + 1) * 256]
                for c in range(qt + 1):
                    S.pe(lambda e, c=c: e.matmul(ops_, lhsT=wTb[i][:, c, :], rhs=v3[:, c, :], start=(c == 0), stop=(c == qt)),
                         reads=[b_vh, b_wT[i]], writes=[PB6s[i]])
                S.act(lambda e: e.activation(out=junk[:, i * 256:(i + 1) * 256], in_=ops_, func=AF.Square, accum_out=ss),
                      reads=[PB6s[i]], writes=[b_junk, b_ss])
                rstd_op(ss, b_ss, ve, b_ve, rs, b_rs, 256)
                S.dve(lambda e: e.scalar_tensor_tensor(out=onbb[i], in0=ops_, scalar=rs, in1=gsub, op0=ALU.mult, op1=ALU.mult),
                      reads=[PB6s[i], b_rs, b_gsub], writes=[b_onb[i]])

            def f_O(d):
                hh, qt, i = d["hh"], d["qt"], d["i"]
                tq = slice(qt * 128, (qt + 1) * 128)
                tps = bank_bf(7)[:, i * 256:(i + 1) * 256]
                for j in range(2):
                    S.pe(lambda e, j=j: e.transpose(out=tps[:, j * 128:(j + 1) * 128], in_=onbb[i][:, j * 128:(j + 1) * 128], identity=ident[:]),
                         reads=[b_onb[i], b_ident], writes=[PB7s[i]])
                S.act(lambda e: e.activation(out=oT[:, :, tq], in_=tps.rearrange("p (j t) -> p j t", j=2), func=AF.Copy),
                      reads=[PB7s[i]], writes=[b_oT])
                if qt == NT - 1:
                    S.dma("sp", lambda e: e.dma_start(
                        out=mergedT[1024 + hh * 256:1024 + (hh + 1) * 256, :].rearrange("(j p) t -> p j t", p=128), in_=oT),
                        reads=[b_oT], writes=[b_scr["mergedT"]])

            Nn = len(dit)
            for n in range(-2, Nn):
                v0 = 0 <= n < Nn
                v1 = 0 <= n + 1 < Nn
                v2 = 0 <= n + 2 < Nn
                if v1:
                    f_a2a(dit[n + 1])
                if v0:
                    f_T(dit[n])
                if v2:
                    f_z(dit[n + 2], 0)
                if v1:
                    f_a2b(dit[n + 1])
                if v0:
                    f_AV(dit[n])
                if v2:
                    f_z(dit[n + 2], 1)
                if v0:
                    f_O(dit[n])
            for bb in PB6s:
                PB[6].W = sorted(set(PB[6].W) | set(bb.W)); PB[6].R = sorted(set(PB[6].R) | set(bb.R))
            for bb in PB7s:
                PB[7].W = sorted(set(PB[7].W) | set(bb.W)); PB[7].R = sorted(set(PB[7].R) | set(bb.R))

        def load_feature_major(dst3, dbufs, src_scr, src_name):
            region_switch(R2_all, Bbuf)
            for k in range(KC):
                S.dma("sp", lambda e, k=k: e.dma_start(out=dst3[:, k, :], in_=src_scr[k * 128:(k + 1) * 128, :]),
                      reads=[b_scr[src_name]], writes=[dbufs[k]])

        def odd_mixer():
            w_in = Wd["od_w_in"]
            region_switch(R2_all, Bbuf)
            def evac_u(ps, pbuf, nchg, tb):
                S.act(lambda e: e.activation(out=Bb[:, nchg, tb * 512:(tb + 1) * 512], in_=ps, func=AF.Gelu_apprx_tanh),
                      reads=[pbuf], writes=[Bbuf[nchg]])
            linear_A(A, A_tb, w_in[:, 0:2048], KC, evac_u)

            def evac_v(ps, pbuf, tt, cb, ex=None):
                i = next_stg()
                S.act(lambda e: e.activation(out=stg[i][:], in_=ps, func=AF.Gelu_apprx_tanh), reads=[pbuf], writes=[b_stg[i]])
                S.dma("sp", lambda e: e.dma_start(out=vsc[tt * 128:(tt + 1) * 128, cb * 512:(cb + 1) * 512], in_=stg[i][:]),
                      reads=[b_stg[i]], writes=[b_scr["vsc"]])
            linear_B(A, A_tt, w_in[:, 2048:4096], KC, evac_v)
            wsn = R1f[:, 0:2048].rearrange("p (g s) -> p g s", g=16)
            wsb = R1[:, 4096:6144].rearrange("p (g s) -> p g s", g=16)
            wmT = R1[:, 6144:8192].rearrange("p (g t) -> p g t", g=16)
            bsb = R1f[:, 4096:6144].rearrange("p (g t) -> p g t", g=16)
            lng = R1f[:, 6144:8192]
            lnb = R1f[:, 8192:10240]
            vnb = R1[:, 20480:22528]
            tmpf = R1f[:, 11264:11776]
            b_ws, b_wmT, b_bsb, b_ln, b_vnb, b_tmpf = (Buf(n) for n in ["wsn", "wmT", "bsb", "ln", "vnb", "tmpf"])
            region_switch(R1_all, [b_ws, b_wmT, b_bsb, b_ln, b_vnb, b_tmpf])
            S.dma("sp", lambda e: e.dma_start(out=wsn, in_=Wd["od_w_s"].rearrange("g t s -> t g s")), writes=[b_ws])
            S.dma("sp", lambda e: e.dma_start(out=bsb.rearrange("p g t -> p (g t)"), in_=Wd["od_b_s"].partition_broadcast(128)), writes=[b_bsb])
            S.dma("sp", lambda e: e.dma_start(out=lng, in_=Wd["od_ln_g"].partition_broadcast(128)), writes=[b_ln])
            S.dma("sp", lambda e: e.dma_start(out=lnb, in_=Wd["od_ln_b"].partition_broadcast(128)), writes=[b_ln])
            S.dve(lambda e: e.tensor_copy(out=wsb, in_=wsn), reads=[b_ws], writes=[b_ws])
            S.pool(lambda e: e.memset(wsb[0:64, :, 64:128], 0.0), reads=[b_ws], writes=[b_ws])
            for half in range(2):
                for j in range(8):
                    g = half * 8 + j
                    S.pe(lambda e, half=half, j=j, g=g: e.transpose(out=bank_bf(4 + half)[:, j * 128:(j + 1) * 128], in_=wsb[:, g, :],
                                                                    identity=ident[:]), reads=[b_ws, b_ident], writes=[PB[4 + half]])
                S.dve(lambda e, half=half: e.tensor_copy(out=wmT[:, half * 8:(half + 1) * 8, :],
                                                         in_=bank_bf(4 + half).rearrange("p (g t) -> p g t", g=8)),
                      reads=[PB[4 + half]], writes=[b_wmT])
            FMAX = 512
            stats = R1f[:, 12288:12288 + 4 * 6]
            mv = R1f[:, 12320:12322]
            prefetch_w(Wd["od_w_out"][:, 0:512], KC)
            prefetch_w(Wd["od_w_out"][:, 512:1024], KC)
            vbuf = [hst, mst]
            b_vbuf = [b_hst, b_mst]
            vnbb = [R1[:, 20480:22528], R1[:, 26624:28672]]
            b_vnbb = [b_vnb, Buf("vnb1")]
            region_switch(R1_all, [b_vnbb[1]])
            sp_sm = [[smalloc() for _ in range(3)] for _ in range(2)]
            statsb = [R1f[:, 12288 + i * 32:12288 + i * 32 + 24] for i in range(2)]
            mvb = [R1f[:, 12352 + i * 4:12352 + i * 4 + 2] for i in range(2)]
            b_stats = [Buf("stats0"), Buf("stats1")]
            region_switch(R1_all, b_stats)

            def g1(n):
                i = n % 2
                vt = vbuf[i]
                stats, mv = statsb[i], mvb[i]
                (ve, b_ve), (rs, b_rs), (nmr, b_nmr) = sp_sm[i]
                rows = slice(n * 128, (n + 1) * 128)
                S.dma("sp", lambda e: e.dma_start(out=vt[:], in_=vsc[rows, :]), reads=[b_scr["vsc"]], writes=[b_vbuf[i]])
                for c in range(4):
                    S.dve(lambda e, c=c: e.bn_stats(out=stats[:, c * 6:(c + 1) * 6], in_=vt[:, c * 512:(c + 1) * 512]), reads=[b_vbuf[i]], writes=[b_stats[i]])
                S.dve(lambda e: e.bn_aggr(out=mv, in_=stats), reads=[b_stats[i]], writes=[b_stats[i]])
                S.dve(lambda e: e.tensor_scalar(out=ve, in0=mv[:, 1:2], scalar1=EPS, scalar2=None, op0=ALU.add), reads=[b_stats[i]], writes=[b_ve])
                S.pool(lambda e: e.tensor_tensor(out=rs, in0=ve, in1=negh, op=ALU.pow), reads=[b_ve, b_negh], writes=[b_rs])
                S.dve(lambda e: e.scalar_tensor_tensor(out=nmr, in0=mv[:, 0:1], scalar=-1.0, in1=rs, op0=ALU.mult, op1=ALU.mult),
                      reads=[b_stats[i], b_rs], writes=[b_nmr])
                S.act(lambda e: e.activation(out=vt[:], in_=vt[:], func=AF.Identity, scale=rs, bias=nmr), reads=[b_vbuf[i], b_rs, b_nmr], writes=[b_vbuf[i]])
                S.pool(lambda e: e.tensor_tensor(out=vt[:], in0=vt[:], in1=lng, op=ALU.mult), reads=[b_vbuf[i], b_ln], writes=[b_vbuf[i]])
                S.dve(lambda e: e.tensor_tensor(out=vnbb[i], in0=vt[:], in1=lnb, op=ALU.add), reads=[b_vbuf[i], b_ln], writes=[b_vnbb[i]])

            def g2(n):
                i = n % 2
                vn = vnbb[i]
                for g4 in range(4):
                    b = next_pb()
                    for j in range(4):
                        g = g4 * 4 + j
                        S.pe(lambda e, b=b, j=j, g=g: e.matmul(bank(b)[:, j * 128:(j + 1) * 128], lhsT=vn[:, g * 128:(g + 1) * 128], rhs=wmT[:, g, :],
                                                               start=True, stop=True), reads=[b_vnbb[i], b_wmT], writes=[PB[b]])
                    tf = R1f[:, 11264 + (g4 % 2) * 512:11264 + (g4 % 2 + 1) * 512]
                    b_tf = [b_tmpf, b_tmpf2][g4 % 2]
                    S.dve(lambda e, b=b, g4=g4, tf=tf: e.tensor_tensor(out=tf.rearrange("p (g t) -> p g t", g=4),
                                                                       in0=bank(b).rearrange("p (g t) -> p g t", g=4),
                                                                       in1=bsb[:, g4 * 4:(g4 + 1) * 4, :], op=ALU.add),
                          reads=[PB[b], b_bsb], writes=[b_tf])
                    yv = Bb[:, g4 * 4:(g4 + 1) * 4, n * 128:(n + 1) * 128]
                    eng = S.pool if g4 % 2 == 0 else S.dve
                    eng(lambda e, yv=yv, tf=tf: e.tensor_tensor(out=yv, in0=yv, in1=tf.rearrange("p (g t) -> p g t", g=4), op=ALU.mult),
                        reads=[b_tf] + Bbuf[g4 * 4:(g4 + 1) * 4], writes=Bbuf[g4 * 4:(g4 + 1) * 4])

            b_tmpf2 = Buf("tmpf2")
            region_switch(R1_all, [b_tmpf2])
            for n in range(-1, NT):
                if n + 1 < NT:
                    g1(n + 1)
                if n >= 0:
                    g2(n)
            linear_B(Bb, B_all, Wd["od_w_out"], KC, evac_to_ms)

        prefetch_w(Wd["ev_w_in"][:, 0:1024][:, 0:512], KC)
        prefetch_w(Wd["ev_w_in"][:, 0:1024][:, 512:1024], KC)
        upd_pass(x_in, None, None, None, "norm", GC_EV)
        even_mixer()
        if stop in ("qkv", "sb"):
            pass
        else:
            load_feature_major(Bb, Bbuf, mergedT, "mergedT")
            linear_B(Bb, B_all, Wd["ev_w_out"], KC, evac_to_ms)
            if stop == "mix0":
                upd_pass(x_in, ms, Wd["ev_norm_post"], out, None, None, final=True)
            else:
                prefetch_w(Wd["ffn_w1"][0][:, 0:2048][:, 0:512], KC)
                prefetch_w(Wd["ffn_w1"][0][:, 0:2048][:, 512:1024], KC)
                upd_pass(x_in, ms, Wd["ev_norm_post"], hs, "norm", GC_F0)
                ffn(0)
                if stop == "ffn0":
                    upd_pass(hs, ms, Wd["ffn_norm_post"][0:1, :], out, None, None, final=True)
                else:
                    prefetch_w(Wd["ple_w_gate"][0][:, 0:512], KC)
                    prefetch_w(Wd["ple_w_gate"][0][:, 512:1024], KC)
                    upd_pass(hs, ms, Wd["ffn_norm_post"][0:1, :], hs, "raw", None)
                    ple(0)
                    if stop == "l0":
                        upd_pass(hs, ms, Wd["ple_norm"][0:1, :], out, None, None, final=True)
                    else:
                        prefetch_w(Wd["od_w_in"][:, 0:2048][:, 0:512], KC)
                        prefetch_w(Wd["od_w_in"][:, 0:2048][:, 512:1024], KC)
                        upd_pass(hs, ms, Wd["ple_norm"][0:1, :], hs, "norm", GC_OD)
                        odd_mixer()
                        if stop == "mix1":
                            upd_pass(hs, ms, Wd["od_norm_post"], out, None, None, final=True)
                        else:
                            prefetch_w(Wd["ffn_w1"][1][:, 0:2048][:, 0:512], KC)
                            prefetch_w(Wd["ffn_w1"][1][:, 0:2048][:, 512:1024], KC)
                            upd_pass(hs, ms, Wd["od_norm_post"], hs, "norm", GC_F1)
                            ffn(1)
                            prefetch_w(Wd["ple_w_gate"][1][:, 0:512], KC)
                            prefetch_w(Wd["ple_w_gate"][1][:, 512:1024], KC)
                            upd_pass(hs, ms, Wd["ffn_norm_post"][1:2, :], hs, "raw", None)
                            ple(1)
                            upd_pass(hs, ms, Wd["ple_norm"][1:2, :], out, None, None, final=True)
        if stop in ("qkv", "sb"):
            o = S.dma("sp", lambda e: e.dma_start(out=out[0:128, :], in_=hst[:]), reads=[b_hst], writes=[b_out])
            out_ops.append(o)
        assert not pref, list(pref)
        S.emit(final_wait_ops=out_ops)
    return nc


def make_in_maps(inputs):
    f = lambda a: np.ascontiguousarray(np.asarray(a))
    shared = {}
    for name, shape in W_SPECS:
        shared[name] = f(inputs[name]).reshape(shape)
    maps = []
    for b in range(8):
        m = dict(shared)
        m["x"] = f(inputs["x"][b])
        m["p"] = f(inputs["p"][:, b])
        m["positions"] = f(inputs["positions"][b:b + 1]).astype(np.int32)
        maps.append(m)
    return maps


_NC_CACHE = {}


def kernel(**inputs):
    if "nc" not in _NC_CACHE:
        _NC_CACHE["nc"] = build()
    nc = _NC_CACHE["nc"]
    maps = make_in_maps(inputs)
    res = run_bass_kernel_spmd(nc, maps, core_ids=list(range(8)))
    return np.stack([np.asarray(r["out"]) for r in res.results], axis=0).astype(np.float32)
```
